# Optimizing a Trainium2 kernel written in Bass

```python
import math
import jax
import jax.numpy as jnp
from jax import lax
import numpy as np

D_MODEL = 1024
BATCH = 32
SEQ = 2048
DEPTH = 2

GRID_W = 64
CTX_LEN = 256
N_MOD = 9
D_FF = 2816
FFN_HALF = 0.5
ALPHA = (2 * DEPTH) ** 0.25
BETA = (8 * DEPTH) ** -0.25
N_BRANCH = 4
BRANCH_W = 256
NA_HEADS = 4
NA_DH = 64
NA_KH_MAX = 8
NA_KW = 16
ML_HEADS = 4
ML_DQK = 64
ML_DV = 64
ML_CHUNK = 64
ML_F_BIAS = 3.0
GLA_HEADS = 4
GLA_DK = 32
GLA_DV = 64
GLA_RANK = 16
GLA_TAU = 16.0
GLA_CHUNK = 16
DA_HEADS = 4
DA_DQK = 32
DA_DV = 64
DA_QBLOCK = 128
ROPE_BASE = 10000.0

COLS = (
    ("na_q", NA_HEADS * NA_DH), ("na_k", NA_HEADS * NA_DH), ("na_v", NA_HEADS * NA_DH),
    ("ml_q", ML_HEADS * ML_DQK), ("ml_k", ML_HEADS * ML_DQK), ("ml_v", ML_HEADS * ML_DV),
    ("ml_o", ML_HEADS * ML_DV), ("ml_if", 4 * ML_HEADS),
    ("gla_q", GLA_HEADS * GLA_DK), ("gla_k", GLA_HEADS * GLA_DK), ("gla_v", GLA_HEADS * GLA_DV),
    ("gla_g", GLA_HEADS * GLA_DV), ("gla_a", 2 * GLA_RANK),
    ("da_q", DA_HEADS * 2 * DA_DQK), ("da_k", DA_HEADS * 2 * DA_DQK), ("da_v", DA_HEADS * DA_DV),
    ("gates", N_BRANCH * D_MODEL),
)
N_COLS = sum(w for _, w in COLS)

kernel_name = "hybrid_dit_na_mlstm_gla_diffattn"


def _layer_norm(x, g, b, eps=1e-5):
    xf = x.astype(jnp.float32)
    mu = jnp.mean(xf, axis=-1, keepdims=True)
    var = jnp.mean(jnp.square(xf - mu), axis=-1, keepdims=True)
    return ((xf - mu) * lax.rsqrt(var + eps) * g + b).astype(x.dtype)


def _modulate(x, shift, scale):
    return x * (1 + scale) + shift


def _swiglu(h, w_in, w_out):
    a, b = jnp.split(h @ w_in, 2, axis=-1)
    return (jax.nn.silu(a) * b) @ w_out


def _heads(t, n):
    B, S, W = t.shape
    return t.reshape(B, S, n, W // n).transpose(0, 2, 1, 3)


def _merge(t):
    B, H, S, d = t.shape
    return t.transpose(0, 2, 1, 3).reshape(B, S, H * d)


def _to_chunks(t, size):
    B, H, S = t.shape[:3]
    return jnp.moveaxis(t.reshape(B, H, S // size, size, *t.shape[3:]), 2, 0)


def _from_chunks(t):
    nc, B, H, size = t.shape[:4]
    return jnp.moveaxis(t, 0, 2).reshape(B, H, nc * size, *t.shape[4:])


def _head_norm(y, g, center, eps=1e-6):
    yf = y.astype(jnp.float32)
    if center:
        yf = yf - jnp.mean(yf, axis=-1, keepdims=True)
    yf = yf * lax.rsqrt(jnp.mean(jnp.square(yf), axis=-1, keepdims=True) + eps)
    return _merge(yf) * g


def _split_cols(p):
    out, off = {}, 0
    for name, w in COLS:
        out[name] = p[..., off:off + w]
        off += w
    return out


def _dense_attention(q, k, v):
    s = jnp.einsum("bhqd,bhkd->bhqk", q, k).astype(jnp.float32) * q.shape[-1] ** -0.5
    return jnp.einsum("bhqk,bhkd->bhqd", jax.nn.softmax(s, axis=-1).astype(v.dtype), v)


def _neighbourhood_attention(q, k, v, qc, kc, vc, rpb, rows, ctx_out):
    B, H, S, d = q.shape
    kh = min(NA_KH_MAX, rows)
    scale = d ** -0.5
    qg = q.reshape(B, H, rows, GRID_W, d)
    kg = k.reshape(B, H, rows, GRID_W, d)
    vg = v.reshape(B, H, rows, GRID_W, d)
    col = jnp.arange(GRID_W)
    c0 = jnp.clip(col - NA_KW // 2, 0, GRID_W - NA_KW)
    col_ok = (col[None, :] >= c0[:, None]) & (col[None, :] < c0[:, None] + NA_KW)
    dc_idx = jnp.clip(col[None, :] - col[:, None] + NA_KW - 1, 0, 2 * NA_KW - 2)
    n_loc = kh * GRID_W

    def row_block(r):
        r0 = jnp.clip(r - kh // 2, 0, rows - kh)
        k_blk = lax.dynamic_slice_in_dim(kg, r0, kh, axis=2)
        v_blk = lax.dynamic_slice_in_dim(vg, r0, kh, axis=2)
        q_row = lax.dynamic_index_in_dim(qg, r, axis=2, keepdims=False)
        dr_idx = r0 + jnp.arange(kh) - r + NA_KH_MAX - 1
        bias = rpb[:, dr_idx[None, :, None], dc_idx[:, None, :]]
        s_loc = jnp.einsum("bhqd,bhrkd->bhqrk", q_row, k_blk).astype(jnp.float32) * scale + bias
        s_loc = jnp.where(col_ok[:, None, :], s_loc, -jnp.inf).reshape(B, H, GRID_W, n_loc)
        s_ctx = jnp.einsum("bhqd,bhtd->bhqt", q_row, kc).astype(jnp.float32) * scale
        p = jax.nn.softmax(jnp.concatenate([s_loc, s_ctx], axis=-1), axis=-1).astype(v.dtype)
        p_loc = p[..., :n_loc].reshape(B, H, GRID_W, kh, GRID_W)
        return (jnp.einsum("bhqrk,bhrkd->bhqd", p_loc, v_blk)
                + jnp.einsum("bhqt,bhtd->bhqd", p[..., n_loc:], vc))

    out = _from_chunks(lax.map(row_block, jnp.arange(rows)))
    outc = _dense_attention(qc, kc, vc) if ctx_out else None
    return out, outc


def _mlstm_scan(q, k, v, log_i, log_f, state):
    L = ML_CHUNK
    mask = jnp.tril(jnp.ones((L, L), dtype=bool))

    def step(carry, inp):
        c_mat, n_vec, m_prev = carry
        qb, kb, vb, ib, fb = inp
        b = jnp.cumsum(fb, axis=-1)
        d_log = jnp.where(mask, b[..., :, None] - b[..., None, :] + ib[..., None, :], -jnp.inf)
        inter = b + m_prev[..., None]
        m_t = jnp.maximum(inter, jnp.max(d_log, axis=-1))
        w = jnp.exp(d_log - m_t[..., None])
        w_inter = jnp.exp(inter - m_t)
        qk = jnp.einsum("bhtd,bhsd->bhts", qb, kb) * w
        num = (jnp.einsum("bhts,bhsv->bhtv", qk, vb)
               + w_inter[..., None] * jnp.einsum("bhtd,bhdv->bhtv", qb, c_mat))
        den = jnp.sum(qk, axis=-1) + w_inter * jnp.einsum("bhtd,bhd->bht", qb, n_vec)
        h = num / jnp.maximum(jnp.abs(den), jnp.exp(-m_t))[..., None]
        b_end = b[..., -1]
        k_log = b_end[..., None] - b + ib
        m_new = jnp.maximum(b_end + m_prev, jnp.max(k_log, axis=-1))
        wk = jnp.exp(k_log - m_new[..., None])
        decay = jnp.exp(b_end + m_prev - m_new)
        c_mat = decay[..., None, None] * c_mat + jnp.einsum("bhsd,bhsv->bhdv", kb * wk[..., None], vb)
        n_vec = decay[..., None] * n_vec + jnp.einsum("bhs,bhsd->bhd", wk, kb)
        return (c_mat, n_vec, m_new), h

    xs = tuple(_to_chunks(t, L) for t in (q, k, v, log_i, log_f))
    state, hs = lax.scan(step, state, xs)
    return _from_chunks(hs), state


def _gla_scan(q, k, v, log_a, state):
    L = GLA_CHUNK
    mask = jnp.tril(jnp.ones((L, L), dtype=bool))

    def step(st, inp):
        qb, kb, vb, ab = inp
        b = jnp.cumsum(ab, axis=2)
        rel = jnp.where(mask[:, :, None], b[:, :, :, None, :] - b[:, :, None, :, :], -jnp.inf)
        att = jnp.einsum("bhtd,bhsd,bhtsd->bhts", qb, kb, jnp.exp(rel))
        o = jnp.einsum("bhts,bhsv->bhtv", att, vb) + jnp.einsum("bhtd,bhdv->bhtv", qb * jnp.exp(b), st)
        b_end = b[:, :, -1:, :]
        st = (jnp.exp(b_end[:, :, 0, :])[..., None] * st
              + jnp.einsum("bhsd,bhsv->bhdv", kb * jnp.exp(b_end - b), vb))
        return st, o

    xs = tuple(_to_chunks(t, L) for t in (q, k, v, log_a))
    state, os_ = lax.scan(step, state, xs)
    return _from_chunks(os_), state


def _flip(ts):
    return tuple(jnp.flip(t, axis=2) for t in ts)


def _bidirectional(scan, lat_f, lat_b, ctx_f, ctx_b, init):
    hc_f, st_f = scan(*ctx_f, init)
    h_f, _ = scan(*lat_f, st_f)
    hc_b, st_b = scan(*_flip(ctx_b), init)
    h_b, _ = scan(*_flip(lat_b), st_b)
    return h_f + jnp.flip(h_b, axis=2), hc_f + jnp.flip(hc_b, axis=2)


def _axial_rope_tables(S, dim, dtype):
    t = jnp.arange(S)
    n_f = dim // 4
    inv = ROPE_BASE ** (-jnp.arange(n_f, dtype=jnp.float32) / n_f)

    def cs(pos):
        ang = pos.astype(jnp.float32)[:, None] * inv
        return jnp.cos(ang).astype(dtype), jnp.sin(ang).astype(dtype)

    return cs(t // GRID_W), cs(t % GRID_W)


def _rotate(y, cos, sin):
    y1, y2 = jnp.split(y, 2, axis=-1)
    return jnp.concatenate([y1 * cos - y2 * sin, y1 * sin + y2 * cos], axis=-1)


def _rope_2d(x, row_cs, col_cs):
    xr, xc = jnp.split(x, 2, axis=-1)
    return jnp.concatenate([_rotate(xr, *row_cs), _rotate(xc, *col_cs)], axis=-1)


def _diff_map(q1, q2, k1, k2, v, lam):
    scale = DA_DQK ** -0.5
    a1 = jax.nn.softmax(jnp.einsum("bhqd,bhkd->bhqk", q1, k1).astype(jnp.float32) * scale, axis=-1)
    a2 = jax.nn.softmax(jnp.einsum("bhqd,bhkd->bhqk", q2, k2).astype(jnp.float32) * scale, axis=-1)
    return jnp.einsum("bhqk,bhkd->bhqd", (a1 - lam * a2).astype(v.dtype), v)


def _diff_attention(q1, q2, k1, k2, v, q1c, q2c, k1c, k2c, vc, lam, ctx_out):
    S = q1.shape[2]
    k1a = jnp.concatenate([k1, k1c], axis=2)
    k2a = jnp.concatenate([k2, k2c], axis=2)
    va = jnp.concatenate([v, vc], axis=2)

    def block(i):
        s0 = i * DA_QBLOCK
        return _diff_map(lax.dynamic_slice_in_dim(q1, s0, DA_QBLOCK, axis=2),
                         lax.dynamic_slice_in_dim(q2, s0, DA_QBLOCK, axis=2), k1a, k2a, va, lam)

    out = _from_chunks(lax.map(block, jnp.arange(S // DA_QBLOCK)))
    outc = _diff_map(q1c, q2c, k1c, k2c, vc, lam) if ctx_out else None
    return out, outc


def _prep(p, ml_gate_b, gla_w_a2, gla_b_a):
    f32 = jnp.float32
    P = _split_cols(p)
    g = (P["ml_if"] + ml_gate_b).astype(f32)
    i_f, f_f, i_b, f_b = [t.transpose(0, 2, 1) for t in jnp.split(g, 4, axis=-1)]
    f_f, f_b = jax.nn.log_sigmoid(f_f), jax.nn.log_sigmoid(f_b)
    mq = _heads(P["ml_q"], ML_HEADS).astype(f32) * ML_DQK ** -0.5
    mk = _heads(P["ml_k"], ML_HEADS).astype(f32)
    mv = _heads(P["ml_v"], ML_HEADS).astype(f32)
    gq = _heads(P["gla_q"], GLA_HEADS).astype(f32) * GLA_DK ** -0.5
    gk = _heads(P["gla_k"], GLA_HEADS).astype(f32)
    gv = _heads(P["gla_v"], GLA_HEADS).astype(f32)
    a_f, a_b = jnp.split(P["gla_a"], 2, axis=-1)

    def decay(a, j):
        return _heads(jax.nn.log_sigmoid((a @ gla_w_a2[j] + gla_b_a[j]).astype(f32)) / GLA_TAU, GLA_HEADS)

    dq1, dq2 = jnp.split(_heads(P["da_q"], DA_HEADS), 2, axis=-1)
    dk1, dk2 = jnp.split(_heads(P["da_k"], DA_HEADS), 2, axis=-1)
    return {
        "na": (_heads(P["na_q"], NA_HEADS), _heads(P["na_k"], NA_HEADS), _heads(P["na_v"], NA_HEADS)),
        "ml_f": (mq, mk, mv, i_f, f_f),
        "ml_b": (mq, mk, mv, i_b, f_b),
        "ml_o": P["ml_o"],
        "gla_f": (gq, gk, gv, decay(a_f, 0)),
        "gla_b": (gq, gk, gv, decay(a_b, 1)),
        "gla_g": P["gla_g"],
        "da": (dq1, dq2, dk1, dk2, _heads(P["da_v"], DA_HEADS)),
        "gates": P["gates"],
    }


def _merge_branches(ys, gates, w_branch, w_out):
    g = jnp.split(gates, N_BRANCH, axis=-1)
    acc = jax.nn.sigmoid(g[0]) * (ys[0] @ w_branch[0])
    for j in range(1, N_BRANCH):
        acc = acc + jax.nn.sigmoid(g[j]) * (ys[j] @ w_branch[j])
    return acc @ w_out


def _token_mixer(h, hc, ctx_out, w_mix_in, na_rpb, ml_gate_b, ml_norm_g, gla_w_a2, gla_b_a,
                 gla_norm_g, da_lambda, da_norm_g, lambda_init, w_branch, w_out):
    B, S, _ = h.shape
    rows = S // GRID_W
    dt = h.dtype
    lt = _prep(h @ w_mix_in, ml_gate_b, gla_w_a2, gla_b_a)
    cx = _prep(hc @ w_mix_in, ml_gate_b, gla_w_a2, gla_b_a)

    o_na, oc_na = _neighbourhood_attention(*lt["na"], *cx["na"], na_rpb, rows, ctx_out)

    ml_init = (jnp.zeros((B, ML_HEADS, ML_DQK, ML_DV), jnp.float32),
               jnp.zeros((B, ML_HEADS, ML_DQK), jnp.float32),
               jnp.zeros((B, ML_HEADS), jnp.float32))
    h_ml, hc_ml = _bidirectional(_mlstm_scan, lt["ml_f"], lt["ml_b"], cx["ml_f"], cx["ml_b"], ml_init)

    gla_init = jnp.zeros((B, GLA_HEADS, GLA_DK, GLA_DV), jnp.float32)
    h_gla, hc_gla = _bidirectional(_gla_scan, lt["gla_f"], lt["gla_b"], cx["gla_f"], cx["gla_b"], gla_init)

    row_cs, col_cs = _axial_rope_tables(S, DA_DQK, dt)
    q1, q2, k1, k2, v = lt["da"]
    q1, q2, k1, k2 = [_rope_2d(t, row_cs, col_cs) for t in (q1, q2, k1, k2)]
    lq1, lk1, lq2, lk2 = da_lambda.astype(jnp.float32)
    lam = jnp.exp(jnp.sum(lq1 * lk1)) - jnp.exp(jnp.sum(lq2 * lk2)) + lambda_init
    o_da, oc_da = _diff_attention(q1, q2, k1, k2, v, *cx["da"], lam, ctx_out)

    def finish(t, o_na_, h_ml_, h_gla_, o_da_):
        y_na = _merge(o_na_)
        y_ml = jax.nn.sigmoid(t["ml_o"]) * _head_norm(h_ml_, ml_norm_g, True).astype(dt)
        y_gla = jax.nn.silu(t["gla_g"]) * _head_norm(h_gla_, gla_norm_g, False).astype(dt)
        y_da = ((1.0 - lambda_init) * _head_norm(o_da_, da_norm_g, False)).astype(dt)
        return _merge_branches((y_na, y_ml, y_gla, y_da), t["gates"], w_branch, w_out)

    y = finish(lt, o_na, h_ml, h_gla, o_da)
    yc = finish(cx, oc_na, hc_ml, hc_gla, oc_da) if ctx_out else None
    return y, yc


def setup_inputs(seed: int = 0) -> dict:
    key = jax.random.key(seed)
    ks = jax.random.split(key, 21)

    def nrm(k, shape, s):
        return jax.random.normal(k, shape, jnp.float32) * s

    gate_base = jnp.concatenate([jnp.zeros((ML_HEADS,), jnp.float32), jnp.full((ML_HEADS,), ML_F_BIAS, jnp.float32),
                                 jnp.zeros((ML_HEADS,), jnp.float32), jnp.full((ML_HEADS,), ML_F_BIAS, jnp.float32)])
    return {
        "x": nrm(ks[0], (BATCH, SEQ, D_MODEL), 1.0),
        "c": nrm(ks[1], (BATCH, D_MODEL), 1.0),
        "ctx": nrm(ks[2], (BATCH, CTX_LEN, D_MODEL), 1.0),
        "c_ctx": nrm(ks[3], (D_MODEL,), 1.0),
        "w_ada": nrm(ks[4], (DEPTH, D_MODEL, N_MOD * D_MODEL), D_MODEL ** -0.5),
        "b_ada": nrm(ks[5], (DEPTH, N_MOD * D_MODEL), 0.02),
        "ln_g": 1.0 + nrm(ks[6], (DEPTH, 3, D_MODEL), 0.02),
        "ln_b": nrm(ks[7], (DEPTH, 3, D_MODEL), 0.02),
        "ffn_w_in": nrm(ks[8], (DEPTH, 2, D_MODEL, 2 * D_FF), D_MODEL ** -0.5),
        "ffn_w_out": nrm(ks[9], (DEPTH, 2, D_FF, D_MODEL), BETA * D_FF ** -0.5),
        "w_mix_in": nrm(ks[10], (DEPTH, D_MODEL, N_COLS), D_MODEL ** -0.5),
        "na_rpb": nrm(ks[11], (DEPTH, NA_HEADS, 2 * NA_KH_MAX - 1, 2 * NA_KW - 1), 0.1),
        "ml_gate_b": gate_base + nrm(ks[12], (DEPTH, 4 * ML_HEADS), 0.1),
        "ml_norm_g": 1.0 + nrm(ks[13], (DEPTH, ML_HEADS * ML_DV), 0.02),
        "gla_w_a2": nrm(ks[14], (DEPTH, 2, GLA_RANK, GLA_HEADS * GLA_DK), GLA_RANK ** -0.5),
        "gla_b_a": nrm(ks[15], (DEPTH, 2, GLA_HEADS * GLA_DK), 0.1),
        "gla_norm_g": 1.0 + nrm(ks[16], (DEPTH, GLA_HEADS * GLA_DV), 0.02),
        "da_lambda": nrm(ks[17], (DEPTH, 4, DA_DQK), 0.1),
        "da_norm_g": 1.0 + nrm(ks[18], (DEPTH, DA_HEADS * DA_DV), 0.02),
        "w_branch": nrm(ks[19], (DEPTH, N_BRANCH, BRANCH_W, D_MODEL), BRANCH_W ** -0.5),
        "w_out": nrm(ks[20], (DEPTH, D_MODEL, D_MODEL), BETA * D_MODEL ** -0.5),
    }


def reference(x, c, ctx, c_ctx, w_ada, b_ada, ln_g, ln_b, ffn_w_in, ffn_w_out, w_mix_in, na_rpb,
              ml_gate_b, ml_norm_g, gla_w_a2, gla_b_a, gla_norm_g, da_lambda, da_norm_g, w_branch, w_out):
    lat, cx = x, ctx
    for l in range(DEPTH):
        ctx_out = l < DEPTH - 1
        lambda_init = 0.8 - 0.6 * math.exp(-0.3 * l)
        mod = jnp.split((jax.nn.silu(c) @ w_ada[l] + b_ada[l])[:, None, :], N_MOD, axis=-1)
        modc = jnp.split(jax.nn.silu(c_ctx) @ w_ada[l] + b_ada[l], N_MOD, axis=-1)

        lat = _layer_norm(ALPHA * lat + FFN_HALF * mod[2] * _swiglu(_modulate(lat, mod[0], mod[1]),
                          ffn_w_in[l, 0], ffn_w_out[l, 0]), ln_g[l, 0], ln_b[l, 0])
        cx = _layer_norm(ALPHA * cx + FFN_HALF * modc[2] * _swiglu(_modulate(cx, modc[0], modc[1]),
                         ffn_w_in[l, 0], ffn_w_out[l, 0]), ln_g[l, 0], ln_b[l, 0])

        y, yc = _token_mixer(_modulate(lat, mod[3], mod[4]), _modulate(cx, modc[3], modc[4]), ctx_out,
                             w_mix_in[l], na_rpb[l], ml_gate_b[l], ml_norm_g[l], gla_w_a2[l], gla_b_a[l],
                             gla_norm_g[l], da_lambda[l], da_norm_g[l], lambda_init, w_branch[l], w_out[l])
        lat = _layer_norm(ALPHA * lat + mod[5] * y, ln_g[l, 1], ln_b[l, 1])

        lat = _layer_norm(ALPHA * lat + FFN_HALF * mod[8] * _swiglu(_modulate(lat, mod[6], mod[7]),
                          ffn_w_in[l, 1], ffn_w_out[l, 1]), ln_g[l, 2], ln_b[l, 2])
        if ctx_out:
            cx = _layer_norm(ALPHA * cx + modc[5] * yc, ln_g[l, 1], ln_b[l, 1])
            cx = _layer_norm(ALPHA * cx + FFN_HALF * modc[8] * _swiglu(_modulate(cx, modc[6], modc[7]),
                             ffn_w_in[l, 1], ffn_w_out[l, 1]), ln_g[l, 2], ln_b[l, 2])
    return lat
```

```python
import math
from contextlib import ExitStack

import numpy as np
import concourse.bass as bass
import concourse.mybir as mybir
from concourse.bass_utils import run_bass_kernel_spmd

F32 = mybir.dt.float32
BF16 = mybir.dt.bfloat16
ALU = mybir.AluOpType
AF = mybir.ActivationFunctionType
AX = mybir.AxisListType

NCORES = 8
D = 1024
DEPTH = 2
BPC = 4
SEQ = 2048
CTX = 256
T = SEQ + CTX
NT = T // 128
DFF = 2816
NJ = DFF // 128
NMOD = 9
ALPHA = (2 * DEPTH) ** 0.25
LN_EPS = 1e-5 / (ALPHA * ALPHA)
TILES512 = [(0, 512), (512, 512), (1024, 512), (1536, 512), (2048, 256)]


class _Rec:
    def __getattr__(self, name):
        def f(*a, **k):
            self.call = (name, a, k)
        return f


class Tracker:
    ENGS = ("pe", "act", "dve", "pool", "sp")

    def __init__(self, nc, es):
        self.nc = nc
        self.es = es
        self.eng_obj = {"pe": nc.tensor, "act": nc.scalar, "dve": nc.vector, "pool": nc.gpsimd, "sp": nc.sync}
        self.sem = {e: es.enter_context(nc.semaphore("sem_" + e)) for e in ("pe", "act", "dve", "pool")}
        self.count = {e: 0 for e in self.sem}
        self.pending = {e: False for e in self.sem}
        self.dsem = {}
        self.dcount = {}
        self.waited = {e: {} for e in self.ENGS}
        self.writers = {}
        self.readers = {}
        self.stream = {e: [] for e in self.ENGS}
        self.nops = 0

    def _dma_sem(self, slot):
        if slot not in self.dsem:
            self.dsem[slot] = self.es.enter_context(self.nc.semaphore("d_" + str(len(self.dsem))))
            self.dcount[slot] = 0
        return self.dsem[slot]

    def _deps(self, r, w):
        deps = []
        for x in r:
            t = self.writers.get(x)
            if t is not None:
                deps.append(t)
        for x in w:
            t = self.writers.get(x)
            if t is not None:
                deps.append(t)
            deps.extend(self.readers.get(x, ()))
        return deps

    def _commit(self, tok, r, w):
        for x in r:
            self.readers.setdefault(x, []).append(tok)
        for x in w:
            self.writers[x] = tok
            self.readers[x] = []

    def _waits(self, eng, deps):
        need = {}
        for kind, key, val in deps:
            if kind == "E" and key == "pe" and eng == "pe":
                continue
            k = (kind, key)
            if val > need.get(k, 0):
                need[k] = val
        out = []
        for k, val in need.items():
            if self.waited[eng].get(k, 0) >= val:
                continue
            self.waited[eng][k] = val
            sem = self.sem[k[1]] if k[0] == "E" else self.dsem[k[1]]
            out.append((sem, val))
        return out

    def op(self, eng, fn, r=(), w=(), sig=True):
        rec = _Rec()
        fn(rec)
        name_, a_, k_ = rec.call
        fn = lambda e, name_=name_, a_=a_, k_=k_: getattr(e, name_)(*a_, **k_)
        deps = self._deps(r, w)
        waits = self._waits(eng, deps)
        if sig:
            self.count[eng] += 1
            tok = ("E", eng, self.count[eng])
        else:
            tok = ("E", eng, self.count[eng] + 1)
        self._commit(tok, r, w)
        sem = self.sem[eng]

        def emit(e, waits=waits, fn=fn, sig=sig, sem=sem):
            for s, v in waits:
                e.wait_ge(s, v)
            ins = fn(e)
            if sig:
                ins.then_inc(sem, 1)
        self.stream[eng].append(emit)
        self.nops += 1
        return tok

    def dma(self, q, out, in_, slot, r=(), w=()):
        deps = self._deps(r, w)
        waits = self._waits(q, deps)
        sem = self._dma_sem(slot)
        self.dcount[slot] += 16
        tok = ("D", slot, self.dcount[slot])
        self._commit(tok, r, w)

        def emit(e, waits=waits, sem=sem, out=out, in_=in_):
            for s, v in waits:
                e.wait_ge(s, v)
            e.dma_start(out=out, in_=in_).then_inc(sem, 16)
        self.stream[q].append(emit)
        return tok

    def finish(self, eng="sp"):
        deps = []
        for e in self.sem:
            if self.count[e]:
                deps.append(("E", e, self.count[e]))
        for s, c in self.dcount.items():
            if c:
                deps.append(("D", s, c))
        waits = self._waits(eng, deps)

        def emit(e, waits=waits):
            for s, v in waits:
                e.wait_ge(s, v)
        self.stream[eng].append(emit)

    def phase_end(self):
        self.finish("sp")
        self.flush()
        self.stream = {e: [] for e in self.ENGS}

    def flush(self):
        with self.nc.Block() as block:
            for name, deco in (("sp", block.sync), ("pe", block.tensor), ("act", block.scalar),
                               ("dve", block.vector), ("pool", block.gpsimd)):
                lst = self.stream[name]

                def body(e, lst=lst):
                    for f in lst:
                        f(e)
                deco(body)


def _mix_cols():
    widths = (("na_q", 256), ("na_k", 256), ("na_v", 256), ("ml_q", 256), ("ml_k", 256), ("ml_v", 256),
              ("ml_o", 256), ("ml_if", 16), ("gla_q", 128), ("gla_k", 128), ("gla_v", 256), ("gla_g", 256),
              ("gla_a", 32), ("da_q", 256), ("da_k", 256), ("da_v", 256), ("gates", 4096))
    off, o = {}, 0
    for n, w in widths:
        off[n] = (o, w)
        o += w
    return off, o


MIXOFF, NCOLS = _mix_cols()


def xm_keys(b, t0, n):
    return [("XM", b, tt) for tt in range(t0 // 128, (t0 + n) // 128)]


class Builder:
    def __init__(self, debug=None):
        self.debug = debug
        self.nc = bass.Bass("TRN2", target_bir_lowering=False)
        self.es = ExitStack()
        self.tk = Tracker(self.nc, self.es)
        self.rr = 0

    def dram_in(self, name, shape, dt=F32):
        return self.nc.dram_tensor(name, list(shape), dt, kind="ExternalInput").ap()

    def dram_out(self, name, shape, dt=F32):
        return self.nc.dram_tensor(name, list(shape), dt, kind="ExternalOutput").ap()

    def dram_scr(self, name, shape, dt):
        return self.nc.dram_tensor(name, list(shape), dt, kind="Internal").ap()

    def sb(self, name, shape, dt):
        return self.es.enter_context(self.nc.sbuf_tensor(name, list(shape), dt))

    def ps(self, name, shape, dt=F32):
        return self.es.enter_context(self.nc.psum_tensor(name, list(shape), dt))

    def mm(self, out, lhsT, rhs, start, stop, r, w, sig=None, sgc=False):
        if sig is None:
            sig = stop
        return self.tk.op("pe", lambda e: e.matmul(out, lhsT=lhsT, rhs=rhs, start=start, stop=stop, skip_group_check=sgc),
                          r=r, w=w, sig=sig)

    def anyeng(self, engs=("dve", "pool", "act")):
        self.rr += 1
        return engs[self.rr % len(engs)]


GRID_W = 64
NEG = -30000.0
LAMBDA_INIT = [0.8 - 0.6 * math.exp(-0.3 * l) for l in range(DEPTH)]


def na_geometry():
    pats, pat_index, per_tile = [], {}, []
    k = np.arange(128)
    q = np.arange(128)
    for i in range(16):
        qr = (2 * i + q // 64)[None, :]
        qc = (q % 64)[None, :]
        r0 = np.clip(qr - 4, 0, 32 - 8)
        c0 = np.clip(qc - 8, 0, GRID_W - 16)
        lst = []
        for j in range(16):
            kr = (2 * j + k // 64)[:, None]
            kc = (k % 64)[:, None]
            valid = (kr >= r0) & (kr < r0 + 8) & (kc >= c0) & (kc < c0 + 16)
            if not valid.any():
                continue
            dr = np.where(valid, kr - qr + 7, 0).astype(np.int64)
            dc = np.where(valid, np.clip(kc - qc + 15, 0, 30), 0).astype(np.int64)
            key = valid.tobytes() + dr.tobytes() + dc.tobytes()
            if key not in pat_index:
                pat_index[key] = len(pats)
                pats.append((valid, dr, dc))
            lst.append((j, pat_index[key]))
        per_tile.append(lst)
    return pats, per_tile


NA_PATS, NA_TILES = na_geometry()
NPAT = len(NA_PATS)


def build_program(debug=None):
    debug = debug or {}
    B = Builder(debug)
    nc, tk = B.nc, B.tk
    L = DEPTH

    x_in = B.dram_in("x_in", [BPC, T, D])
    cT_in = B.dram_in("cT", [128, 8, 5])
    w_ada = B.dram_in("w_ada", [L, D, NMOD * D])
    b_adaT = B.dram_in("b_adaT", [128, L, 72])
    lnT = B.dram_in("lnT", [128, L, 3, 2, 8])
    ffn_w_in = B.dram_in("ffn_w_in", [L, 2, D, 2 * DFF])
    ffn_w_out = B.dram_in("ffn_w_out", [L, 2, DFF, D])
    w_mix = B.dram_in("w_mix_in", [L, D, NCOLS])
    w_sw = B.dram_in("w_mix_sw", [L, D, 512])
    w_branch = B.dram_in("w_branch", [L, 4, 256, D])
    w_outp = B.dram_in("w_out", [L, D, D])
    nab_in = B.dram_in("nab", [L, 4, NPAT, 128, 128])
    consts_in = B.dram_in("consts", [128, 1024])
    rope_in = B.dram_in("rope", [2, 128, SEQ])
    gb_in = B.dram_in("ml_gate_b", [L, 16])
    gains_in = B.dram_in("gainsT", [128, L, 4, 2])
    wa2_in = B.dram_in("gla_wa2", [L, 2, 17, 128])
    dal_in = B.dram_in("da_lambda", [L, 128])
    out_d = B.dram_out("out", [BPC, SEQ, D])
    dbg_y = B.dram_out("dbg_y", [4, 128, 2, T], BF16) if debug.get("dump_y") else None
    XM = B.dram_scr("xm", [BPC, 128, 8, T], F32)
    WIN = B.dram_scr("win_bf", [L, 2, NJ // 2, 128, 8, 2, 256], BF16)
    WOUT = B.dram_scr("wout_bf", [L, 2, 8, 128, NJ, 128], BF16)
    WMS = B.dram_scr("wmix_bf", [L, 128, 128, 8, 256], BF16)
    WBSC = B.dram_scr("wbr_bf", [L, 8, 128, 4, 2, 128], BF16)
    wcache = {}

    consts = B.sb("consts_sb", [128, 1024], F32)
    ident = consts[:, 0:128]
    tri = [consts[:, 128:256], consts[:, 256:384]]
    bd4 = consts[:, 384:640]
    bd2 = consts[:, 640:770]
    hm = consts[:, 770:774]
    mAB = consts[:, 774:776]
    identb = B.sb("identb", [128, 128], BF16)
    ones_f = B.sb("ones_f", [128, 128], F32)
    cT = B.sb("cT_sb", [128, 8, 5], F32)
    modp = B.sb("modp", [128, L, 72, 5], F32)
    sc1 = B.sb("sc1", [128, L, 3, 8, 5], F32)
    gsc = B.sb("gsc", [128, L, 3, 8, 5], F32)
    badaT = B.sb("badaT", [128, L, 72], F32)
    lnp = B.sb("lnp", [128, L, 3, 2, 8], F32)
    geff = B.sb("geff", [128, L, 4, 2], F32)
    nlam = B.sb("nlam", [128, L], F32)
    gbias = B.sb("gbias", [128, L, 16], F32)
    dl = B.sb("dl", [128, 128], F32)
    dls = B.sb("dls", [128, 4], F32)
    pb = [B.ps(f"pb{i}", [128, 512], F32) for i in range(8)]
    rot = {}

    def nxt(name, items):
        i = rot.get(name, 0)
        rot[name] = i + 1
        return items[i % len(items)]

    def bank(name, ids):
        i = nxt(name, ids)
        return pb[i], f"pb{i}"

    def prologue():
        es = ExitStack()
        adaw = [es.enter_context(nc.sbuf_tensor(f"adaw{i}", [128, 1152], F32)) for i in range(2)]
        xtok = [es.enter_context(nc.sbuf_tensor(f"pxtok{i}", [128, D], F32)) for i in range(2)]
        stgs = [es.enter_context(nc.sbuf_tensor(f"pstg{i}", [128, 8, 128], F32)) for i in range(2)]
        tk.dma("sp", consts[:], consts_in[:, :], "c_consts", w=["consts"])
        tk.dma("sp", cT[:], cT_in[:, :, :], "c_ct", w=["cT"])
        tk.dma("sp", badaT[:], b_adaT[:, :, :], "c_bada", w=["badaT"])
        tk.dma("sp", lnp[:], lnT[:, :, :, :, :], "c_ln", w=["lnp"])
        tk.dma("sp", geff[:], gains_in[:, :, :, :], "c_geff", w=["geff"])
        for l in range(L):
            tk.dma("sp", gbias[:, l, :], gb_in[l:l + 1, :].partition_broadcast(128), f"c_gb{l}", w=[("gbias", l)])
        tk.op("pool", lambda e: e.memset(ones_f[:], 1.0), w=["ones_f"])
        tk.op("dve", lambda e: e.tensor_copy(out=identb[:], in_=ident), r=["consts"], w=["identb"])
        for l in range(L):
            tk.op("dve", lambda e, l=l: e.tensor_scalar(out=geff[:, l, 3, :], in0=geff[:, l, 3, :], scalar1=1.0 - LAMBDA_INIT[l],
                                                        scalar2=None, op0=ALU.mult), r=["geff"], w=["geff"])
            tk.dma("sp", dl[:], dal_in[l:l + 1, :].partition_broadcast(128), "c_dl", w=["dl"])
            tk.op("dve", lambda e: e.tensor_tensor(out=dl[:, 0:32], in0=dl[:, 0:32], in1=dl[:, 32:64], op=ALU.mult), r=["dl"], w=["dl"])
            tk.op("dve", lambda e: e.tensor_tensor(out=dl[:, 64:96], in0=dl[:, 64:96], in1=dl[:, 96:128], op=ALU.mult), r=["dl"], w=["dl"])
            tk.op("dve", lambda e: e.tensor_reduce(out=dls[:, 0:1], in_=dl[:, 0:32], axis=AX.X, op=ALU.add), r=["dl"], w=["dls"])
            tk.op("dve", lambda e: e.tensor_reduce(out=dls[:, 1:2], in_=dl[:, 64:96], axis=AX.X, op=ALU.add), r=["dl"], w=["dls"])
            tk.op("act", lambda e: e.activation(out=dls[:, 2:4], in_=dls[:, 0:2], func=AF.Exp), r=["dls"], w=["dls"])
            tk.op("dve", lambda e: e.tensor_tensor(out=dls[:, 0:1], in0=dls[:, 3:4], in1=dls[:, 2:3], op=ALU.subtract), r=["dls"], w=["dls"])
            tk.op("dve", lambda e, l=l: e.tensor_scalar(out=nlam[:, l:l + 1], in0=dls[:, 0:1], scalar1=-LAMBDA_INIT[l], scalar2=None,
                                                        op0=ALU.add), r=["dls"], w=["nlam"])
        tk.op("act", lambda e: e.activation(out=cT[:], in_=cT[:], func=AF.Silu), r=["cT"], w=["cT"])
        ai = 0
        for l in range(L):
            for cg in range(8):
                pbk = pb[cg % 2]
                for kc in range(8):
                    buf, bkey = adaw[ai % 2], f"adaw{ai % 2}"
                    ai += 1
                    tk.dma("sp", buf[:], w_ada[l, kc * 128:(kc + 1) * 128, cg * 1152:(cg + 1) * 1152], bkey, w=[bkey])
                    for m in range(9):
                        B.mm(pbk[:, m * 5:(m + 1) * 5], buf[:, m * 128:(m + 1) * 128], cT[:, kc, :],
                             start=(kc == 0 and m == 0), stop=(kc == 7), r=[bkey, "cT"], w=[f"pb{cg % 2}"], sig=(m == 8), sgc=True)
                tk.op("dve", lambda e, l=l, cg=cg, pbk=pbk: e.tensor_tensor(
                    out=modp[:, l, cg * 9:(cg + 1) * 9, :], in0=pbk[:, 0:45].rearrange("p (m c) -> p m c", c=5),
                    in1=badaT[:, l, cg * 9:(cg + 1) * 9].unsqueeze(2).to_broadcast([128, 9, 5]), op=ALU.add),
                    r=[f"pb{cg % 2}", "badaT"], w=["modp"])
        for l in range(L):
            for s in range(3):
                gmul = (0.5 if s != 1 else 1.0) / ALPHA
                tk.op("dve", lambda e, l=l, s=s: e.tensor_scalar(
                    out=sc1[:, l, s, :, :], in0=modp[:, l, (3 * s + 1) * 8:(3 * s + 2) * 8, :], scalar1=1.0, scalar2=None,
                    op0=ALU.add), r=["modp"], w=["sc1"])
                tk.op("dve", lambda e, l=l, s=s, gmul=gmul: e.tensor_scalar(
                    out=gsc[:, l, s, :, :], in0=modp[:, l, (3 * s + 2) * 8:(3 * s + 3) * 8, :], scalar1=gmul, scalar2=None,
                    op0=ALU.mult), r=["modp"], w=["gsc"])
        li = 0
        for b in range(BPC):
            for tt in range(NT):
                buf, bkey = xtok[li % 2], f"pxtok{li % 2}"
                stg, skey = stgs[li % 2], f"pstg{li % 2}"
                li += 1
                tk.dma("sp", buf[:], x_in[b, tt * 128:(tt + 1) * 128, :], bkey, w=[bkey])
                for half in range(2):
                    pbk, pkey = pb[2 + half], f"pb{2 + half}"
                    for q in range(4):
                        fc = half * 4 + q
                        tk.op("pe", lambda e, pbk=pbk, q=q, buf=buf, fc=fc: e.transpose(
                            pbk[:, q * 128:(q + 1) * 128], buf[:, fc * 128:(fc + 1) * 128], ident),
                            r=[bkey, "consts"], w=[pkey], sig=(q == 3))
                    if half:
                        tk.op("dve", lambda e, pbk=pbk, stg=stg: e.tensor_copy(
                            out=stg[:, 4:8, :], in_=pbk[:, :].rearrange("p (q t) -> p q t", q=4)), r=[pkey], w=[(skey, 1)])
                    else:
                        tk.op("act", lambda e, pbk=pbk, stg=stg: e.copy(
                            out=stg[:, 0:4, :], in_=pbk[:, :].rearrange("p (q t) -> p q t", q=4)), r=[pkey], w=[(skey, 0)])
                tk.dma("sp", XM[b, :, :, tt * 128:(tt + 1) * 128], stg[:, :, :], skey, r=[(skey, 0), (skey, 1)], w=[("XM", b, tt)])
        tk.phase_end()
        es.close()

    def ln_tiles(es, n, pfx):
        d = {"n": n, "cnt": 0}
        uid = nxt("uid", list(range(100)))
        d["sq"] = [es.enter_context(nc.sbuf_tensor(f"{pfx}sq{i}_u{uid}", [128, n], F32)) for i in range(2)]
        d["tmp"] = [es.enter_context(nc.sbuf_tensor(f"{pfx}tmp{i}_u{uid}", [128, n], F32)) for i in range(2)]
        for nm in ("mean", "rstd", "pre"):
            d[nm] = es.enter_context(nc.sbuf_tensor(f"{pfx}{nm}_u{uid}", [128, n], F32))
        d["pfx"] = pfx
        return d

    def layer_norm_store(lt, zt, ztk, xob, xok, l, s, b, t0, n):
        psS, psQ = pb[6], pb[7]
        pfx = lt["pfx"]
        mean, rstd, pre = lt["mean"], lt["rstd"], lt["pre"]
        mk, rk, pk = pfx + "mean", pfx + "rstd", pfx + "pre"
        for fc in range(8):
            i = lt["cnt"] % 2
            lt["cnt"] += 1
            sqb, sqk = lt["sq"][i], f"{pfx}sq{i}"
            tk.op("act", lambda e, sqb=sqb, fc=fc: e.activation(out=sqb[:, :n], in_=zt[:, fc, :n], func=AF.Square),
                  r=[(ztk, fc)], w=[sqk])
            B.mm(psS[:, :n], ones_f[:], zt[:, fc, :n], start=(fc == 0), stop=(fc == 7), r=["ones_f", (ztk, fc)], w=["pb6"], sig=True)
            B.mm(psQ[:, :n], ones_f[:], sqb[:, :n], start=(fc == 0), stop=(fc == 7), r=["ones_f", sqk], w=["pb7"], sig=True)
        tk.op("act", lambda e: e.activation(out=mean[:, :n], in_=psS[:, :n], func=AF.Copy, scale=1.0 / D), r=["pb6"], w=[mk])
        tk.op("dve", lambda e: e.tensor_tensor(out=pre[:, :n], in0=mean[:, :n], in1=mean[:, :n], op=ALU.mult), r=[mk], w=[pk])
        tk.op("dve", lambda e: e.scalar_tensor_tensor(out=rstd[:, :n], in0=psQ[:, :n], scalar=1.0 / D, in1=pre[:, :n],
                                                      op0=ALU.mult, op1=ALU.subtract), r=["pb7", pk], w=[rk])
        tk.op("dve", lambda e: e.tensor_scalar(out=rstd[:, :n], in0=rstd[:, :n], scalar1=0.0, scalar2=LN_EPS,
                                               op0=ALU.max, op1=ALU.add), r=[rk], w=[rk])
        tk.op("act", lambda e: e.activation(out=rstd[:, :n], in_=rstd[:, :n], func=AF.Sqrt), r=[rk], w=[rk])
        tk.op("dve", lambda e: e.reciprocal(out=rstd[:, :n], in_=rstd[:, :n]), r=[rk], w=[rk])
        tk.op("dve", lambda e: e.tensor_tensor(out=pre[:, :n], in0=mean[:, :n], in1=rstd[:, :n], op=ALU.mult), r=[mk, rk], w=[pk])
        for fc in range(8):
            i = lt["cnt"] % 2
            lt["cnt"] += 1
            tb, tkey = lt["tmp"][i], f"{pfx}tmp{i}"
            tk.op("pool", lambda e, tb=tb, fc=fc: e.tensor_tensor(out=tb[:, :n], in0=zt[:, fc, :n], in1=rstd[:, :n], op=ALU.mult),
                  r=[(ztk, fc), rk], w=[tkey])
            tk.op("dve", lambda e, tb=tb: e.tensor_tensor(out=tb[:, :n], in0=tb[:, :n], in1=pre[:, :n], op=ALU.subtract),
                  r=[tkey, pk], w=[tkey])
            tk.op("act", lambda e, tb=tb, fc=fc: e.activation(
                out=xob[:, fc, :n], in_=tb[:, :n], func=AF.Identity, scale=lnp[:, l, s, 0, fc:fc + 1], bias=lnp[:, l, s, 1, fc:fc + 1]),
                r=[tkey, "lnp"], w=[xok])
        tk.dma("sp", XM[b, :, :, t0:t0 + n], xob[:, :, :n], xok, r=[xok], w=xm_keys(b, t0, n))

    def ffn_phase(l, k, only=None):
        es = ExitStack()
        uid = nxt("uid", list(range(100)))
        A = lambda name, shape, dt: es.enter_context(nc.sbuf_tensor(f"{name}_u{uid}", list(shape), dt))
        xs = [A(f"fxs{i}", [128, 8, 512], F32) for i in range(2)]
        hT = A("fhT", [128, 8, 512], BF16)
        gT = A("fgT", [128, NJ, 512], BF16)
        sa = [A(f"fsa{i}", [128, 512], F32) for i in range(2)]
        zt = A("fzt", [128, 8, 512], F32)
        xo = [A(f"fxo{i}", [128, 8, 512], F32) for i in range(2)]
        winb = [A(f"fwin{i}", [128, 8, 2, 256], BF16) for i in range(2)]
        woutb = [A(f"fwout{i}", [128, NJ, 128], BF16) for i in range(2)]
        lt = ln_tiles(es, 512, "f")
        s = 0 if k == 0 else 2
        cnt = {"xs": 0, "win": 0, "wout": 0, "sa": 0, "xo": 0, "pa": 0, "py": 0}
        for b in range(BPC):
            for ti in range(5):
                if only is not None and b * 5 + ti >= only:
                    continue
                if l == L - 1 and k == 1 and ti == 4:
                    continue
                first = (b == 0 and ti == 0)
                t0, n = TILES512[ti]
                col = b if ti < 4 else 4
                xb, xk = xs[cnt["xs"] % 2], f"fxs{cnt['xs'] % 2}"
                cnt["xs"] += 1
                tk.dma("sp", xb[:, :, :n], XM[b, :, :, t0:t0 + n], xk, r=xm_keys(b, t0, n), w=[xk])
                for fc in range(8):
                    eng = ("dve", "pool")[fc % 2]
                    tk.op(eng, lambda e, fc=fc, xb=xb, col=col, n=n: e.tensor_scalar(
                        out=hT[:, fc, :n], in0=xb[:, fc, :n], scalar1=sc1[:, l, s, fc, col:col + 1],
                        scalar2=modp[:, l, (3 * s) * 8 + fc, col:col + 1], op0=ALU.mult, op1=ALU.add),
                        r=[xk, "sc1", "modp"], w=[("fhT", fc)])
                hkeys = [("fhT", fc) for fc in range(8)]
                for j in range(NJ):
                    if j % 2 == 0:
                        wb, wk = winb[cnt["win"] % 2], f"fwin{cnt['win'] % 2}"
                        cnt["win"] += 1
                        if first:
                            for ab in range(2):
                                c0 = ab * DFF + j * 128
                                tk.dma("pool", wb[:, :, ab, :], ffn_w_in[l, k, :, c0:c0 + 256].rearrange("(kc p) c -> p kc c", p=128),
                                       f"{wk}_{ab}", w=[(wk, ab)])
                            tk.dma("sp", WIN[l, k, j // 2, :, :, :, :], wb[:, :, :, :], f"{wk}_st", r=[(wk, 0), (wk, 1)], w=[("WIN", l, k, j // 2)])
                        else:
                            tk.dma("sp", wb[:, :, :, :], WIN[l, k, j // 2, :, :, :, :], f"{wk}_ld", r=[("WIN", l, k, j // 2)], w=[(wk, 0), (wk, 1)])
                    jj = j % 2
                    pa, pak = pb[cnt["pa"] % 2], f"pb{cnt['pa'] % 2}"
                    pbb, pbk = pb[2 + cnt["pa"] % 2], f"pb{2 + cnt['pa'] % 2}"
                    cnt["pa"] += 1
                    for kc in range(8):
                        B.mm(pa[:, :n], wb[:, kc, 0, jj * 128:(jj + 1) * 128], hT[:, kc, :n], start=(kc == 0), stop=(kc == 7),
                             r=[(wk, 0), hkeys[kc]], w=[pak])
                    for kc in range(8):
                        B.mm(pbb[:, :n], wb[:, kc, 1, jj * 128:(jj + 1) * 128], hT[:, kc, :n], start=(kc == 0), stop=(kc == 7),
                             r=[(wk, 1), hkeys[kc]], w=[pbk])
                    sab, sak = sa[cnt["sa"] % 2], f"fsa{cnt['sa'] % 2}"
                    cnt["sa"] += 1
                    tk.op("act", lambda e, sab=sab, pa=pa, n=n: e.activation(out=sab[:, :n], in_=pa[:, :n], func=AF.Silu), r=[pak], w=[sak])
                    tk.op("dve", lambda e, sab=sab, pbb=pbb, j=j, n=n: e.tensor_tensor(out=gT[:, j, :n], in0=sab[:, :n], in1=pbb[:, :n], op=ALU.mult),
                          r=[sak, pbk], w=[("fgT", j)])
                for fc in range(8):
                    wb, wk = woutb[cnt["wout"] % 2], f"fwout{cnt['wout'] % 2}"
                    cnt["wout"] += 1
                    if first:
                        for hh in range(2):
                            tk.dma("pool", wb[:, hh * 11:(hh + 1) * 11, :],
                                   ffn_w_out[l, k, hh * 1408:(hh + 1) * 1408, fc * 128:(fc + 1) * 128].rearrange("(j p) c -> p j c", p=128),
                                   f"{wk}_{hh}", w=[(wk, hh)])
                        tk.dma("sp", WOUT[l, k, fc, :, :, :], wb[:, :, :], f"{wk}_st", r=[(wk, 0), (wk, 1)], w=[("WOUT", l, k, fc)])
                    else:
                        tk.dma("sp", wb[:, :, :], WOUT[l, k, fc, :, :, :], f"{wk}_ld", r=[("WOUT", l, k, fc)], w=[(wk, 0), (wk, 1)])
                    py, pyk = pb[4 + cnt["py"] % 2], f"pb{4 + cnt['py'] % 2}"
                    cnt["py"] += 1
                    for j in range(NJ):
                        B.mm(py[:, :n], wb[:, j, :], gT[:, j, :n], start=(j == 0), stop=(j == NJ - 1), r=[(wk, j // 11), ("fgT", j)], w=[pyk])
                    tk.op("dve", lambda e, fc=fc, py=py, xb=xb, col=col, n=n: e.scalar_tensor_tensor(
                        out=zt[:, fc, :n], in0=py[:, :n], scalar=gsc[:, l, s, fc, col:col + 1], in1=xb[:, fc, :n],
                        op0=ALU.mult, op1=ALU.add), r=[pyk, "gsc", xk], w=[("fzt", fc)])
                xob, xok = xo[cnt["xo"] % 2], f"fxo{cnt['xo'] % 2}"
                cnt["xo"] += 1
                layer_norm_store(lt, zt, "fzt", xob, xok, l, s, b, t0, n)
        tk.phase_end()
        es.close()

    def epilogue():
        es = ExitStack()
        xs = [es.enter_context(nc.sbuf_tensor(f"exs{i}", [128, 8, 128], F32)) for i in range(2)]
        xtok = [es.enter_context(nc.sbuf_tensor(f"extok{i}", [128, D], F32)) for i in range(2)]
        li = 0
        for b in range(BPC):
            for tt in range(SEQ // 128):
                xb, xk = xs[li % 2], f"exs{li % 2}"
                ob, ok = xtok[li % 2], f"extok{li % 2}"
                li += 1
                tk.dma("sp", xb[:, :, :], XM[b, :, :, tt * 128:(tt + 1) * 128], xk, r=[("XM", b, tt)], w=[xk])
                for half in range(2):
                    pbk, pkey = pb[2 + half], f"pb{2 + half}"
                    for q in range(4):
                        fc = half * 4 + q
                        tk.op("pe", lambda e, pbk=pbk, q=q, xb=xb, fc=fc: e.transpose(
                            pbk[:, q * 128:(q + 1) * 128], xb[:, fc, :], ident), r=[xk, "consts"], w=[pkey], sig=(q == 3))
                    if half:
                        tk.op("dve", lambda e, pbk=pbk, ob=ob: e.tensor_copy(out=ob[:, 512:1024], in_=pbk[:, :]), r=[pkey], w=[(ok, 1)])
                    else:
                        tk.op("act", lambda e, pbk=pbk, ob=ob: e.copy(out=ob[:, 0:512], in_=pbk[:, :]), r=[pkey], w=[(ok, 0)])
                tk.dma("sp", out_d[b, tt * 128:(tt + 1) * 128, :], ob[:], ok, r=[(ok, 0), (ok, 1)], w=[("OUT", b, tt)])
        tk.phase_end()
        es.close()

    def mixer_phase(l):
        es = ExitStack()
        uid = nxt("uid", list(range(100)))
        A = lambda name, shape, dt: es.enter_context(nc.sbuf_tensor(f"{name}_u{uid}", list(shape), dt))
        hT = A("mhT", [128, 8, T], BF16)
        G = [A(f"mG{i}", [128, 2 * T], BF16) for i in range(4)]
        VA = A("mVA", [128, NT, 260], BF16)
        HACC = A("mH", [128, NT, 256], F32)
        yT = A("myT", [128, 4, 2, T], BF16)
        wsl = [A(f"mws{i}", [128, 8, 256], BF16) for i in range(2)]
        PT = [A(f"mPT{i}", [128, 512], BF16) for i in range(3)]
        tA = [A(f"mtA{i}", [128, 512], F32) for i in range(3)]
        ropeb = [A(f"mrope{i}", [128, 512], F32) for i in range(2)]
        nabb = A("mnab", [128, 5, 4, 128], BF16)
        sm = A("msm", [128, 64], F32)
        GIF = A("mGIF", [128, NT, 16], F32)
        LLc = A("mLLc", [128, NT, 8], F32)
        CS = A("mCS", [128, NT, 16], F32)
        UU = A("mUU", [128, NT, 8], F32)
        RR = A("mRR", [128, NT, 8], F32)
        DECP = A("mDECP", [128, NT, 2, 2], F32)
        Sst = [A(f"mS{i}", [128, 256], F32) for i in range(2)]
        Sbf = [A(f"mSb{i}", [128, 256], BF16) for i in range(2)]
        EE = [A(f"mEE{i}", [128, 128], F32) for i in range(6)]
        LL = A("mLL", [128, 128], F32)
        qkt = [A(f"mqk{i}", [128, 6, 128], BF16) for i in range(2)]
        ktk = [A(f"mktk{i}", [128, 128], BF16) for i in range(2)]
        wa2 = A("mwa2", [32, 2, 128], BF16)
        WBs = [A(f"mWB{i}", [128, 4, 2, 128], BF16) for i in range(2)]
        gwb = [A(f"mgw{i}", [128, 8, 128], BF16) for i in range(2)]
        accT = A("macc", [128, 8, 512], BF16)
        g32 = [G[i][:].bitcast(F32) for i in range(4)]
        xsb = g32[0][:, 0:2048].rearrange("p (c t) -> p c t", c=8)
        ztb = g32[1][:, 0:2048].rearrange("p (c t) -> p c t", c=8)
        xob = g32[2][:, 0:2048].rearrange("p (c t) -> p c t", c=8)
        lt = {"n": 256, "cnt": 0, "pfx": "m", "sq": [g32[3][:, 0:256], g32[3][:, 256:512]], "tmp": [g32[3][:, 512:768], g32[3][:, 768:1024]],
              "mean": g32[3][:, 1024:1280], "rstd": g32[3][:, 1280:1536], "pre": g32[3][:, 1536:1792]}
        GKEYS = [(f"mG{i}", h) for i in range(4) for h in range(2)]
        TAILKEYS = ["mxs0", "mxo0", "msq0", "msq1", "mtmp0", "mtmp1", "mmean", "mrstd", "mpre"] + [("mzt", fc) for fc in range(8)]

        def fence(to_tail):
            r_, w_ = (GKEYS, TAILKEYS) if to_tail else (TAILKEYS, GKEYS)
            tk.op("dve", lambda e: e.memset(sm[:, 60:61], 0.0), r=r_, w=w_)

        fm = lambda i: G[i][:].rearrange("p (c t) -> p c t", c=2)
        tm = lambda i: G[i][:].rearrange("p (t f) -> p t f", f=256)
        gk = lambda i, half: (f"mG{i}", half)
        wmix = w_mix[l]
        wcnt = {"n": 0}
        hkeys = [("mhT", fc) for fc in range(8)]
        TT_ORDER = {0: [16, 17] + list(range(16)), 1: [17, 16] + list(range(15, -1, -1))}

        def cached_slab(dst, dkey, src, n, ckey):
            if (l, ckey) not in wcache:
                wcache[(l, ckey)] = len([1 for k_ in wcache if k_[0] == l])
                idx = wcache[(l, ckey)]
                tk.dma("pool", dst[:, :, :n], src, dkey, w=[dkey])
                tk.dma("sp", WMS[l, idx, :, :, :n], dst[:, :, :n], dkey + "_st", r=[dkey], w=[("WMS", l, idx)])
            else:
                idx = wcache[(l, ckey)]
                tk.dma("sp", dst[:, :, :n], WMS[l, idx, :, :, :n], dkey + "_ld", r=[("WMS", l, idx)], w=[dkey])

        def load_w(src, n, ckey):
            i = wcnt["n"] % 2
            wcnt["n"] += 1
            wb, wk = wsl[i], f"mws{i}"
            cached_slab(wb, wk, src.rearrange("(kc p) c -> p kc c", p=128), n, ckey)
            return wb, wk

        def proj_fm(src, n, evac, banks=(0, 1)):
            wb, wk = load_w(src, n, ("c", src.offset, n))
            for (t0, nt) in TILES512:
                bk, bkey = bank("pj", banks)
                for kc in range(8):
                    B.mm(bk[:n, :nt], wb[:, kc, :n], hT[:, kc, t0:t0 + nt], start=(kc == 0), stop=(kc == 7), r=[wk, hkeys[kc]], w=[bkey])
                evac(bk, bkey, t0, nt)

        def proj_tm(src, n, evac, banks=(0, 1)):
            wb, wk = load_w(src, n, ("c", src.offset, n))
            for tt in range(NT):
                bk, bkey = bank("pj", banks)
                for kc in range(8):
                    B.mm(bk[:, :n], hT[:, kc, tt * 128:(tt + 1) * 128], wb[:, kc, :n], start=(kc == 0), stop=(kc == 7), r=[wk, hkeys[kc]], w=[bkey])
                evac(bk, bkey, tt)

        def half_of(t0):
            return 0 if t0 < T // 2 else 1

        def finish_branch(j, norm, center, gate_i):
            for tt in range(NT):
                hk = ("mH", tt)
                src = HACC[:, tt, :]
                if norm:
                    x3 = HACC[:, tt, :].rearrange("p (h e) -> p h e", h=4)
                    t0_, t1_ = tA[0], tA[1]
                    if center:
                        tk.op("dve", lambda e, x3=x3: e.tensor_reduce(out=sm[:, 0:4], in_=x3, axis=AX.X, op=ALU.add), r=[hk], w=["msm"])
                        tk.op("dve", lambda e: e.tensor_scalar(out=sm[:, 0:4], in0=sm[:, 0:4], scalar1=-1.0 / 64, scalar2=None, op0=ALU.mult),
                              r=["msm"], w=["msm"])
                        tk.op("dve", lambda e, x3=x3, t0_=t0_: e.tensor_tensor(
                            out=t0_[:, 0:256].rearrange("p (h e) -> p h e", h=4), in0=x3,
                            in1=sm[:, 0:4].unsqueeze(2).to_broadcast([128, 4, 64]), op=ALU.add), r=[hk, "msm"], w=["mtA0"])
                        xc, xck = t0_[:, 0:256], "mtA0"
                    else:
                        xc, xck = src, hk
                    tk.op("act", lambda e, xc=xc, t1_=t1_: e.activation(out=t1_[:, 0:256], in_=xc, func=AF.Square), r=[xck], w=["mtA1"])
                    tk.op("dve", lambda e, t1_=t1_: e.tensor_reduce(out=sm[:, 4:8], in_=t1_[:, 0:256].rearrange("p (h e) -> p h e", h=4),
                                                                    axis=AX.X, op=ALU.add), r=["mtA1"], w=["msm"])
                    tk.op("dve", lambda e: e.tensor_scalar(out=sm[:, 4:8], in0=sm[:, 4:8], scalar1=1.0 / 64, scalar2=1e-6, op0=ALU.mult, op1=ALU.add),
                          r=["msm"], w=["msm"])
                    tk.op("act", lambda e: e.activation(out=sm[:, 4:8], in_=sm[:, 4:8], func=AF.Sqrt), r=["msm"], w=["msm"])
                    tk.op("dve", lambda e: e.reciprocal(out=sm[:, 4:8], in_=sm[:, 4:8]), r=["msm"], w=["msm"])
                    tk.op("dve", lambda e, xc=xc, t1_=t1_: e.tensor_tensor(
                        out=t1_[:, 0:256].rearrange("p (h e) -> p h e", h=4), in0=xc.rearrange("p (h e) -> p h e", h=4),
                        in1=sm[:, 4:8].unsqueeze(2).to_broadcast([128, 4, 64]), op=ALU.mult), r=[xck, "msm"], w=["mtA1"])
                    cur, curk = t1_[:, 0:256], "mtA1"
                    if gate_i is not None:
                        tk.op("dve", lambda e, t1_=t1_, tt=tt: e.tensor_tensor(out=t1_[:, 0:256], in0=t1_[:, 0:256], in1=tm(gate_i)[:, tt, :], op=ALU.mult),
                              r=["mtA1", gk(gate_i, tt // 9)], w=["mtA1"])
                else:
                    cur, curk = src, hk
                bk, bkey = bank("fin", (2, 3))
                for c in range(2):
                    tk.op("pe", lambda e, bk=bk, c=c, cur=cur: e.transpose(bk[:, c * 128:(c + 1) * 128], cur[:, c * 128:(c + 1) * 128], ident),
                          r=[curk, "consts"], w=[bkey], sig=(c == 1))
                tk.op("act", lambda e, bk=bk, tt=tt: e.copy(out=yT[:, j, :, tt * 128:(tt + 1) * 128], in_=bk[:, 0:256].rearrange("p (c t) -> p c t", c=2)),
                      r=[bkey], w=[("myT", j, tt)])

        def branch_na(b):
            QT, KT = fm(0), fm(1)
            for c in range(2):
                proj_fm(wmix[:, c * 128:(c + 1) * 128], 128,
                        lambda bk, bkey, t0, nt, c=c: tk.op("act", lambda e: e.activation(out=QT[:, c, t0:t0 + nt], in_=bk[:, :nt], func=AF.Copy, scale=0.125),
                                                            r=[bkey], w=[gk(0, c)]))
                proj_fm(wmix[:, 256 + c * 128:256 + (c + 1) * 128], 128,
                        lambda bk, bkey, t0, nt, c=c: tk.op("dve", lambda e: e.tensor_copy(out=KT[:, c, t0:t0 + nt], in_=bk[:, :nt]),
                                                            r=[bkey], w=[gk(1, c)]))
            tk.op("pool", lambda e: e.memset(VA[:, :, :].rearrange("p t (h e) -> p (t h) e", h=4)[:, :, 64:65], 1.0), w=[("mVA", tt) for tt in range(NT)])
            proj_tm(wmix[:, 512:768], 256,
                    lambda bk, bkey, tt: tk.op("dve", lambda e: e.tensor_copy(
                        out=VA[:, tt, :].rearrange("p (h e) -> p h e", h=4)[:, :, 0:64], in_=bk[:, 0:256].rearrange("p (h e) -> p h e", h=4)),
                        r=[bkey], w=[("mVA", tt)]))
            for i in range(NT):
                local = NA_TILES[i] if i < 16 else []
                for slot, (j, pid) in enumerate(local):
                    tk.dma("pool", nabb[:, slot, :, :], nab_in[l, :, pid, :, :].rearrange("h k q -> k h q"), f"mnab{slot}", w=[("mnab", slot)])
                ktiles = [(j, slot) for slot, (j, pid) in enumerate(local)] + [(16, None), (17, None)]
                ob, okey = bank("naO", (4, 5))
                for h in range(4):
                    c, base = h // 2, (h % 2) * 64
                    for idx, (j, slot) in enumerate(ktiles):
                        sbk, skey = bank("naS", (0, 1, 2, 3))
                        B.mm(sbk[:, :128], KT[base:base + 64, c, j * 128:(j + 1) * 128], QT[base:base + 64, c, i * 128:(i + 1) * 128],
                             start=True, stop=(slot is None), r=[gk(1, c), gk(0, c)], w=[skey], sig=True)
                        if slot is not None:
                            B.mm(sbk[:, :128], identb[:], nabb[:, slot, h, :], start=False, stop=True, r=["identb", ("mnab", slot)], w=[skey], sig=True)
                        pi = nxt("pt", (0, 1, 2))
                        pt, ptk = PT[pi], f"mPT{pi}"
                        tk.op("act", lambda e, pt=pt, sbk=sbk: e.activation(out=pt[:, :128], in_=sbk[:, :128], func=AF.Exp), r=[skey], w=[ptk])
                        B.mm(ob[:, h * 65:(h + 1) * 65], pt[:, :128], VA[:, j, h * 65:(h + 1) * 65], start=(idx == 0), stop=(idx == len(ktiles) - 1),
                             r=[ptk, ("mVA", j)], w=[okey], sig=True)
                o3 = ob[:, 0:260].rearrange("p (h e) -> p h e", h=4)
                tk.op("dve", lambda e, o3=o3: e.reciprocal(out=sm[:, 8:12], in_=o3[:, :, 64]), r=[okey], w=["msm"])
                tk.op("dve", lambda e, o3=o3, i=i: e.tensor_tensor(out=HACC[:, i, :].rearrange("p (h e) -> p h e", h=4), in0=o3[:, :, 0:64],
                                                                  in1=sm[:, 8:12].unsqueeze(2).to_broadcast([128, 4, 64]), op=ALU.mult),
                      r=[okey, "msm"], w=[("mH", i)])
            finish_branch(0, False, False, None)

        def branch_da(b):
            qA, qB, kTc = fm(0)[:, 0, :], fm(0)[:, 1, :], fm(1)[:, 0, :]
            tk.op("pool", lambda e: e.memset(VA[:, :, :].rearrange("p t (h e) -> p (t h) e", h=4)[:, :, 64:65], 1.0), w=[("mVA", tt) for tt in range(NT)])
            proj_tm(wmix[:, MIXOFF["da_v"][0]:MIXOFF["da_v"][0] + 256], 256,
                    lambda bk, bkey, tt: tk.op("dve", lambda e: e.tensor_copy(
                        out=VA[:, tt, :].rearrange("p (h e) -> p h e", h=4)[:, :, 0:64], in_=bk[:, 0:256].rearrange("p (h e) -> p h e", h=4)),
                        r=[bkey], w=[("mVA", tt)]))
            for c in range(2):
                for which in ("q", "k"):
                    col0 = MIXOFF["da_" + which][0] + c * 128
                    sw0 = (0 if which == "q" else 256) + c * 128
                    w1, w1k = load_w(wmix[:, col0:col0 + 128], 128, ("da1", which, c))
                    w2, w2k = load_w(w_sw[l][:, sw0:sw0 + 128], 128, ("da2", which, c))
                    for (t0, nt) in TILES512:
                        b1, b1k = bank("pj", (0, 1))
                        for kc in range(8):
                            B.mm(b1[:, :nt], w1[:, kc, :128], hT[:, kc, t0:t0 + nt], start=(kc == 0), stop=(kc == 7), r=[w1k, hkeys[kc]], w=[b1k])
                        r_, rk_ = tA[2], "mtA2"
                        if t0 < SEQ:
                            b2, b2k = bank("pj2", (2, 3))
                            for kc in range(8):
                                B.mm(b2[:, :nt], w2[:, kc, :128], hT[:, kc, t0:t0 + nt], start=(kc == 0), stop=(kc == 7), r=[w2k, hkeys[kc]], w=[b2k])
                            tk.dma("sp", ropeb[0][:, :nt], rope_in[0, :, t0:t0 + nt], "mrope0", w=["mrope0"])
                            tk.dma("sp", ropeb[1][:, :nt], rope_in[1, :, t0:t0 + nt], "mrope1", w=["mrope1"])
                            tk.op("dve", lambda e, b1=b1, nt=nt: e.tensor_tensor(out=tA[0][:, :nt], in0=b1[:, :nt], in1=ropeb[0][:, :nt], op=ALU.mult),
                                  r=[b1k, "mrope0"], w=["mtA0"])
                            tk.op("dve", lambda e, b2=b2, nt=nt: e.tensor_tensor(out=tA[1][:, :nt], in0=b2[:, :nt], in1=ropeb[1][:, :nt], op=ALU.mult),
                                  r=[b2k, "mrope1"], w=["mtA1"])
                            tk.op("pool", lambda e, nt=nt: e.tensor_tensor(out=r_[:, :nt], in0=tA[0][:, :nt], in1=tA[1][:, :nt], op=ALU.add),
                                  r=["mtA0", "mtA1"], w=[rk_])
                        else:
                            tk.op("pool" if False else "dve", lambda e, b1=b1, nt=nt: e.tensor_copy(out=r_[:, :nt], in_=b1[:, :nt]), r=[b1k], w=[rk_])
                        if which == "q":
                            tk.op("act", lambda e, t0=t0, nt=nt: e.activation(out=qA[:, t0:t0 + nt], in_=r_[:, :nt], func=AF.Copy, scale=mAB[:, 0:1]),
                                  r=[rk_, "consts"], w=[gk(0, 0)])
                            tk.op("act", lambda e, t0=t0, nt=nt: e.activation(out=qB[:, t0:t0 + nt], in_=r_[:, :nt], func=AF.Copy, scale=mAB[:, 1:2]),
                                  r=[rk_, "consts"], w=[gk(0, 1)])
                        else:
                            tk.op("act", lambda e, t0=t0, nt=nt: e.copy(out=kTc[:, t0:t0 + nt], in_=r_[:, :nt]), r=[rk_], w=[gk(1, 0)])
                for qt, (q0, qn) in enumerate(TILES512):
                    keyt = list(range(NT)) if qt < 4 else [16, 17]
                    nsub = qn // 128
                    for hl in range(2):
                        h, base = 2 * c + hl, hl * 64
                        obs = [bank("daO", (4, 5, 6, 7)) for _ in range(2)]
                        steps = [(kt, m) for kt in keyt for m in range(2)]

                        def smm(step):
                            kt, m = steps[step]
                            sbk, skey = bank("daS", (0, 1, 2, 3))
                            qsrc, qk_ = (qA, gk(0, 0)) if m == 0 else (qB, gk(0, 1))
                            B.mm(sbk[:, :qn], kTc[base:base + 64, kt * 128:(kt + 1) * 128], qsrc[base:base + 64, q0:q0 + qn],
                                 start=True, stop=True, r=[gk(1, 0), qk_], w=[skey], sig=True)
                            return sbk, skey
                        nxt_s = smm(0)
                        for step, (kt, m) in enumerate(steps):
                            sbk, skey = nxt_s
                            if step + 1 < len(steps):
                                nxt_s = smm(step + 1)
                            pi = nxt("pt", (0, 1, 2))
                            pt, ptk = PT[pi], f"mPT{pi}"
                            tk.op("act", lambda e, pt=pt, sbk=sbk: e.activation(out=pt[:, :qn], in_=sbk[:, :qn], func=AF.Exp), r=[skey], w=[ptk])
                            ob, okey = obs[m]
                            first, last = (kt == keyt[0]), (kt == keyt[-1])
                            for sub in range(nsub):
                                B.mm(ob[:, sub * 65:(sub + 1) * 65], pt[:, sub * 128:(sub + 1) * 128], VA[:, kt, h * 65:(h + 1) * 65],
                                     start=(first and sub == 0), stop=last, r=[ptk, ("mVA", kt)], w=[okey],
                                     sig=(sub == nsub - 1), sgc=True)
                        o1 = obs[0][0][:, 0:nsub * 65].rearrange("p (s e) -> p s e", e=65)
                        o2 = obs[1][0][:, 0:nsub * 65].rearrange("p (s e) -> p s e", e=65)
                        k1, k2 = obs[0][1], obs[1][1]
                        tk.op("dve", lambda e, o1=o1: e.reciprocal(out=sm[:, 16:16 + nsub], in_=o1[:, :, 64]), r=[k1], w=["msm"])
                        tk.op("dve", lambda e, o2=o2: e.reciprocal(out=sm[:, 20:20 + nsub], in_=o2[:, :, 64]), r=[k2], w=["msm"])
                        tk.op("dve", lambda e: e.tensor_scalar(out=sm[:, 20:20 + nsub], in0=sm[:, 20:20 + nsub], scalar1=nlam[:, l:l + 1], scalar2=None,
                                                               op0=ALU.mult), r=["msm", "nlam"], w=["msm"])
                        t0_, t1_ = tA[0], tA[1]
                        tk.op("dve", lambda e, o1=o1: e.tensor_tensor(out=t0_[:, 0:nsub * 64].rearrange("p (s e) -> p s e", e=64), in0=o1[:, :, 0:64],
                                                                      in1=sm[:, 16:16 + nsub].unsqueeze(2).to_broadcast([128, nsub, 64]), op=ALU.mult),
                              r=[k1, "msm"], w=["mtA0"])
                        tk.op("dve", lambda e, o2=o2: e.tensor_tensor(out=t1_[:, 0:nsub * 64].rearrange("p (s e) -> p s e", e=64), in0=o2[:, :, 0:64],
                                                                      in1=sm[:, 20:20 + nsub].unsqueeze(2).to_broadcast([128, nsub, 64]), op=ALU.mult),
                              r=[k2, "msm"], w=["mtA1"])
                        tt0 = q0 // 128
                        tk.op("dve", lambda e, h=h, tt0=tt0: e.tensor_tensor(
                            out=HACC[:, tt0:tt0 + nsub, h * 64:(h + 1) * 64], in0=t0_[:, 0:nsub * 64].rearrange("p (s e) -> p s e", e=64),
                            in1=t1_[:, 0:nsub * 64].rearrange("p (s e) -> p s e", e=64), op=ALU.add),
                            r=["mtA0", "mtA1"], w=[("mH", tt0 + s_) for s_ in range(nsub)])
            finish_branch(3, True, False, None)

        def state_update(ci, ds_bank, ds_key, ncol, dec_ap, dec_key, bdm):
            S, Sb = Sst[ci], Sbf[ci]
            sk, sbk_ = f"mS{ci}", f"mSb{ci}"
            t2, t2k = tA[2], "mtA2"
            tk.op("dve", lambda e: e.scalar_tensor_tensor(out=t2[:, :ncol], in0=ds_bank[:, :ncol], scalar=dec_ap, in1=bdm, op0=ALU.mult, op1=ALU.mult),
                  r=[ds_key, dec_key, "consts"], w=[t2k])
            tk.op("dve", lambda e: e.scalar_tensor_tensor(out=S[:, :ncol], in0=S[:, :ncol], scalar=dec_ap, in1=t2[:, :ncol], op0=ALU.mult, op1=ALU.add),
                  r=[sk, dec_key, t2k], w=[sk])
            tk.op("pool", lambda e: e.tensor_copy(out=Sb[:, :], in_=S[:, :]), r=[sk], w=[sbk_])

        def branch_ml(b):
            QT, KT, KTOK, VRAW = fm(0), fm(1), tm(2), tm(3)
            o = MIXOFF
            for c in range(2):
                proj_fm(wmix[:, o["ml_q"][0] + c * 128:o["ml_q"][0] + (c + 1) * 128], 128,
                        lambda bk, bkey, t0, nt, c=c: tk.op("act", lambda e: e.copy(out=QT[:, c, t0:t0 + nt], in_=bk[:, :nt]), r=[bkey], w=[gk(0, c)]))
                proj_fm(wmix[:, o["ml_k"][0] + c * 128:o["ml_k"][0] + (c + 1) * 128], 128,
                        lambda bk, bkey, t0, nt, c=c: tk.op("dve", lambda e: e.tensor_copy(out=KT[:, c, t0:t0 + nt], in_=bk[:, :nt]), r=[bkey], w=[gk(1, c)]))
            proj_tm(wmix[:, o["ml_k"][0]:o["ml_k"][0] + 256], 256,
                    lambda bk, bkey, tt: tk.op("act", lambda e: e.copy(out=KTOK[:, tt, :], in_=bk[:, 0:256]), r=[bkey], w=[gk(2, tt // 9)]))
            proj_tm(wmix[:, o["ml_v"][0]:o["ml_v"][0] + 256], 256,
                    lambda bk, bkey, tt: tk.op("dve", lambda e: e.tensor_copy(out=VRAW[:, tt, :], in_=bk[:, 0:256]), r=[bkey], w=[gk(3, tt // 9)]))
            proj_tm(wmix[:, o["ml_if"][0]:o["ml_if"][0] + 16], 16,
                    lambda bk, bkey, tt: tk.op("dve", lambda e: e.tensor_tensor(out=GIF[:, tt, :], in0=bk[:, 0:16], in1=gbias[:, l, :], op=ALU.add),
                                               r=[bkey, ("gbias", l)], w=["mGIF"]))
            if debug.get("ml_stop") == 1:
                return
            for d in range(2):
                tk.op("act", lambda e, d=d: e.activation(out=LLc[:, :, d * 4:(d + 1) * 4], in_=GIF[:, :, 4 + 8 * d:8 + 8 * d], func=AF.Exp, scale=-1.0),
                      r=["mGIF"], w=["mLLc"])
            tk.op("act", lambda e: e.activation(out=LLc[:, :, :], in_=LLc[:, :, :], func=AF.Ln, bias=1.0), r=["mLLc"], w=["mLLc"])
            for tt in range(NT):
                bk, bkey = bank("pj", (0, 1))
                B.mm(bk[:, 0:4], tri[0], LLc[:, tt, 0:4], start=True, stop=True, r=["consts", "mLLc"], w=[bkey], sig=True)
                B.mm(bk[:, 4:8], tri[1], LLc[:, tt, 4:8], start=True, stop=True, r=["consts", "mLLc"], w=[bkey], sig=True)
                B.mm(bk[:, 8:16], ones_f[:], LLc[:, tt, 0:8], start=True, stop=True, r=["ones_f", "mLLc"], w=[bkey], sig=True)
                tk.op("act", lambda e, bk=bk, tt=tt: e.copy(out=CS[:, tt, :], in_=bk[:, 0:16]), r=[bkey], w=["mCS"])
            if debug.get("ml_stop") == 2:
                return
            for d in range(2):
                tk.op("dve", lambda e, d=d: e.tensor_tensor(out=UU[:, :, d * 4:(d + 1) * 4], in0=GIF[:, :, 8 * d:8 * d + 4], in1=CS[:, :, d * 4:(d + 1) * 4], op=ALU.add),
                      r=["mGIF", "mCS"], w=["mUU"])
            tk.op("act", lambda e: e.activation(out=UU[:, :, :], in_=UU[:, :, :], func=AF.Exp), r=["mUU"], w=["mUU"])
            tk.op("act", lambda e: e.activation(out=RR[:, :, :], in_=CS[:, :, 0:8], func=AF.Exp, scale=-1.0, bias=math.log(0.125)), r=["mCS"], w=["mRR"])
            tk.op("act", lambda e: e.activation(out=CS[:, :, 8:16], in_=CS[:, :, 8:16], func=AF.Exp, scale=-1.0), r=["mCS"], w=["mCS"])
            for d in range(2):
                for c in range(2):
                    for hl in range(2):
                        tk.op("dve", lambda e, d=d, c=c, hl=hl: e.tensor_copy(
                            out=DECP[hl * 64:(hl + 1) * 64, :, d, c:c + 1], in_=CS[hl * 64:(hl + 1) * 64, :, 8 + d * 4 + 2 * c + hl:8 + d * 4 + 2 * c + hl + 1]),
                            r=["mCS"], w=["mDECP"])
            if debug.get("ml_stop") == 3:
                return
            for d in range(2):
                for tt in range(NT):
                    va3 = VA[:, tt, :].rearrange("p (h e) -> p h e", h=4)
                    for h in range(4):
                        tk.op("act", lambda e, tt=tt, d=d, h=h: e.activation(
                            out=VA[:, tt, h * 65:h * 65 + 64], in_=VRAW[:, tt, h * 64:(h + 1) * 64], func=AF.Copy, scale=UU[:, tt, d * 4 + h:d * 4 + h + 1]),
                            r=[gk(3, tt // 9), "mUU"], w=[("mVA", tt)])
                    tk.op("act", lambda e, va3=va3, tt=tt, d=d: e.copy(out=va3[:, :, 64], in_=UU[:, tt, d * 4:(d + 1) * 4]),
                          r=["mUU"], w=[("mVA", tt)])
                if debug.get("ml_stop") == 4:
                    return
                for c in range(2):
                    tk.op("dve", lambda e, c=c: e.memset(Sst[c][:, :], 0.0), w=[f"mS{c}"])
                    tk.op("pool", lambda e, c=c: e.memset(Sbf[c][:, :], 0.0), w=[f"mSb{c}"])
                for tt in TT_ORDER[d]:
                    ts_ = slice(tt * 128, (tt + 1) * 128)
                    for c in range(2):
                        pi = nxt("pt", (0, 1, 2))
                        at, atk = PT[pi], f"mPT{pi}"
                        for hl in range(2):
                            sbk, skey = bank("scS", (0, 1, 6, 7))
                            B.mm(sbk[:, 0:128], KT[hl * 64:(hl + 1) * 64, c, ts_], QT[hl * 64:(hl + 1) * 64, c, ts_],
                                 start=True, stop=True, r=[gk(1, c), gk(0, c)], w=[skey], sig=True)
                            tk.op("dve", lambda e, at=at, sbk=sbk, d=d, hl=hl: e.tensor_tensor(
                                out=at[:, hl * 128:(hl + 1) * 128], in0=sbk[:, 0:128], in1=tri[d], op=ALU.mult),
                                r=[skey, "consts"], w=[atk])
                        if debug.get("ml_scan", 9) < 7:
                            continue
                        ob, okey = bank("scO", (2, 3))
                        B.mm(ob[:, 0:130], QT[:, c, ts_], Sbf[c][:, 0:130], start=True, stop=False, r=[gk(0, c), f"mSb{c}"], w=[okey], sig=True)
                        for hl in range(2):
                            h = 2 * c + hl
                            B.mm(ob[:, hl * 65:(hl + 1) * 65], at[:, hl * 128:(hl + 1) * 128], VA[:, tt, h * 65:(h + 1) * 65], start=False, stop=(hl == 1),
                                 r=[atk, ("mVA", tt)], w=[okey], sig=True)
                        if debug.get("ml_scan", 9) < 8:
                            continue
                        dbk, dkey = bank("scD", (4, 5))
                        B.mm(dbk[:, 0:130], KTOK[:, tt, c * 128:(c + 1) * 128], VA[:, tt, 2 * c * 65:(2 * c + 2) * 65], start=True, stop=True,
                             r=[gk(2, tt // 9), ("mVA", tt)], w=[dkey], sig=True)
                        state_update(c, dbk, dkey, 130, DECP[:, tt, d, c:c + 1], "mDECP", bd2)
                        if debug.get("ml_scan", 9) < 9:
                            continue
                        o3 = ob[:, 0:130].rearrange("p (h e) -> p h e", h=2)
                        rr = RR[:, tt, d * 4 + 2 * c:d * 4 + 2 * c + 2]
                        tk.op("dve", lambda e, o3=o3, rr=rr: e.tensor_tensor(out=sm[:, 24:26], in0=o3[:, :, 64], in1=rr, op=ALU.mult), r=[okey, "mRR"], w=["msm"])
                        tk.op("dve", lambda e: e.scalar_tensor_tensor(out=sm[:, 26:28], in0=sm[:, 24:26], scalar=-1.0, in1=sm[:, 24:26], op0=ALU.mult, op1=ALU.max),
                              r=["msm"], w=["msm"])
                        tk.op("dve", lambda e: e.tensor_scalar(out=sm[:, 26:28], in0=sm[:, 26:28], scalar1=1.0, scalar2=None, op0=ALU.max), r=["msm"], w=["msm"])
                        tk.op("dve", lambda e: e.reciprocal(out=sm[:, 26:28], in_=sm[:, 26:28]), r=["msm"], w=["msm"])
                        tk.op("dve", lambda e, rr=rr: e.tensor_tensor(out=sm[:, 28:30], in0=sm[:, 26:28], in1=rr, op=ALU.mult), r=["msm", "mRR"], w=["msm"])
                        hv = HACC[:, tt, 2 * c * 64:(2 * c + 2) * 64].rearrange("p (h e) -> p h e", h=2)
                        if d == 0:
                            tk.op("dve", lambda e, o3=o3, hv=hv: e.tensor_tensor(out=hv, in0=o3[:, :, 0:64],
                                                                               in1=sm[:, 28:30].unsqueeze(2).to_broadcast([128, 2, 64]), op=ALU.mult),
                                  r=[okey, "msm"], w=[("mH", tt)])
                        else:
                            t0_ = tA[0]
                            tk.op("dve", lambda e, o3=o3, t0_=t0_: e.tensor_tensor(out=t0_[:, 0:128].rearrange("p (h e) -> p h e", h=2), in0=o3[:, :, 0:64],
                                                                                   in1=sm[:, 28:30].unsqueeze(2).to_broadcast([128, 2, 64]), op=ALU.mult),
                                  r=[okey, "msm"], w=["mtA0"])
                            tk.op("dve", lambda e, hv=hv, t0_=t0_: e.tensor_tensor(out=hv, in0=hv, in1=t0_[:, 0:128].rearrange("p (h e) -> p h e", h=2), op=ALU.add),
                                  r=["mtA0", ("mH", tt)], w=[("mH", tt)])
            if debug.get("ml_stop") == 5:
                return
            gate = tm(0)
            proj_tm(wmix[:, o["ml_o"][0]:o["ml_o"][0] + 256], 256,
                    lambda bk, bkey, tt: tk.op("act", lambda e: e.activation(out=gate[:, tt, :], in_=bk[:, 0:256], func=AF.Sigmoid),
                                               r=[bkey], w=[gk(0, tt // 9)]))
            finish_branch(1, True, True, 0)

        def branch_gla(b):
            QT, KT, KTOK, VRAW = fm(0)[:, 0, :], fm(0)[:, 1, :], tm(2), tm(3)
            AT = fm(1)
            o = MIXOFF
            proj_fm(wmix[:, o["gla_q"][0]:o["gla_q"][0] + 128], 128,
                    lambda bk, bkey, t0, nt: tk.op("act", lambda e: e.activation(out=QT[:, t0:t0 + nt], in_=bk[:, :nt], func=AF.Copy, scale=32 ** -0.5),
                                                   r=[bkey], w=[gk(0, 0)]))
            proj_fm(wmix[:, o["gla_k"][0]:o["gla_k"][0] + 128], 128,
                    lambda bk, bkey, t0, nt: tk.op("dve", lambda e: e.tensor_copy(out=KT[:, t0:t0 + nt], in_=bk[:, :nt]), r=[bkey], w=[gk(0, 1)]))
            proj_tm(wmix[:, o["gla_k"][0]:o["gla_k"][0] + 128], 128,
                    lambda bk, bkey, tt: tk.op("act", lambda e: e.copy(out=KTOK[:, tt, 0:128], in_=bk[:, 0:128]), r=[bkey], w=[gk(2, tt // 9)]))
            proj_tm(wmix[:, o["gla_v"][0]:o["gla_v"][0] + 256], 256,
                    lambda bk, bkey, tt: tk.op("dve", lambda e: e.tensor_copy(out=VRAW[:, tt, :], in_=bk[:, 0:256]), r=[bkey], w=[gk(3, tt // 9)]))
            for d in range(2):
                tk.op("pool", lambda e, d=d: e.memset(AT[0:32, d, :], 1.0), w=[gk(1, d)])
                proj_fm(wmix[:, o["gla_a"][0] + 16 * d:o["gla_a"][0] + 16 * (d + 1)], 16,
                        lambda bk, bkey, t0, nt, d=d: tk.op("act", lambda e: e.copy(out=AT[0:16, d, t0:t0 + nt], in_=bk[0:16, :nt]), r=[bkey], w=[gk(1, d)]))
            tk.dma("pool", wa2[0:17, :, :], wa2_in[l].rearrange("d r c -> r d c"), "mwa2", w=["mwa2"])
            for d in range(2):
                tk.op("dve", lambda e: e.memset(Sst[0][:, :], 0.0), w=["mS0"])
                tk.op("pool", lambda e: e.memset(Sbf[0][:, :], 0.0), w=["mSb0"])
                for tt in TT_ORDER[d]:
                    ts_ = slice(tt * 128, (tt + 1) * 128)
                    zb, zkey = bank("glz", (0, 1))
                    B.mm(zb[:, 0:128], AT[0:17, d, ts_], wa2[0:17, d, :], start=True, stop=True, r=[gk(1, d), "mwa2"], w=[zkey], sig=True)
                    tk.op("act", lambda e, zb=zb: e.activation(out=LL[:, :], in_=zb[:, 0:128], func=AF.Exp, scale=-1.0), r=[zkey], w=["mLL"])
                    tk.op("act", lambda e: e.activation(out=LL[:, :], in_=LL[:, :], func=AF.Ln, bias=1.0), r=["mLL"], w=["mLL"])
                    cb, ckey = bank("glc", (2, 3))
                    B.mm(cb[:, 0:128], tri[d], LL[:, :], start=True, stop=True, r=["consts", "mLL"], w=[ckey], sig=True)
                    B.mm(cb[:, 128:256], LL[:, :], tri[d], start=True, stop=True, r=["consts", "mLL"], w=[ckey], sig=True)
                    ei = [nxt("ee", (0, 1, 2, 3, 4, 5)) for _ in range(3)]
                    E1, E2, E3 = EE[ei[0]], EE[ei[1]], EE[ei[2]]
                    e1k, e2k, e3k = f"mEE{ei[0]}", f"mEE{ei[1]}", f"mEE{ei[2]}"
                    tk.op("act", lambda e, cb=cb, E1=E1: e.activation(out=E1[:, :], in_=cb[:, 128:256], func=AF.Exp, scale=-1.0 / 16), r=[ckey], w=[e1k])
                    tk.op("act", lambda e, cb=cb, E2=E2: e.activation(out=E2[:, :], in_=cb[:, 128:256], func=AF.Exp, scale=1.0 / 16), r=[ckey], w=[e2k])
                    tk.op("act", lambda e, cb=cb, E3=E3: e.activation(out=E3[:, :], in_=cb[:, 0:128], func=AF.Exp, scale=1.0 / 16), r=[ckey], w=[e3k])
                    qi = nxt("qk", (0, 1))
                    qk_, qkk = qkt[qi], f"mqk{qi}"
                    ktb, ktbk = ktk[qi], f"mktk{qi}"
                    for h in range(4):
                        tk.op("dve", lambda e, h=h, qk_=qk_, E1=E1: e.scalar_tensor_tensor(out=qk_[:, h, :], in0=QT[:, ts_], scalar=hm[:, h:h + 1], in1=E1[:, :],
                                                                                          op0=ALU.mult, op1=ALU.mult), r=[gk(0, 0), "consts", e1k], w=[(qkk, h)])
                    tk.op("pool", lambda e, qk_=qk_, E1=E1: e.tensor_tensor(out=qk_[:, 4, :], in0=QT[:, ts_], in1=E1[:, :], op=ALU.mult), r=[gk(0, 0), e1k], w=[(qkk, 4)])
                    tk.op("pool", lambda e, qk_=qk_, E2=E2: e.tensor_tensor(out=qk_[:, 5, :], in0=KT[:, ts_], in1=E2[:, :], op=ALU.mult), r=[gk(0, 1), e2k], w=[(qkk, 5)])
                    tk.op("pool", lambda e, ktb=ktb, E3=E3, tt=tt: e.tensor_tensor(out=ktb[:, :], in0=KTOK[:, tt, 0:128], in1=E3[:, :], op=ALU.mult),
                          r=[gk(2, tt // 9), e3k], w=[ktbk])
                    sbk, skey = bank("scS", (4, 5))
                    for h in range(4):
                        B.mm(sbk[:, h * 128:(h + 1) * 128], qk_[:, 5, :], qk_[:, h, :], start=True, stop=True, r=[(qkk, 5), (qkk, h)], w=[skey], sig=(h == 3))
                    pi = nxt("pt", (0, 1, 2))
                    at, atk = PT[pi], f"mPT{pi}"
                    tk.op("dve", lambda e, at=at, sbk=sbk, d=d: e.tensor_tensor(
                        out=at[:, :].rearrange("p (h t) -> p h t", h=4), in0=sbk[:, :].rearrange("p (h t) -> p h t", h=4),
                        in1=tri[d].unsqueeze(1).to_broadcast([128, 4, 128]), op=ALU.mult), r=[skey, "consts"], w=[atk])
                    ob, okey = bank("scO", (6,))
                    B.mm(ob[:, 0:256], qk_[:, 4, :], Sbf[0][:, :], start=True, stop=False, r=[(qkk, 4), "mSb0"], w=[okey], sig=True)
                    for h in range(4):
                        B.mm(ob[:, h * 64:(h + 1) * 64], at[:, h * 128:(h + 1) * 128], VRAW[:, tt, h * 64:(h + 1) * 64], start=False, stop=(h == 3),
                             r=[atk, gk(3, tt // 9)], w=[okey], sig=True)
                    dbk, dkey = bank("scD", (7,))
                    B.mm(dbk[:, 0:256], ktb[:, :], VRAW[:, tt, :], start=True, stop=True, r=[ktbk, gk(3, tt // 9)], w=[dkey], sig=True)
                    dcol = 127 if d == 0 else 0
                    state_update(0, dbk, dkey, 256, E1[:, dcol:dcol + 1], e1k, bd4)
                    if d == 0:
                        tk.op("act", lambda e, ob=ob, tt=tt: e.copy(out=HACC[:, tt, :], in_=ob[:, 0:256]), r=[okey], w=[("mH", tt)])
                    else:
                        tk.op("dve", lambda e, ob=ob, tt=tt: e.tensor_tensor(out=HACC[:, tt, :], in0=HACC[:, tt, :], in1=ob[:, 0:256], op=ALU.add),
                              r=[okey, ("mH", tt)], w=[("mH", tt)])
            gate = tm(1)
            proj_tm(wmix[:, o["gla_g"][0]:o["gla_g"][0] + 256], 256,
                    lambda bk, bkey, tt: tk.op("act", lambda e: e.activation(out=gate[:, tt, :], in_=bk[:, 0:256], func=AF.Silu),
                                               r=[bkey], w=[gk(1, tt // 9)]))
            finish_branch(2, True, False, 1)

        def merge_out(b):
            gcol = MIXOFF["gates"][0]
            gsrc = wmix[:, gcol:gcol + 4096].rearrange("(kc p) (j f c) -> p kc j f c", p=128, j=4, f=8)
            for ti, (t0, nt) in enumerate(TILES512):
                col = b if ti < 4 else 4
                for fc in range(8):
                    wi = nxt("wbs", (0, 1))
                    wbt, wbk = WBs[wi], f"mWB{wi}"
                    wkeys = [(wbk, j) for j in range(4)]
                    if (l, "wbs", fc) not in wcache:
                        wcache[(l, "wbs", fc)] = True
                        for j in range(4):
                            tk.dma("pool", wbt[:, j, :, :], w_branch[l, j, :, fc * 128:(fc + 1) * 128].rearrange("(c p) x -> p c x", p=128), f"{wbk}_{j}", w=[(wbk, j)])
                        tk.op("dve", lambda e, wbt=wbt: e.tensor_tensor(out=wbt[:, :, :, :], in0=wbt[:, :, :, :],
                                                                      in1=geff[:, l, :, :].unsqueeze(3).to_broadcast([128, 4, 2, 128]), op=ALU.mult),
                              r=wkeys + ["geff"], w=wkeys)
                        tk.dma("sp", WBSC[l, fc, :, :, :, :], wbt[:, :, :, :], f"{wbk}_st", r=wkeys, w=[("WBSC", l, fc)])
                    else:
                        tk.dma("sp", wbt[:, :, :, :], WBSC[l, fc, :, :, :, :], f"{wbk}_ld", r=[("WBSC", l, fc)], w=wkeys)
                    acc, acck = tA[2], "mtA2"
                    for j in range(4):
                        gi = nxt("gw", (0, 1))
                        gw_, gwk = gwb[gi], f"mgw{gi}"
                        cached_slab(gw_, gwk, gsrc[:, :, j, fc, :], 128, ("gate", fc, j))
                        gb_, gbk = bank("mgG", (0, 1))
                        for kc in range(8):
                            B.mm(gb_[:, :nt], gw_[:, kc, :], hT[:, kc, t0:t0 + nt], start=(kc == 0), stop=(kc == 7), r=[gwk, hkeys[kc]], w=[gbk])
                        yb_, ybk = bank("mgY", (2, 3))
                        for c in range(2):
                            B.mm(yb_[:, :nt], wbt[:, j, c, :], yT[:, j, c, t0:t0 + nt], start=(c == 0), stop=(c == 1),
                                 r=[(wbk, j)] + [("myT", j, tt) for tt in range(t0 // 128, (t0 + nt) // 128)], w=[ybk])
                        si = nxt("sg", (0, 1))
                        sg, sgk = tA[si], f"mtA{si}"
                        tk.op("act", lambda e, sg=sg, gb_=gb_: e.activation(out=sg[:, :nt], in_=gb_[:, :nt], func=AF.Sigmoid), r=[gbk], w=[sgk])
                        if j == 0:
                            tk.op("dve", lambda e, sg=sg, yb_=yb_: e.tensor_tensor(out=acc[:, :nt], in0=sg[:, :nt], in1=yb_[:, :nt], op=ALU.mult),
                                  r=[sgk, ybk], w=[acck])
                        else:
                            tk.op("dve", lambda e, sg=sg, yb_=yb_: e.tensor_tensor(out=sg[:, :nt], in0=sg[:, :nt], in1=yb_[:, :nt], op=ALU.mult),
                                  r=[sgk, ybk], w=[sgk])
                            if j < 3:
                                tk.op("dve", lambda e, sg=sg: e.tensor_tensor(out=acc[:, :nt], in0=acc[:, :nt], in1=sg[:, :nt], op=ALU.add),
                                      r=[sgk, acck], w=[acck])
                            else:
                                tk.op("dve", lambda e, sg=sg, fc=fc: e.tensor_tensor(out=accT[:, fc, :nt], in0=acc[:, :nt], in1=sg[:, :nt], op=ALU.add),
                                      r=[sgk, acck], w=[("macc", fc)])
                for hh in range(nt // 256):
                    s0, n = t0 + hh * 256, 256
                    tk.dma("sp", xsb[:, :, :], XM[b, :, :, s0:s0 + n], "mxs0", r=xm_keys(b, s0, n), w=["mxs0"])
                    for fo in range(8):
                        wb, wk = load_w(w_outp[l][:, fo * 128:(fo + 1) * 128], 128, ("wo", fo))
                        yb_, ybk = bank("moY", (4, 5))
                        for kc in range(8):
                            B.mm(yb_[:, :n], wb[:, kc, :128], accT[:, kc, hh * 256:hh * 256 + n], start=(kc == 0), stop=(kc == 7),
                                 r=[wk, ("macc", kc)], w=[ybk])
                        tk.op("dve", lambda e, fo=fo, yb_=yb_, col=col: e.scalar_tensor_tensor(
                            out=ztb[:, fo, :], in0=yb_[:, :n], scalar=gsc[:, l, 1, fo, col:col + 1], in1=xsb[:, fo, :], op0=ALU.mult, op1=ALU.add),
                            r=[ybk, "gsc", "mxs0"], w=[("mzt", fo)])
                    layer_norm_store(lt, ztb, "mzt", xob, "mxo0", l, 1, b, s0, n)

        for b in range(BPC):
            if b >= debug.get("mix_b", BPC):
                continue
            fence(True)
            for s0 in range(0, T, 256):
                col = b if s0 < SEQ else 4
                tk.dma("sp", xsb[:, :, :], XM[b, :, :, s0:s0 + 256], "mxs0", r=xm_keys(b, s0, 256), w=["mxs0"])
                for fc in range(8):
                    eng = ("dve", "pool")[fc % 2]
                    tk.op(eng, lambda e, fc=fc, col=col, s0=s0: e.tensor_scalar(
                        out=hT[:, fc, s0:s0 + 256], in0=xsb[:, fc, :], scalar1=sc1[:, l, 1, fc, col:col + 1],
                        scalar2=modp[:, l, 24 + fc, col:col + 1], op0=ALU.mult, op1=ALU.add), r=["mxs0", "sc1", "modp"], w=[("mhT", fc)])
            fence(False)
            want = debug.get("branches", "amgd")
            if "a" in want:
                branch_na(b)
            if "m" in want:
                branch_ml(b)
            if "g" in want:
                branch_gla(b)
            if "d" in want:
                branch_da(b)
            if dbg_y is not None and b == 0:
                for j in [jj for jj, ch in enumerate("amgd") if ch in want]:
                    tk.dma("sp", dbg_y[j, :, :, :], yT[:, j, :, :], "dbgy", r=[("myT", j, tt) for tt in range(NT)], w=[("dbgy", j)])
            if not debug.get("skip_merge"):
                fence(True)
                merge_out(b)
        tk.phase_end()
        es.close()


    prologue()
    nl = debug.get("layers", L)
    for l in range(nl):
        ffn_phase(l, 0, only=debug.get("ffn_tiles"))
        if debug.get("stop") == ("ffn1", l):
            break
        mixer_phase(l)
        if debug.get("stop") == ("mix", l):
            break
        ffn_phase(l, 1)
    epilogue()
    return nc


def _consts():
    c = np.zeros((128, 1024), np.float32)
    p = np.arange(128)
    c[:, 0:128] = np.eye(128)
    c[:, 128:256] = (p[:, None] <= p[None, :])
    c[:, 256:384] = (p[:, None] >= p[None, :])
    c[:, 384:640] = ((p[:, None] // 32) == (np.arange(256)[None, :] // 64))
    c[:, 640:770] = ((p[:, None] // 64) == (np.arange(130)[None, :] // 65))
    c[:, 770:774] = ((p[:, None] // 32) == np.arange(4)[None, :])
    c[:, 774] = np.where((p % 64) < 32, 32 ** -0.5, 0.0)
    c[:, 775] = np.where((p % 64) >= 32, 32 ** -0.5, 0.0)
    return c


def _rope():
    t = np.arange(SEQ)
    inv = (10000.0 ** (-np.arange(8, dtype=np.float32) / 8)).astype(np.float32)
    tab = np.zeros((2, 128, SEQ), np.float32)
    for p in range(128):
        i = p % 32
        pos = (t // GRID_W) if i < 16 else (t % GRID_W)
        ang = pos.astype(np.float32) * inv[i % 8]
        tab[0, p] = np.cos(ang)
        tab[1, p] = np.sin(ang) * (-1.0 if (i % 16) < 8 else 1.0)
    return tab


def _partner_cols(off):
    idx = np.arange(256)
    i = idx % 32
    partner = np.where((i % 16) < 8, idx + 8, idx - 8)
    return off + partner


_CONSTS = _consts()
_ROPE = _rope()


def make_in_maps(inp):
    f = lambda a: np.ascontiguousarray(np.asarray(a, dtype=np.float32))
    x, c, ctx, c_ctx = f(inp["x"]), f(inp["c"]), f(inp["ctx"]), f(inp["c_ctx"])
    b_adaT = f(f(inp["b_ada"]).reshape(DEPTH, 72, 128).transpose(2, 0, 1))
    ln = np.stack([f(inp["ln_g"]), f(inp["ln_b"])], axis=2)
    lnT = f(ln.reshape(DEPTH, 3, 2, 8, 128).transpose(4, 0, 1, 2, 3))
    wmix = f(inp["w_mix_in"])
    sw_cols = np.concatenate([_partner_cols(MIXOFF["da_q"][0]), _partner_cols(MIXOFF["da_k"][0])])
    rpb = f(inp["na_rpb"])
    nab = np.empty((DEPTH, 4, NPAT, 128, 128), np.float32)
    for pi, (valid, dr, dc) in enumerate(NA_PATS):
        g = rpb[:, :, dr, dc]
        nab[:, :, pi] = np.where(valid[None, None], g, np.float32(NEG))
    gains = np.ones((DEPTH, 4, 256), np.float32)
    gains[:, 1], gains[:, 2], gains[:, 3] = f(inp["ml_norm_g"]), f(inp["gla_norm_g"]), f(inp["da_norm_g"])
    gainsT = f(gains.reshape(DEPTH, 4, 2, 128).transpose(3, 0, 1, 2))
    wa2 = f(np.concatenate([f(inp["gla_w_a2"]), f(inp["gla_b_a"])[:, :, None, :]], axis=2))
    shared = {
        "w_ada": f(inp["w_ada"]), "b_adaT": b_adaT, "lnT": lnT,
        "ffn_w_in": f(inp["ffn_w_in"]), "ffn_w_out": f(inp["ffn_w_out"]),
        "w_mix_in": wmix, "w_mix_sw": f(wmix[:, :, sw_cols]), "w_branch": f(inp["w_branch"]), "w_out": f(inp["w_out"]),
        "nab": nab, "consts": _CONSTS, "rope": _ROPE, "ml_gate_b": f(inp["ml_gate_b"]), "gainsT": gainsT,
        "gla_wa2": wa2, "da_lambda": f(f(inp["da_lambda"]).reshape(DEPTH, 128)),
    }
    maps = []
    for r in range(NCORES):
        bs = slice(r * BPC, (r + 1) * BPC)
        cc = np.concatenate([c[bs], c_ctx[None, :]], axis=0)
        m = dict(shared)
        m["x_in"] = f(np.concatenate([x[bs], ctx[bs]], axis=1))
        m["cT"] = f(cc.reshape(5, 8, 128).transpose(2, 1, 0))
        maps.append(m)
    return maps


def kernel(**inputs):
    nc = build_program()
    maps = make_in_maps(inputs)
    res = run_bass_kernel_spmd(nc, maps, core_ids=list(range(NCORES)))
    return np.concatenate([r["out"] for r in res.results], axis=0).astype(np.float32)
```

```python
import math
from contextlib import ExitStack

import numpy as np
import concourse.bass as bass
import concourse.mybir as mybir
from concourse.bass_utils import run_bass_kernel_spmd

F32 = mybir.dt.float32
BF16 = mybir.dt.bfloat16
ALU = mybir.AluOpType
AF = mybir.ActivationFunctionType
AX = mybir.AxisListType

NCORES = 8
D = 1024
DEPTH = 2
BPC = 4
SEQ = 2048
CTX = 256
T = SEQ + CTX
NT = T // 128
DFF = 2816
NJ = DFF // 128
NMOD = 9
ALPHA = (2 * DEPTH) ** 0.25
LN_EPS = 1e-5 / (ALPHA * ALPHA)
TILES512 = [(0, 512), (512, 512), (1024, 512), (1536, 512), (2048, 256)]


class _Rec:
    def __getattr__(self, name):
        def f(*a, **k):
            self.call = (name, a, k)
        return f


class Tracker:
    ENGS = ("pe", "act", "dve", "pool", "sp")

    def __init__(self, nc, es):
        self.nc = nc
        self.es = es
        self.eng_obj = {"pe": nc.tensor, "act": nc.scalar, "dve": nc.vector, "pool": nc.gpsimd, "sp": nc.sync}
        self.sem = {e: es.enter_context(nc.semaphore("sem_" + e)) for e in ("pe", "act", "dve", "pool")}
        self.count = {e: 0 for e in self.sem}
        self.pending = {e: False for e in self.sem}
        self.dsem = {}
        self.dcount = {}
        self.waited = {e: {} for e in self.ENGS}
        self.writers = {}
        self.readers = {}
        self.stream = {e: [] for e in self.ENGS}
        self.nops = 0

    def _dma_sem(self, slot):
        if slot not in self.dsem:
            self.dsem[slot] = self.es.enter_context(self.nc.semaphore("d_" + str(len(self.dsem))))
            self.dcount[slot] = 0
        return self.dsem[slot]

    def _deps(self, r, w):
        deps = []
        for x in r:
            t = self.writers.get(x)
            if t is not None:
                deps.append(t)
        for x in w:
            t = self.writers.get(x)
            if t is not None:
                deps.append(t)
            deps.extend(self.readers.get(x, ()))
        return deps

    def _commit(self, tok, r, w):
        for x in r:
            self.readers.setdefault(x, []).append(tok)
        for x in w:
            self.writers[x] = tok
            self.readers[x] = []

    def _waits(self, eng, deps):
        need = {}
        for kind, key, val in deps:
            if kind == "E" and key == "pe" and eng == "pe":
                continue
            k = (kind, key)
            if val > need.get(k, 0):
                need[k] = val
        out = []
        for k, val in need.items():
            if self.waited[eng].get(k, 0) >= val:
                continue
            self.waited[eng][k] = val
            sem = self.sem[k[1]] if k[0] == "E" else self.dsem[k[1]]
            out.append((sem, val))
        return out

    def op(self, eng, fn, r=(), w=(), sig=True):
        rec = _Rec()
        fn(rec)
        name_, a_, k_ = rec.call
        fn = lambda e, name_=name_, a_=a_, k_=k_: getattr(e, name_)(*a_, **k_)
        deps = self._deps(r, w)
        waits = self._waits(eng, deps)
        if sig:
            self.count[eng] += 1
            tok = ("E", eng, self.count[eng])
        else:
            tok = ("E", eng, self.count[eng] + 1)
        self._commit(tok, r, w)
        sem = self.sem[eng]

        def emit(e, waits=waits, fn=fn, sig=sig, sem=sem):
            for s, v in waits:
                e.wait_ge(s, v)
            ins = fn(e)
            if sig:
                ins.then_inc(sem, 1)
        self.stream[eng].append(emit)
        self.nops += 1
        return tok

    def dma(self, q, out, in_, slot, r=(), w=()):
        deps = self._deps(r, w)
        waits = self._waits(q, deps)
        sem = self._dma_sem(slot)
        self.dcount[slot] += 16
        tok = ("D", slot, self.dcount[slot])
        self._commit(tok, r, w)

        def emit(e, waits=waits, sem=sem, out=out, in_=in_):
            for s, v in waits:
                e.wait_ge(s, v)
            e.dma_start(out=out, in_=in_).then_inc(sem, 16)
        self.stream[q].append(emit)
        return tok

    def finish(self, eng="sp"):
        deps = []
        for e in self.sem:
            if self.count[e]:
                deps.append(("E", e, self.count[e]))
        for s, c in self.dcount.items():
            if c:
                deps.append(("D", s, c))
        waits = self._waits(eng, deps)

        def emit(e, waits=waits):
            for s, v in waits:
                e.wait_ge(s, v)
        self.stream[eng].append(emit)

    def phase_end(self):
        self.finish("sp")
        self.flush()
        self.stream = {e: [] for e in self.ENGS}

    def flush(self):
        with self.nc.Block() as block:
            for name, deco in (("sp", block.sync), ("pe", block.tensor), ("act", block.scalar),
                               ("dve", block.vector), ("pool", block.gpsimd)):
                lst = self.stream[name]

                def body(e, lst=lst):
                    for f in lst:
                        f(e)
                deco(body)


def _mix_cols():
    widths = (("na_q", 256), ("na_k", 256), ("na_v", 256), ("ml_q", 256), ("ml_k", 256), ("ml_v", 256),
              ("ml_o", 256), ("ml_if", 16), ("gla_q", 128), ("gla_k", 128), ("gla_v", 256), ("gla_g", 256),
              ("gla_a", 32), ("da_q", 256), ("da_k", 256), ("da_v", 256), ("gates", 4096))
    off, o = {}, 0
    for n, w in widths:
        off[n] = (o, w)
        o += w
    return off, o


MIXOFF, NCOLS = _mix_cols()


def xm_keys(b, t0, n):
    return [("XM", b, tt) for tt in range(t0 // 128, (t0 + n) // 128)]


class Builder:
    def __init__(self, debug=None):
        self.debug = debug
        self.nc = bass.Bass("TRN2", target_bir_lowering=False)
        self.es = ExitStack()
        self.tk = Tracker(self.nc, self.es)
        self.rr = 0

    def dram_in(self, name, shape, dt=F32):
        return self.nc.dram_tensor(name, list(shape), dt, kind="ExternalInput").ap()

    def dram_out(self, name, shape, dt=F32):
        return self.nc.dram_tensor(name, list(shape), dt, kind="ExternalOutput").ap()

    def dram_scr(self, name, shape, dt):
        return self.nc.dram_tensor(name, list(shape), dt, kind="Internal").ap()

    def sb(self, name, shape, dt):
        return self.es.enter_context(self.nc.sbuf_tensor(name, list(shape), dt))

    def ps(self, name, shape, dt=F32):
        return self.es.enter_context(self.nc.psum_tensor(name, list(shape), dt))

    def mm(self, out, lhsT, rhs, start, stop, r, w, sig=None, sgc=False):
        if sig is None:
            sig = stop
        return self.tk.op("pe", lambda e: e.matmul(out, lhsT=lhsT, rhs=rhs, start=start, stop=stop, skip_group_check=sgc),
                          r=r, w=w, sig=sig)

    def anyeng(self, engs=("dve", "pool", "act")):
        self.rr += 1
        return engs[self.rr % len(engs)]


GRID_W = 64
NEG = -30000.0
LAMBDA_INIT = [0.8 - 0.6 * math.exp(-0.3 * l) for l in range(DEPTH)]


def na_geometry():
    pats, pat_index, per_tile = [], {}, []
    k = np.arange(128)
    q = np.arange(128)
    for i in range(16):
        qr = (2 * i + q // 64)[None, :]
        qc = (q % 64)[None, :]
        r0 = np.clip(qr - 4, 0, 32 - 8)
        c0 = np.clip(qc - 8, 0, GRID_W - 16)
        lst = []
        for j in range(16):
            kr = (2 * j + k // 64)[:, None]
            kc = (k % 64)[:, None]
            valid = (kr >= r0) & (kr < r0 + 8) & (kc >= c0) & (kc < c0 + 16)
            if not valid.any():
                continue
            dr = np.where(valid, kr - qr + 7, 0).astype(np.int64)
            dc = np.where(valid, np.clip(kc - qc + 15, 0, 30), 0).astype(np.int64)
            key = valid.tobytes() + dr.tobytes() + dc.tobytes()
            if key not in pat_index:
                pat_index[key] = len(pats)
                pats.append((valid, dr, dc))
            lst.append((j, pat_index[key]))
        per_tile.append(lst)
    return pats, per_tile


NA_PATS, NA_TILES = na_geometry()
NPAT = len(NA_PATS)


def build_program(debug=None):
    debug = debug or {}
    B = Builder(debug)
    nc, tk = B.nc, B.tk
    L = DEPTH

    x_in = B.dram_in("x_in", [BPC, T, D])
    cT_in = B.dram_in("cT", [128, 8, 5])
    w_ada = B.dram_in("w_ada", [L, D, NMOD * D])
    b_adaT = B.dram_in("b_adaT", [128, L, 72])
    lnT = B.dram_in("lnT", [128, L, 3, 2, 8])
    ffn_w_in = B.dram_in("ffn_w_in", [L, 2, D, 2 * DFF])
    ffn_w_out = B.dram_in("ffn_w_out", [L, 2, DFF, D])
    w_mix = B.dram_in("w_mix_in", [L, D, NCOLS])
    w_sw = B.dram_in("w_mix_sw", [L, D, 512])
    w_branch = B.dram_in("w_branch", [L, 4, 256, D])
    w_outp = B.dram_in("w_out", [L, D, D])
    nab_in = B.dram_in("nab", [L, 4, NPAT, 128, 128])
    consts_in = B.dram_in("consts", [128, 1024])
    rope_in = B.dram_in("rope", [2, 128, SEQ])
    gb_in = B.dram_in("ml_gate_b", [L, 16])
    gains_in = B.dram_in("gainsT", [128, L, 4, 2])
    wa2_in = B.dram_in("gla_wa2", [L, 2, 17, 128])
    dal_in = B.dram_in("da_lambda", [L, 128])
    out_d = B.dram_out("out", [BPC, SEQ, D])
    dbg_y = B.dram_out("dbg_y", [4, 128, 2, T], BF16) if debug.get("dump_y") else None
    XM = B.dram_scr("xm", [BPC, 128, 8, T], F32)
    WIN = B.dram_scr("win_bf", [L, 2, NJ // 2, 128, 8, 2, 256], BF16)
    WOUT = B.dram_scr("wout_bf", [L, 2, 8, 128, NJ, 128], BF16)
    WMS = B.dram_scr("wmix_bf", [L, 128, 128, 8, 256], BF16)
    WBSC = B.dram_scr("wbr_bf", [L, 8, 128, 4, 2, 128], BF16)
    wcache = {}

    consts = B.sb("consts_sb", [128, 1024], F32)
    ident = consts[:, 0:128]
    tri = [consts[:, 128:256], consts[:, 256:384]]
    bd4 = consts[:, 384:640]
    bd2 = consts[:, 640:770]
    hm = consts[:, 770:774]
    mAB = consts[:, 774:776]
    identb = B.sb("identb", [128, 128], BF16)
    ones_f = B.sb("ones_f", [128, 128], F32)
    cT = B.sb("cT_sb", [128, 8, 5], F32)
    modp = B.sb("modp", [128, L, 72, 5], F32)
    sc1 = B.sb("sc1", [128, L, 3, 8, 5], F32)
    gsc = B.sb("gsc", [128, L, 3, 8, 5], F32)
    badaT = B.sb("badaT", [128, L, 72], F32)
    lnp = B.sb("lnp", [128, L, 3, 2, 8], F32)
    geff = B.sb("geff", [128, L, 4, 2], F32)
    nlam = B.sb("nlam", [128, L], F32)
    gbias = B.sb("gbias", [128, L, 16], F32)
    dl = B.sb("dl", [128, 128], F32)
    dls = B.sb("dls", [128, 4], F32)
    pb = [B.ps(f"pb{i}", [128, 512], F32) for i in range(8)]
    rot = {}

    def nxt(name, items):
        i = rot.get(name, 0)
        rot[name] = i + 1
        return items[i % len(items)]

    def bank(name, ids):
        i = nxt(name, ids)
        return pb[i], f"pb{i}"

    def prologue():
        es = ExitStack()
        adaw = [es.enter_context(nc.sbuf_tensor(f"adaw{i}", [128, 1152], F32)) for i in range(2)]
        xtok = [es.enter_context(nc.sbuf_tensor(f"pxtok{i}", [128, D], F32)) for i in range(2)]
        stgs = [es.enter_context(nc.sbuf_tensor(f"pstg{i}", [128, 8, 128], F32)) for i in range(2)]
        tk.dma("sp", consts[:], consts_in[:, :], "c_consts", w=["consts"])
        tk.dma("sp", cT[:], cT_in[:, :, :], "c_ct", w=["cT"])
        tk.dma("sp", badaT[:], b_adaT[:, :, :], "c_bada", w=["badaT"])
        tk.dma("sp", lnp[:], lnT[:, :, :, :, :], "c_ln", w=["lnp"])
        tk.dma("sp", geff[:], gains_in[:, :, :, :], "c_geff", w=["geff"])
        for l in range(L):
            tk.dma("sp", gbias[:, l, :], gb_in[l:l + 1, :].partition_broadcast(128), f"c_gb{l}", w=[("gbias", l)])
        tk.op("pool", lambda e: e.memset(ones_f[:], 1.0), w=["ones_f"])
        tk.op("dve", lambda e: e.tensor_copy(out=identb[:], in_=ident), r=["consts"], w=["identb"])
        for l in range(L):
            tk.op("dve", lambda e, l=l: e.tensor_scalar(out=geff[:, l, 3, :], in0=geff[:, l, 3, :], scalar1=1.0 - LAMBDA_INIT[l],
                                                        scalar2=None, op0=ALU.mult), r=["geff"], w=["geff"])
            tk.dma("sp", dl[:], dal_in[l:l + 1, :].partition_broadcast(128), "c_dl", w=["dl"])
            tk.op("dve", lambda e: e.tensor_tensor(out=dl[:, 0:32], in0=dl[:, 0:32], in1=dl[:, 32:64], op=ALU.mult), r=["dl"], w=["dl"])
            tk.op("dve", lambda e: e.tensor_tensor(out=dl[:, 64:96], in0=dl[:, 64:96], in1=dl[:, 96:128], op=ALU.mult), r=["dl"], w=["dl"])
            tk.op("dve", lambda e: e.tensor_reduce(out=dls[:, 0:1], in_=dl[:, 0:32], axis=AX.X, op=ALU.add), r=["dl"], w=["dls"])
            tk.op("dve", lambda e: e.tensor_reduce(out=dls[:, 1:2], in_=dl[:, 64:96], axis=AX.X, op=ALU.add), r=["dl"], w=["dls"])
            tk.op("act", lambda e: e.activation(out=dls[:, 2:4], in_=dls[:, 0:2], func=AF.Exp), r=["dls"], w=["dls"])
            tk.op("dve", lambda e: e.tensor_tensor(out=dls[:, 0:1], in0=dls[:, 3:4], in1=dls[:, 2:3], op=ALU.subtract), r=["dls"], w=["dls"])
            tk.op("dve", lambda e, l=l: e.tensor_scalar(out=nlam[:, l:l + 1], in0=dls[:, 0:1], scalar1=-LAMBDA_INIT[l], scalar2=None,
                                                        op0=ALU.add), r=["dls"], w=["nlam"])
        tk.op("act", lambda e: e.activation(out=cT[:], in_=cT[:], func=AF.Silu), r=["cT"], w=["cT"])
        ai = 0
        for l in range(L):
            for cg in range(8):
                pbk = pb[cg % 2]
                for kc in range(8):
                    buf, bkey = adaw[ai % 2], f"adaw{ai % 2}"
                    ai += 1
                    tk.dma("sp", buf[:], w_ada[l, kc * 128:(kc + 1) * 128, cg * 1152:(cg + 1) * 1152], bkey, w=[bkey])
                    for m in range(9):
                        B.mm(pbk[:, m * 5:(m + 1) * 5], buf[:, m * 128:(m + 1) * 128], cT[:, kc, :],
                             start=(kc == 0 and m == 0), stop=(kc == 7), r=[bkey, "cT"], w=[f"pb{cg % 2}"], sig=(m == 8), sgc=True)
                tk.op("dve", lambda e, l=l, cg=cg, pbk=pbk: e.tensor_tensor(
                    out=modp[:, l, cg * 9:(cg + 1) * 9, :], in0=pbk[:, 0:45].rearrange("p (m c) -> p m c", c=5),
                    in1=badaT[:, l, cg * 9:(cg + 1) * 9].unsqueeze(2).to_broadcast([128, 9, 5]), op=ALU.add),
                    r=[f"pb{cg % 2}", "badaT"], w=["modp"])
        for l in range(L):
            for s in range(3):
                gmul = (0.5 if s != 1 else 1.0) / ALPHA
                tk.op("dve", lambda e, l=l, s=s: e.tensor_scalar(
                    out=sc1[:, l, s, :, :], in0=modp[:, l, (3 * s + 1) * 8:(3 * s + 2) * 8, :], scalar1=1.0, scalar2=None,
                    op0=ALU.add), r=["modp"], w=["sc1"])
                tk.op("dve", lambda e, l=l, s=s, gmul=gmul: e.tensor_scalar(
                    out=gsc[:, l, s, :, :], in0=modp[:, l, (3 * s + 2) * 8:(3 * s + 3) * 8, :], scalar1=gmul, scalar2=None,
                    op0=ALU.mult), r=["modp"], w=["gsc"])
        li = 0
        for b in range(BPC):
            for tt in range(NT):
                buf, bkey = xtok[li % 2], f"pxtok{li % 2}"
                stg, skey = stgs[li % 2], f"pstg{li % 2}"
                li += 1
                tk.dma("sp", buf[:], x_in[b, tt * 128:(tt + 1) * 128, :], bkey, w=[bkey])
                for half in range(2):
                    pbk, pkey = pb[2 + half], f"pb{2 + half}"
                    for q in range(4):
                        fc = half * 4 + q
                        tk.op("pe", lambda e, pbk=pbk, q=q, buf=buf, fc=fc: e.transpose(
                            pbk[:, q * 128:(q + 1) * 128], buf[:, fc * 128:(fc + 1) * 128], ident),
                            r=[bkey, "consts"], w=[pkey], sig=(q == 3))
                    if half:
                        tk.op("dve", lambda e, pbk=pbk, stg=stg: e.tensor_copy(
                            out=stg[:, 4:8, :], in_=pbk[:, :].rearrange("p (q t) -> p q t", q=4)), r=[pkey], w=[(skey, 1)])
                    else:
                        tk.op("act", lambda e, pbk=pbk, stg=stg: e.copy(
                            out=stg[:, 0:4, :], in_=pbk[:, :].rearrange("p (q t) -> p q t", q=4)), r=[pkey], w=[(skey, 0)])
                tk.dma("sp", XM[b, :, :, tt * 128:(tt + 1) * 128], stg[:, :, :], skey, r=[(skey, 0), (skey, 1)], w=[("XM", b, tt)])
        tk.phase_end()
        es.close()

    def ln_tiles(es, n, pfx):
        d = {"n": n, "cnt": 0}
        uid = nxt("uid", list(range(100)))
        d["sq"] = [es.enter_context(nc.sbuf_tensor(f"{pfx}sq{i}_u{uid}", [128, n], F32)) for i in range(2)]
        d["tmp"] = [es.enter_context(nc.sbuf_tensor(f"{pfx}tmp{i}_u{uid}", [128, n], F32)) for i in range(2)]
        for nm in ("mean", "rstd", "pre"):
            d[nm] = es.enter_context(nc.sbuf_tensor(f"{pfx}{nm}_u{uid}", [128, n], F32))
        d["pfx"] = pfx
        return d

    def layer_norm_store(lt, zt, ztk, xob, xok, l, s, b, t0, n):
        psS, psQ = pb[6], pb[7]
        pfx = lt["pfx"]
        mean, rstd, pre = lt["mean"], lt["rstd"], lt["pre"]
        mk, rk, pk = pfx + "mean", pfx + "rstd", pfx + "pre"
        for fc in range(8):
            i = lt["cnt"] % 2
            lt["cnt"] += 1
            sqb, sqk = lt["sq"][i], f"{pfx}sq{i}"
            tk.op("act", lambda e, sqb=sqb, fc=fc: e.activation(out=sqb[:, :n], in_=zt[:, fc, :n], func=AF.Square),
                  r=[(ztk, fc)], w=[sqk])
            B.mm(psS[:, :n], ones_f[:], zt[:, fc, :n], start=(fc == 0), stop=(fc == 7), r=["ones_f", (ztk, fc)], w=["pb6"], sig=True)
            B.mm(psQ[:, :n], ones_f[:], sqb[:, :n], start=(fc == 0), stop=(fc == 7), r=["ones_f", sqk], w=["pb7"], sig=True)
        tk.op("act", lambda e: e.activation(out=mean[:, :n], in_=psS[:, :n], func=AF.Copy, scale=1.0 / D), r=["pb6"], w=[mk])
        tk.op("dve", lambda e: e.tensor_tensor(out=pre[:, :n], in0=mean[:, :n], in1=mean[:, :n], op=ALU.mult), r=[mk], w=[pk])
        tk.op("dve", lambda e: e.scalar_tensor_tensor(out=rstd[:, :n], in0=psQ[:, :n], scalar=1.0 / D, in1=pre[:, :n],
                                                      op0=ALU.mult, op1=ALU.subtract), r=["pb7", pk], w=[rk])
        tk.op("dve", lambda e: e.tensor_scalar(out=rstd[:, :n], in0=rstd[:, :n], scalar1=0.0, scalar2=LN_EPS,
                                               op0=ALU.max, op1=ALU.add), r=[rk], w=[rk])
        tk.op("act", lambda e: e.activation(out=rstd[:, :n], in_=rstd[:, :n], func=AF.Sqrt), r=[rk], w=[rk])
        tk.op("dve", lambda e: e.reciprocal(out=rstd[:, :n], in_=rstd[:, :n]), r=[rk], w=[rk])
        tk.op("dve", lambda e: e.tensor_tensor(out=pre[:, :n], in0=mean[:, :n], in1=rstd[:, :n], op=ALU.mult), r=[mk, rk], w=[pk])
        for fc in range(8):
            i = lt["cnt"] % 2
            lt["cnt"] += 1
            tb, tkey = lt["tmp"][i], f"{pfx}tmp{i}"
            tk.op("pool", lambda e, tb=tb, fc=fc: e.tensor_tensor(out=tb[:, :n], in0=zt[:, fc, :n], in1=rstd[:, :n], op=ALU.mult),
                  r=[(ztk, fc), rk], w=[tkey])
            tk.op("dve", lambda e, tb=tb: e.tensor_tensor(out=tb[:, :n], in0=tb[:, :n], in1=pre[:, :n], op=ALU.subtract),
                  r=[tkey, pk], w=[tkey])
            tk.op("act", lambda e, tb=tb, fc=fc: e.activation(
                out=xob[:, fc, :n], in_=tb[:, :n], func=AF.Identity, scale=lnp[:, l, s, 0, fc:fc + 1], bias=lnp[:, l, s, 1, fc:fc + 1]),
                r=[tkey, "lnp"], w=[xok])
        tk.dma("sp", XM[b, :, :, t0:t0 + n], xob[:, :, :n], xok, r=[xok], w=xm_keys(b, t0, n))

    def ffn_phase(l, k, only=None):
        es = ExitStack()
        uid = nxt("uid", list(range(100)))
        A = lambda name, shape, dt: es.enter_context(nc.sbuf_tensor(f"{name}_u{uid}", list(shape), dt))
        xs = [A(f"fxs{i}", [128, 8, 512], F32) for i in range(2)]
        hT = A("fhT", [128, 8, 1024], BF16)
        gT = A("fgT", [128, NJ, 1024], BF16)
        sa = [A(f"fsa{i}", [128, 512], F32) for i in range(2)]
        zt = A("fzt", [128, 8, 1024], F32)
        winb = [A(f"fwin{i}", [128, 8, 2, 256], BF16) for i in range(2)]
        woutb = [A(f"fwout{i}", [128, NJ, 128], BF16) for i in range(2)]
        lt = ln_tiles(es, 512, "f")
        s = 0 if k == 0 else 2
        cnt = {"win": 0, "wout": 0, "sa": 0, "pa": 0, "py": 0}
        first = True
        for b in range(BPC):
            for blk in ((0, 1), (2, 3), (4,)):
                tiles = []
                for ti in blk:
                    if only is not None and b * 5 + ti >= only:
                        continue
                    if l == L - 1 and k == 1 and ti == 4:
                        continue
                    tiles.append((len(tiles), ti) + TILES512[ti])
                if not tiles:
                    continue
                for idx, ti, t0, n in tiles:
                    col = b if ti < 4 else 4
                    xb, xk = xs[idx], f"fxs{idx}"
                    tk.dma("sp", xb[:, :, :n], XM[b, :, :, t0:t0 + n], xk, r=xm_keys(b, t0, n), w=[xk])
                    for fc in range(8):
                        eng = ("dve", "pool")[fc % 2]
                        tk.op(eng, lambda e, fc=fc, xb=xb, col=col, n=n, idx=idx: e.tensor_scalar(
                            out=hT[:, fc, idx * 512:idx * 512 + n], in0=xb[:, fc, :n], scalar1=sc1[:, l, s, fc, col:col + 1],
                            scalar2=modp[:, l, (3 * s) * 8 + fc, col:col + 1], op0=ALU.mult, op1=ALU.add),
                            r=[xk, "sc1", "modp"], w=[("fhT", fc, idx)])
                for j in range(NJ):
                    if j % 2 == 0:
                        wb, wk = winb[cnt["win"] % 2], f"fwin{cnt['win'] % 2}"
                        cnt["win"] += 1
                        if first:
                            for ab in range(2):
                                c0 = ab * DFF + j * 128
                                tk.dma("pool", wb[:, :, ab, :], ffn_w_in[l, k, :, c0:c0 + 256].rearrange("(kc p) c -> p kc c", p=128),
                                       f"{wk}_{ab}", w=[(wk, ab)])
                            tk.dma("sp", WIN[l, k, j // 2, :, :, :, :], wb[:, :, :, :], f"{wk}_h", r=[(wk, 0), (wk, 1)], w=[("WIN", l, k, j // 2)])
                        else:
                            tk.dma("sp", wb[:, :, :, :], WIN[l, k, j // 2, :, :, :, :], f"{wk}_h", r=[("WIN", l, k, j // 2)], w=[(wk, 0), (wk, 1)])
                    jj = j % 2
                    for idx, ti, t0, n in tiles:
                        hs = slice(idx * 512, idx * 512 + n)
                        pa, pak = pb[cnt["pa"] % 2], f"pb{cnt['pa'] % 2}"
                        pbb, pbk = pb[2 + cnt["pa"] % 2], f"pb{2 + cnt['pa'] % 2}"
                        cnt["pa"] += 1
                        for kc in range(8):
                            B.mm(pa[:, :n], wb[:, kc, 0, jj * 128:(jj + 1) * 128], hT[:, kc, hs], start=(kc == 0), stop=(kc == 7),
                                 r=[(wk, 0), ("fhT", kc, idx)], w=[pak])
                        for kc in range(8):
                            B.mm(pbb[:, :n], wb[:, kc, 1, jj * 128:(jj + 1) * 128], hT[:, kc, hs], start=(kc == 0), stop=(kc == 7),
                                 r=[(wk, 1), ("fhT", kc, idx)], w=[pbk])
                        sab, sak = sa[cnt["sa"] % 2], f"fsa{cnt['sa'] % 2}"
                        cnt["sa"] += 1
                        tk.op("act", lambda e, sab=sab, pa=pa, n=n: e.activation(out=sab[:, :n], in_=pa[:, :n], func=AF.Silu), r=[pak], w=[sak])
                        tk.op("dve", lambda e, sab=sab, pbb=pbb, j=j, n=n, hs=hs: e.tensor_tensor(out=gT[:, j, hs], in0=sab[:, :n], in1=pbb[:, :n], op=ALU.mult),
                              r=[sak, pbk], w=[("fgT", j, idx)])
                for fc in range(8):
                    wb, wk = woutb[cnt["wout"] % 2], f"fwout{cnt['wout'] % 2}"
                    cnt["wout"] += 1
                    if first:
                        for hh in range(2):
                            tk.dma("pool", wb[:, hh * 11:(hh + 1) * 11, :],
                                   ffn_w_out[l, k, hh * 1408:(hh + 1) * 1408, fc * 128:(fc + 1) * 128].rearrange("(j p) c -> p j c", p=128),
                                   f"{wk}_{hh}", w=[(wk, hh)])
                        tk.dma("sp", WOUT[l, k, fc, :, :, :], wb[:, :, :], f"{wk}_h", r=[(wk, 0), (wk, 1)], w=[("WOUT", l, k, fc)])
                    else:
                        tk.dma("sp", wb[:, :, :], WOUT[l, k, fc, :, :, :], f"{wk}_h", r=[("WOUT", l, k, fc)], w=[(wk, 0), (wk, 1)])
                    for idx, ti, t0, n in tiles:
                        col = b if ti < 4 else 4
                        hs = slice(idx * 512, idx * 512 + n)
                        py, pyk = pb[4 + cnt["py"] % 2], f"pb{4 + cnt['py'] % 2}"
                        cnt["py"] += 1
                        for j in range(NJ):
                            B.mm(py[:, :n], wb[:, j, :], gT[:, j, hs], start=(j == 0), stop=(j == NJ - 1), r=[(wk, j // 11), ("fgT", j, idx)], w=[pyk])
                        tk.op("dve", lambda e, fc=fc, py=py, col=col, n=n, hs=hs, idx=idx: e.scalar_tensor_tensor(
                            out=zt[:, fc, hs], in0=py[:, :n], scalar=gsc[:, l, s, fc, col:col + 1], in1=xs[idx][:, fc, :n],
                            op0=ALU.mult, op1=ALU.add), r=[pyk, "gsc", f"fxs{idx}"], w=[(f"fzt{idx}", fc)])
                first = False
                for idx, ti, t0, n in tiles:
                    layer_norm_store(lt, zt[:, :, idx * 512:(idx + 1) * 512], f"fzt{idx}", xs[idx], f"fxs{idx}", l, s, b, t0, n)
        tk.phase_end()
        es.close()

    def epilogue():
        es = ExitStack()
        xs = [es.enter_context(nc.sbuf_tensor(f"exs{i}", [128, 8, 128], F32)) for i in range(2)]
        xtok = [es.enter_context(nc.sbuf_tensor(f"extok{i}", [128, D], F32)) for i in range(2)]
        li = 0
        for b in range(BPC):
            for tt in range(SEQ // 128):
                xb, xk = xs[li % 2], f"exs{li % 2}"
                ob, ok = xtok[li % 2], f"extok{li % 2}"
                li += 1
                tk.dma("sp", xb[:, :, :], XM[b, :, :, tt * 128:(tt + 1) * 128], xk, r=[("XM", b, tt)], w=[xk])
                for half in range(2):
                    pbk, pkey = pb[2 + half], f"pb{2 + half}"
                    for q in range(4):
                        fc = half * 4 + q
                        tk.op("pe", lambda e, pbk=pbk, q=q, xb=xb, fc=fc: e.transpose(
                            pbk[:, q * 128:(q + 1) * 128], xb[:, fc, :], ident), r=[xk, "consts"], w=[pkey], sig=(q == 3))
                    if half:
                        tk.op("dve", lambda e, pbk=pbk, ob=ob: e.tensor_copy(out=ob[:, 512:1024], in_=pbk[:, :]), r=[pkey], w=[(ok, 1)])
                    else:
                        tk.op("act", lambda e, pbk=pbk, ob=ob: e.copy(out=ob[:, 0:512], in_=pbk[:, :]), r=[pkey], w=[(ok, 0)])
                tk.dma("sp", out_d[b, tt * 128:(tt + 1) * 128, :], ob[:], ok, r=[(ok, 0), (ok, 1)], w=[("OUT", b, tt)])
        tk.phase_end()
        es.close()

    def mixer_phase(l):
        es = ExitStack()
        uid = nxt("uid", list(range(100)))
        A = lambda name, shape, dt: es.enter_context(nc.sbuf_tensor(f"{name}_u{uid}", list(shape), dt))
        hT = A("mhT", [128, 8, T], BF16)
        G = [A(f"mG{i}", [128, 2 * T], BF16) for i in range(4)]
        VA = A("mVA", [128, NT, 260], BF16)
        HACC = A("mH", [128, NT, 256], F32)
        yT = A("myT", [128, 4, 2, T], BF16)
        wsl = [A(f"mws{i}", [128, 8, 256], BF16) for i in range(2)]
        PT = [A(f"mPT{i}", [128, 512], BF16) for i in range(3)]
        tA = [A(f"mtA{i}", [128, 512], F32) for i in range(3)]
        ropeb = [A(f"mrope{i}", [128, 512], F32) for i in range(2)]
        nabb = A("mnab", [128, 5, 4, 128], BF16)
        sm = A("msm", [128, 64], F32)
        GIF = A("mGIF", [128, NT, 16], F32)
        LLc = A("mLLc", [128, NT, 8], F32)
        CS = A("mCS", [128, NT, 16], F32)
        UU = A("mUU", [128, NT, 8], F32)
        RR = A("mRR", [128, NT, 8], F32)
        DECP = A("mDECP", [128, NT, 2, 2], F32)
        Sst = [A(f"mS{i}", [128, 256], F32) for i in range(4)]
        Sbf = [A(f"mSb{i}", [128, 256], BF16) for i in range(4)]
        EE = [A(f"mEE{i}", [128, 128], F32) for i in range(6)]
        LLs = [A(f"mLL{i}", [128, 128], F32) for i in range(2)]
        qkt = [A(f"mqk{i}", [128, 6, 128], BF16) for i in range(2)]
        ktk = [A(f"mktk{i}", [128, 128], BF16) for i in range(2)]
        wa2 = A("mwa2", [32, 2, 128], BF16)
        WBs = [A(f"mWB{i}", [128, 4, 2, 128], BF16) for i in range(2)]
        accT = A("macc", [128, 8, 512], BF16)
        g32 = [G[i][:].bitcast(F32) for i in range(4)]
        xsb = g32[0][:, 0:2048].rearrange("p (c t) -> p c t", c=8)
        ztb = g32[1][:, 0:2048].rearrange("p (c t) -> p c t", c=8)
        xob = g32[2][:, 0:2048].rearrange("p (c t) -> p c t", c=8)
        lt = {"n": 256, "cnt": 0, "pfx": "m", "sq": [g32[3][:, 0:256], g32[3][:, 256:512]], "tmp": [g32[3][:, 512:768], g32[3][:, 768:1024]],
              "mean": g32[3][:, 1024:1280], "rstd": g32[3][:, 1280:1536], "pre": g32[3][:, 1536:1792]}
        hbf = HACC[:, :, :].bitcast(BF16).rearrange("p t f -> p (t f)")
        vbf = VA[:, :, :].rearrange("p t f -> p (t f)")
        gwb = [hbf[:, i * 1024:(i + 1) * 1024].rearrange("p (k c) -> p k c", k=8) for i in range(5)]
        wob = [hbf[:, (5 + i) * 1024:(6 + i) * 1024].rearrange("p (k c) -> p k c", k=8) for i in range(4)] + \
              [vbf[:, i * 1024:(i + 1) * 1024].rearrange("p (k c) -> p k c", k=8) for i in range(4)]
        GKEYS = [(f"mG{i}", h) for i in range(4) for h in range(2)] + [("mH", tt) for tt in range(NT)] + [("mVA", tt) for tt in range(NT)]
        TAILKEYS = ["mxs0", "mxo0", "msq0", "msq1", "mtmp0", "mtmp1", "mmean", "mrstd", "mpre"] + [("mzt", fc) for fc in range(8)] + \
                   [f"mgw{i}" for i in range(5)] + [f"mwo{i}" for i in range(8)]

        def fence(to_tail):
            r_, w_ = (GKEYS, TAILKEYS) if to_tail else (TAILKEYS, GKEYS)
            tk.op("dve", lambda e: e.memset(sm[:, 60:61], 0.0), r=r_, w=w_)

        fm = lambda i: G[i][:].rearrange("p (c t) -> p c t", c=2)
        tm = lambda i: G[i][:].rearrange("p (t f) -> p t f", f=256)
        gk = lambda i, half: (f"mG{i}", half)
        wmix = w_mix[l]
        wcnt = {"n": 0}
        hkeys = [("mhT", fc) for fc in range(8)]
        TT_ORDER = {0: [16, 17] + list(range(16)), 1: [17, 16] + list(range(15, -1, -1))}

        def cached_slab(dst, dkey, src, n, ckey, slot=None):
            if (l, ckey) not in wcache:
                wcache[(l, ckey)] = len([1 for k_ in wcache if k_[0] == l])
                idx = wcache[(l, ckey)]
                tk.dma("pool", dst[:, :, :n], src, dkey, w=[dkey])
                tk.dma("sp", WMS[l, idx, :, :, :n], dst[:, :, :n], (slot or dkey) + "_h", r=[dkey], w=[("WMS", l, idx)])
            else:
                idx = wcache[(l, ckey)]
                tk.dma("sp", dst[:, :, :n], WMS[l, idx, :, :, :n], (slot or dkey) + "_h", r=[("WMS", l, idx)], w=[dkey])

        def load_w(src, n, ckey):
            i = wcnt["n"] % 2
            wcnt["n"] += 1
            wb, wk = wsl[i], f"mws{i}"
            cached_slab(wb, wk, src.rearrange("(kc p) c -> p kc c", p=128), n, ckey)
            return wb, wk

        def proj_fm(src, n, evac, banks=(0, 1)):
            wb, wk = load_w(src, n, ("c", src.offset, n))
            for (t0, nt) in TILES512:
                bk, bkey = bank("pj", banks)
                for kc in range(8):
                    B.mm(bk[:n, :nt], wb[:, kc, :n], hT[:, kc, t0:t0 + nt], start=(kc == 0), stop=(kc == 7), r=[wk, hkeys[kc]], w=[bkey])
                evac(bk, bkey, t0, nt)

        def proj_tm(src, n, evac, banks=(0, 1)):
            wb, wk = load_w(src, n, ("c", src.offset, n))
            for tt in range(NT):
                bk, bkey = bank("pj", banks)
                for kc in range(8):
                    B.mm(bk[:, :n], hT[:, kc, tt * 128:(tt + 1) * 128], wb[:, kc, :n], start=(kc == 0), stop=(kc == 7), r=[wk, hkeys[kc]], w=[bkey])
                evac(bk, bkey, tt)

        def half_of(t0):
            return 0 if t0 < T // 2 else 1

        def finish_branch(j, norm, center, gate_i):
            for tt in range(NT):
                hk = ("mH", tt)
                src = HACC[:, tt, :]
                if norm:
                    x3 = HACC[:, tt, :].rearrange("p (h e) -> p h e", h=4)
                    t0_, t1_ = tA[0], tA[1]
                    if center:
                        tk.op("dve", lambda e, x3=x3: e.tensor_reduce(out=sm[:, 0:4], in_=x3, axis=AX.X, op=ALU.add), r=[hk], w=["msm"])
                        tk.op("dve", lambda e: e.tensor_scalar(out=sm[:, 0:4], in0=sm[:, 0:4], scalar1=-1.0 / 64, scalar2=None, op0=ALU.mult),
                              r=["msm"], w=["msm"])
                        tk.op("dve", lambda e, x3=x3, t0_=t0_: e.tensor_tensor(
                            out=t0_[:, 0:256].rearrange("p (h e) -> p h e", h=4), in0=x3,
                            in1=sm[:, 0:4].unsqueeze(2).to_broadcast([128, 4, 64]), op=ALU.add), r=[hk, "msm"], w=["mtA0"])
                        xc, xck = t0_[:, 0:256], "mtA0"
                    else:
                        xc, xck = src, hk
                    tk.op("act", lambda e, xc=xc, t1_=t1_: e.activation(out=t1_[:, 0:256], in_=xc, func=AF.Square), r=[xck], w=["mtA1"])
                    tk.op("dve", lambda e, t1_=t1_: e.tensor_reduce(out=sm[:, 4:8], in_=t1_[:, 0:256].rearrange("p (h e) -> p h e", h=4),
                                                                    axis=AX.X, op=ALU.add), r=["mtA1"], w=["msm"])
                    tk.op("dve", lambda e: e.tensor_scalar(out=sm[:, 4:8], in0=sm[:, 4:8], scalar1=1.0 / 64, scalar2=1e-6, op0=ALU.mult, op1=ALU.add),
                          r=["msm"], w=["msm"])
                    tk.op("act", lambda e: e.activation(out=sm[:, 4:8], in_=sm[:, 4:8], func=AF.Sqrt), r=["msm"], w=["msm"])
                    tk.op("dve", lambda e: e.reciprocal(out=sm[:, 4:8], in_=sm[:, 4:8]), r=["msm"], w=["msm"])
                    tk.op("dve", lambda e, xc=xc, t1_=t1_: e.tensor_tensor(
                        out=t1_[:, 0:256].rearrange("p (h e) -> p h e", h=4), in0=xc.rearrange("p (h e) -> p h e", h=4),
                        in1=sm[:, 4:8].unsqueeze(2).to_broadcast([128, 4, 64]), op=ALU.mult), r=[xck, "msm"], w=["mtA1"])
                    cur, curk = t1_[:, 0:256], "mtA1"
                    if gate_i is not None:
                        tk.op("dve", lambda e, t1_=t1_, tt=tt: e.tensor_tensor(out=t1_[:, 0:256], in0=t1_[:, 0:256], in1=tm(gate_i)[:, tt, :], op=ALU.mult),
                              r=["mtA1", gk(gate_i, tt // 9)], w=["mtA1"])
                else:
                    cur, curk = src, hk
                bk, bkey = bank("fin", (2, 3))
                for c in range(2):
                    tk.op("pe", lambda e, bk=bk, c=c, cur=cur: e.transpose(bk[:, c * 128:(c + 1) * 128], cur[:, c * 128:(c + 1) * 128], ident),
                          r=[curk, "consts"], w=[bkey], sig=(c == 1))
                tk.op("act", lambda e, bk=bk, tt=tt: e.copy(out=yT[:, j, :, tt * 128:(tt + 1) * 128], in_=bk[:, 0:256].rearrange("p (c t) -> p c t", c=2)),
                      r=[bkey], w=[("myT", j, tt)])

        def branch_na(b):
            QT, KT = fm(0), fm(1)
            for c in range(2):
                proj_fm(wmix[:, c * 128:(c + 1) * 128], 128,
                        lambda bk, bkey, t0, nt, c=c: tk.op("act", lambda e: e.activation(out=QT[:, c, t0:t0 + nt], in_=bk[:, :nt], func=AF.Copy, scale=0.125),
                                                            r=[bkey], w=[gk(0, c)]))
                proj_fm(wmix[:, 256 + c * 128:256 + (c + 1) * 128], 128,
                        lambda bk, bkey, t0, nt, c=c: tk.op("dve", lambda e: e.tensor_copy(out=KT[:, c, t0:t0 + nt], in_=bk[:, :nt]),
                                                            r=[bkey], w=[gk(1, c)]))
            tk.op("pool", lambda e: e.memset(VA[:, :, :].rearrange("p t (h e) -> p (t h) e", h=4)[:, :, 64:65], 1.0), w=[("mVA", tt) for tt in range(NT)])
            proj_tm(wmix[:, 512:768], 256,
                    lambda bk, bkey, tt: tk.op("dve", lambda e: e.tensor_copy(
                        out=VA[:, tt, :].rearrange("p (h e) -> p h e", h=4)[:, :, 0:64], in_=bk[:, 0:256].rearrange("p (h e) -> p h e", h=4)),
                        r=[bkey], w=[("mVA", tt)]))
            for i in range(NT):
                local = NA_TILES[i] if i < 16 else []
                for slot, (j, pid) in enumerate(local):
                    tk.dma("pool", nabb[:, slot, :, :], nab_in[l, :, pid, :, :].rearrange("h k q -> k h q"), f"mnab{slot}", w=[("mnab", slot)])
                ktiles = [(j, slot) for slot, (j, pid) in enumerate(local)] + [(16, None), (17, None)]
                ob, okey = bank("naO", (4, 5))
                for h in range(4):
                    c, base = h // 2, (h % 2) * 64
                    for idx, (j, slot) in enumerate(ktiles):
                        sbk, skey = bank("naS", (0, 1, 2, 3))
                        B.mm(sbk[:, :128], KT[base:base + 64, c, j * 128:(j + 1) * 128], QT[base:base + 64, c, i * 128:(i + 1) * 128],
                             start=True, stop=(slot is None), r=[gk(1, c), gk(0, c)], w=[skey], sig=True)
                        if slot is not None:
                            B.mm(sbk[:, :128], identb[:], nabb[:, slot, h, :], start=False, stop=True, r=["identb", ("mnab", slot)], w=[skey], sig=True)
                        pi = nxt("pt", (0, 1, 2))
                        pt, ptk = PT[pi], f"mPT{pi}"
                        tk.op("act", lambda e, pt=pt, sbk=sbk: e.activation(out=pt[:, :128], in_=sbk[:, :128], func=AF.Exp), r=[skey], w=[ptk])
                        B.mm(ob[:, h * 65:(h + 1) * 65], pt[:, :128], VA[:, j, h * 65:(h + 1) * 65], start=(idx == 0), stop=(idx == len(ktiles) - 1),
                             r=[ptk, ("mVA", j)], w=[okey], sig=True)
                o3 = ob[:, 0:260].rearrange("p (h e) -> p h e", h=4)
                tk.op("dve", lambda e, o3=o3: e.reciprocal(out=sm[:, 8:12], in_=o3[:, :, 64]), r=[okey], w=["msm"])
                tk.op("dve", lambda e, o3=o3, i=i: e.tensor_tensor(out=HACC[:, i, :].rearrange("p (h e) -> p h e", h=4), in0=o3[:, :, 0:64],
                                                                  in1=sm[:, 8:12].unsqueeze(2).to_broadcast([128, 4, 64]), op=ALU.mult),
                      r=[okey, "msm"], w=[("mH", i)])
            finish_branch(0, False, False, None)

        def branch_da(b):
            qA, qB, kTc = fm(0)[:, 0, :], fm(0)[:, 1, :], fm(1)[:, 0, :]
            tk.op("pool", lambda e: e.memset(VA[:, :, :].rearrange("p t (h e) -> p (t h) e", h=4)[:, :, 64:65], 1.0), w=[("mVA", tt) for tt in range(NT)])
            proj_tm(wmix[:, MIXOFF["da_v"][0]:MIXOFF["da_v"][0] + 256], 256,
                    lambda bk, bkey, tt: tk.op("dve", lambda e: e.tensor_copy(
                        out=VA[:, tt, :].rearrange("p (h e) -> p h e", h=4)[:, :, 0:64], in_=bk[:, 0:256].rearrange("p (h e) -> p h e", h=4)),
                        r=[bkey], w=[("mVA", tt)]))
            for c in range(2):
                for which in ("q", "k"):
                    col0 = MIXOFF["da_" + which][0] + c * 128
                    sw0 = (0 if which == "q" else 256) + c * 128
                    w1, w1k = load_w(wmix[:, col0:col0 + 128], 128, ("da1", which, c))
                    w2, w2k = load_w(w_sw[l][:, sw0:sw0 + 128], 128, ("da2", which, c))
                    for (t0, nt) in TILES512:
                        b1, b1k = bank("pj", (0, 1))
                        for kc in range(8):
                            B.mm(b1[:, :nt], w1[:, kc, :128], hT[:, kc, t0:t0 + nt], start=(kc == 0), stop=(kc == 7), r=[w1k, hkeys[kc]], w=[b1k])
                        r_, rk_ = tA[2], "mtA2"
                        if t0 < SEQ:
                            b2, b2k = bank("pj2", (2, 3))
                            for kc in range(8):
                                B.mm(b2[:, :nt], w2[:, kc, :128], hT[:, kc, t0:t0 + nt], start=(kc == 0), stop=(kc == 7), r=[w2k, hkeys[kc]], w=[b2k])
                            tk.dma("sp", ropeb[0][:, :nt], rope_in[0, :, t0:t0 + nt], "mrope0", w=["mrope0"])
                            tk.dma("sp", ropeb[1][:, :nt], rope_in[1, :, t0:t0 + nt], "mrope1", w=["mrope1"])
                            tk.op("dve", lambda e, b1=b1, nt=nt: e.tensor_tensor(out=tA[0][:, :nt], in0=b1[:, :nt], in1=ropeb[0][:, :nt], op=ALU.mult),
                                  r=[b1k, "mrope0"], w=["mtA0"])
                            tk.op("dve", lambda e, b2=b2, nt=nt: e.tensor_tensor(out=tA[1][:, :nt], in0=b2[:, :nt], in1=ropeb[1][:, :nt], op=ALU.mult),
                                  r=[b2k, "mrope1"], w=["mtA1"])
                            tk.op("pool", lambda e, nt=nt: e.tensor_tensor(out=r_[:, :nt], in0=tA[0][:, :nt], in1=tA[1][:, :nt], op=ALU.add),
                                  r=["mtA0", "mtA1"], w=[rk_])
                        else:
                            tk.op("pool" if False else "dve", lambda e, b1=b1, nt=nt: e.tensor_copy(out=r_[:, :nt], in_=b1[:, :nt]), r=[b1k], w=[rk_])
                        if which == "q":
                            tk.op("act", lambda e, t0=t0, nt=nt: e.activation(out=qA[:, t0:t0 + nt], in_=r_[:, :nt], func=AF.Copy, scale=mAB[:, 0:1]),
                                  r=[rk_, "consts"], w=[gk(0, 0)])
                            tk.op("act", lambda e, t0=t0, nt=nt: e.activation(out=qB[:, t0:t0 + nt], in_=r_[:, :nt], func=AF.Copy, scale=mAB[:, 1:2]),
                                  r=[rk_, "consts"], w=[gk(0, 1)])
                        else:
                            tk.op("act", lambda e, t0=t0, nt=nt: e.copy(out=kTc[:, t0:t0 + nt], in_=r_[:, :nt]), r=[rk_], w=[gk(1, 0)])
                for qt, (q0, qn) in enumerate(TILES512):
                    keyt = list(range(NT)) if qt < 4 else [16, 17]
                    nsub = qn // 128
                    for hl in range(2):
                        h, base = 2 * c + hl, hl * 64
                        obs = [bank("daO", (4, 5, 6, 7)) for _ in range(2)]
                        steps = [(kt, m) for kt in keyt for m in range(2)]

                        def smm(step):
                            kt, m = steps[step]
                            sbk, skey = bank("daS", (0, 1, 2, 3))
                            qsrc, qk_ = (qA, gk(0, 0)) if m == 0 else (qB, gk(0, 1))
                            B.mm(sbk[:, :qn], kTc[base:base + 64, kt * 128:(kt + 1) * 128], qsrc[base:base + 64, q0:q0 + qn],
                                 start=True, stop=True, r=[gk(1, 0), qk_], w=[skey], sig=True)
                            return sbk, skey
                        nxt_s = smm(0)
                        for step, (kt, m) in enumerate(steps):
                            sbk, skey = nxt_s
                            if step + 1 < len(steps):
                                nxt_s = smm(step + 1)
                            pi = nxt("pt", (0, 1, 2))
                            pt, ptk = PT[pi], f"mPT{pi}"
                            tk.op("act", lambda e, pt=pt, sbk=sbk: e.activation(out=pt[:, :qn], in_=sbk[:, :qn], func=AF.Exp), r=[skey], w=[ptk])
                            ob, okey = obs[m]
                            first, last = (kt == keyt[0]), (kt == keyt[-1])
                            for sub in range(nsub):
                                B.mm(ob[:, sub * 65:(sub + 1) * 65], pt[:, sub * 128:(sub + 1) * 128], VA[:, kt, h * 65:(h + 1) * 65],
                                     start=(first and sub == 0), stop=last, r=[ptk, ("mVA", kt)], w=[okey],
                                     sig=(sub == nsub - 1), sgc=True)
                        o1 = obs[0][0][:, 0:nsub * 65].rearrange("p (s e) -> p s e", e=65)
                        o2 = obs[1][0][:, 0:nsub * 65].rearrange("p (s e) -> p s e", e=65)
                        k1, k2 = obs[0][1], obs[1][1]
                        tk.op("dve", lambda e, o1=o1: e.reciprocal(out=sm[:, 16:16 + nsub], in_=o1[:, :, 64]), r=[k1], w=["msm"])
                        tk.op("dve", lambda e, o2=o2: e.reciprocal(out=sm[:, 20:20 + nsub], in_=o2[:, :, 64]), r=[k2], w=["msm"])
                        tk.op("dve", lambda e: e.tensor_scalar(out=sm[:, 20:20 + nsub], in0=sm[:, 20:20 + nsub], scalar1=nlam[:, l:l + 1], scalar2=None,
                                                               op0=ALU.mult), r=["msm", "nlam"], w=["msm"])
                        t0_, t1_ = tA[0], tA[1]
                        tk.op("dve", lambda e, o1=o1: e.tensor_tensor(out=t0_[:, 0:nsub * 64].rearrange("p (s e) -> p s e", e=64), in0=o1[:, :, 0:64],
                                                                      in1=sm[:, 16:16 + nsub].unsqueeze(2).to_broadcast([128, nsub, 64]), op=ALU.mult),
                              r=[k1, "msm"], w=["mtA0"])
                        tk.op("dve", lambda e, o2=o2: e.tensor_tensor(out=t1_[:, 0:nsub * 64].rearrange("p (s e) -> p s e", e=64), in0=o2[:, :, 0:64],
                                                                      in1=sm[:, 20:20 + nsub].unsqueeze(2).to_broadcast([128, nsub, 64]), op=ALU.mult),
                              r=[k2, "msm"], w=["mtA1"])
                        tt0 = q0 // 128
                        tk.op("dve", lambda e, h=h, tt0=tt0: e.tensor_tensor(
                            out=HACC[:, tt0:tt0 + nsub, h * 64:(h + 1) * 64], in0=t0_[:, 0:nsub * 64].rearrange("p (s e) -> p s e", e=64),
                            in1=t1_[:, 0:nsub * 64].rearrange("p (s e) -> p s e", e=64), op=ALU.add),
                            r=["mtA0", "mtA1"], w=[("mH", tt0 + s_) for s_ in range(nsub)])
            finish_branch(3, True, False, None)

        def state_update(ci, ds_bank, ds_key, ncol, dec_ap, dec_key, bdm):
            S, Sb = Sst[ci], Sbf[ci]
            sk, sbk_ = f"mS{ci}", f"mSb{ci}"
            t2, t2k = tA[2], "mtA2"
            tk.op("dve", lambda e: e.scalar_tensor_tensor(out=t2[:, :ncol], in0=ds_bank[:, :ncol], scalar=dec_ap, in1=bdm, op0=ALU.mult, op1=ALU.mult),
                  r=[ds_key, dec_key, "consts"], w=[t2k])
            tk.op("dve", lambda e: e.scalar_tensor_tensor(out=S[:, :ncol], in0=S[:, :ncol], scalar=dec_ap, in1=t2[:, :ncol], op0=ALU.mult, op1=ALU.add),
                  r=[sk, dec_key, t2k], w=[sk])
            tk.op("pool", lambda e: e.tensor_copy(out=Sb[:, :], in_=S[:, :]), r=[sk], w=[sbk_])

        def branch_ml(b):
            QT, KT, KTOK, VRAW = fm(0), fm(1), tm(2), tm(3)
            o = MIXOFF
            for c in range(2):
                proj_fm(wmix[:, o["ml_q"][0] + c * 128:o["ml_q"][0] + (c + 1) * 128], 128,
                        lambda bk, bkey, t0, nt, c=c: tk.op("act", lambda e: e.copy(out=QT[:, c, t0:t0 + nt], in_=bk[:, :nt]), r=[bkey], w=[gk(0, c)]))
                proj_fm(wmix[:, o["ml_k"][0] + c * 128:o["ml_k"][0] + (c + 1) * 128], 128,
                        lambda bk, bkey, t0, nt, c=c: tk.op("dve", lambda e: e.tensor_copy(out=KT[:, c, t0:t0 + nt], in_=bk[:, :nt]), r=[bkey], w=[gk(1, c)]))
            proj_tm(wmix[:, o["ml_k"][0]:o["ml_k"][0] + 256], 256,
                    lambda bk, bkey, tt: tk.op("act", lambda e: e.copy(out=KTOK[:, tt, :], in_=bk[:, 0:256]), r=[bkey], w=[gk(2, tt // 9)]))
            proj_tm(wmix[:, o["ml_v"][0]:o["ml_v"][0] + 256], 256,
                    lambda bk, bkey, tt: tk.op("dve", lambda e: e.tensor_copy(out=VRAW[:, tt, :], in_=bk[:, 0:256]), r=[bkey], w=[gk(3, tt // 9)]))
            proj_tm(wmix[:, o["ml_if"][0]:o["ml_if"][0] + 16], 16,
                    lambda bk, bkey, tt: tk.op("dve", lambda e: e.tensor_tensor(out=GIF[:, tt, :], in0=bk[:, 0:16], in1=gbias[:, l, :], op=ALU.add),
                                               r=[bkey, ("gbias", l)], w=["mGIF"]))
            if debug.get("ml_stop") == 1:
                return
            for d in range(2):
                tk.op("act", lambda e, d=d: e.activation(out=LLc[:, :, d * 4:(d + 1) * 4], in_=GIF[:, :, 4 + 8 * d:8 + 8 * d], func=AF.Exp, scale=-1.0),
                      r=["mGIF"], w=["mLLc"])
            tk.op("act", lambda e: e.activation(out=LLc[:, :, :], in_=LLc[:, :, :], func=AF.Ln, bias=1.0), r=["mLLc"], w=["mLLc"])
            for tt in range(NT):
                bk, bkey = bank("pj", (0, 1))
                B.mm(bk[:, 0:4], tri[0], LLc[:, tt, 0:4], start=True, stop=True, r=["consts", "mLLc"], w=[bkey], sig=True)
                B.mm(bk[:, 4:8], tri[1], LLc[:, tt, 4:8], start=True, stop=True, r=["consts", "mLLc"], w=[bkey], sig=True)
                B.mm(bk[:, 8:16], ones_f[:], LLc[:, tt, 0:8], start=True, stop=True, r=["ones_f", "mLLc"], w=[bkey], sig=True)
                tk.op("act", lambda e, bk=bk, tt=tt: e.copy(out=CS[:, tt, :], in_=bk[:, 0:16]), r=[bkey], w=["mCS"])
            if debug.get("ml_stop") == 2:
                return
            for d in range(2):
                tk.op("dve", lambda e, d=d: e.tensor_tensor(out=UU[:, :, d * 4:(d + 1) * 4], in0=GIF[:, :, 8 * d:8 * d + 4], in1=CS[:, :, d * 4:(d + 1) * 4], op=ALU.add),
                      r=["mGIF", "mCS"], w=["mUU"])
            tk.op("act", lambda e: e.activation(out=UU[:, :, :], in_=UU[:, :, :], func=AF.Exp), r=["mUU"], w=["mUU"])
            tk.op("act", lambda e: e.activation(out=RR[:, :, :], in_=CS[:, :, 0:8], func=AF.Exp, scale=-1.0, bias=math.log(0.125)), r=["mCS"], w=["mRR"])
            tk.op("act", lambda e: e.activation(out=CS[:, :, 8:16], in_=CS[:, :, 8:16], func=AF.Exp, scale=-1.0), r=["mCS"], w=["mCS"])
            for d in range(2):
                for c in range(2):
                    for hl in range(2):
                        tk.op("dve", lambda e, d=d, c=c, hl=hl: e.tensor_copy(
                            out=DECP[hl * 64:(hl + 1) * 64, :, d, c:c + 1], in_=CS[hl * 64:(hl + 1) * 64, :, 8 + d * 4 + 2 * c + hl:8 + d * 4 + 2 * c + hl + 1]),
                            r=["mCS"], w=["mDECP"])
            if debug.get("ml_stop") == 3:
                return
            tk.op("pool", lambda e: e.memset(VA[:, :, :].rearrange("p t (h e) -> p (t h) e", h=4)[:, :, 64:65], 1.0), w=[("mVA", tt) for tt in range(NT)])
            for tt in range(NT):
                tk.op("dve", lambda e, tt=tt: e.tensor_copy(out=VA[:, tt, :].rearrange("p (h e) -> p h e", h=4)[:, :, 0:64],
                                                           in_=VRAW[:, tt, :].rearrange("p (h e) -> p h e", h=4)),
                      r=[gk(3, tt // 9)], w=[("mVA", tt)])
            for si in range(4):
                tk.op("dve", lambda e, si=si: e.memset(Sst[si][:, :], 0.0), w=[f"mS{si}"])
                tk.op("pool", lambda e, si=si: e.memset(Sbf[si][:, :], 0.0), w=[f"mSb{si}"])
            for step in range(NT):
                for d in range(2):
                    tt = TT_ORDER[d][step]
                    first_visit = (step, d) <= (TT_ORDER[1 - d].index(tt), 1 - d)
                    ts_ = slice(tt * 128, (tt + 1) * 128)
                    for c in range(2):
                        si = 2 * d + c
                        pi = nxt("pt", (0, 1, 2))
                        at, atk = PT[pi], f"mPT{pi}"
                        ki = nxt("qk", (0, 1))
                        ktb, ktbk = ktk[ki], f"mktk{ki}"
                        for hl in range(2):
                            ucol = d * 4 + 2 * c + hl
                            sbk, skey = bank("scS", (0, 1, 6, 7))
                            B.mm(sbk[:, 0:128], KT[hl * 64:(hl + 1) * 64, c, ts_], QT[hl * 64:(hl + 1) * 64, c, ts_],
                                 start=True, stop=True, r=[gk(1, c), gk(0, c)], w=[skey], sig=True)
                            tk.op("dve", lambda e, at=at, sbk=sbk, d=d, hl=hl, ucol=ucol, tt=tt: e.scalar_tensor_tensor(
                                out=at[:, hl * 128:(hl + 1) * 128], in0=sbk[:, 0:128], scalar=UU[:, tt, ucol:ucol + 1], in1=tri[d],
                                op0=ALU.mult, op1=ALU.mult), r=[skey, "consts", "mUU"], w=[atk])
                            tk.op("act", lambda e, ktb=ktb, hl=hl, ucol=ucol, tt=tt, c=c: e.activation(
                                out=ktb[:, hl * 64:(hl + 1) * 64], in_=KTOK[:, tt, c * 128 + hl * 64:c * 128 + (hl + 1) * 64], func=AF.Copy,
                                scale=UU[:, tt, ucol:ucol + 1]), r=[gk(2, tt // 9), "mUU"], w=[ktbk])
                        ob, okey = bank("scO", (2, 3))
                        B.mm(ob[:, 0:130], QT[:, c, ts_], Sbf[si][:, 0:130], start=True, stop=False, r=[gk(0, c), f"mSb{si}"], w=[okey], sig=True)
                        for hl in range(2):
                            h = 2 * c + hl
                            B.mm(ob[:, hl * 65:(hl + 1) * 65], at[:, hl * 128:(hl + 1) * 128], VA[:, tt, h * 65:(h + 1) * 65], start=False, stop=(hl == 1),
                                 r=[atk, ("mVA", tt)], w=[okey], sig=True)
                        dbk, dkey = bank("scD", (4, 5))
                        B.mm(dbk[:, 0:130], ktb[:, :], VA[:, tt, 2 * c * 65:(2 * c + 2) * 65], start=True, stop=True,
                             r=[ktbk, ("mVA", tt)], w=[dkey], sig=True)
                        state_update(si, dbk, dkey, 130, DECP[:, tt, d, c:c + 1], "mDECP", bd2)
                        o3 = ob[:, 0:130].rearrange("p (h e) -> p h e", h=2)
                        rr = RR[:, tt, d * 4 + 2 * c:d * 4 + 2 * c + 2]
                        tk.op("dve", lambda e, o3=o3, rr=rr: e.tensor_tensor(out=sm[:, 24:26], in0=o3[:, :, 64], in1=rr, op=ALU.mult), r=[okey, "mRR"], w=["msm"])
                        tk.op("dve", lambda e: e.scalar_tensor_tensor(out=sm[:, 26:28], in0=sm[:, 24:26], scalar=-1.0, in1=sm[:, 24:26], op0=ALU.mult, op1=ALU.max),
                              r=["msm"], w=["msm"])
                        tk.op("dve", lambda e: e.tensor_scalar(out=sm[:, 26:28], in0=sm[:, 26:28], scalar1=1.0, scalar2=None, op0=ALU.max), r=["msm"], w=["msm"])
                        tk.op("dve", lambda e: e.reciprocal(out=sm[:, 26:28], in_=sm[:, 26:28]), r=["msm"], w=["msm"])
                        tk.op("dve", lambda e, rr=rr: e.tensor_tensor(out=sm[:, 28:30], in0=sm[:, 26:28], in1=rr, op=ALU.mult), r=["msm", "mRR"], w=["msm"])
                        hv = HACC[:, tt, 2 * c * 64:(2 * c + 2) * 64].rearrange("p (h e) -> p h e", h=2)
                        if first_visit:
                            tk.op("dve", lambda e, o3=o3, hv=hv: e.tensor_tensor(out=hv, in0=o3[:, :, 0:64],
                                                                               in1=sm[:, 28:30].unsqueeze(2).to_broadcast([128, 2, 64]), op=ALU.mult),
                                  r=[okey, "msm"], w=[("mH", tt)])
                        else:
                            t0_ = tA[0]
                            tk.op("dve", lambda e, o3=o3, t0_=t0_: e.tensor_tensor(out=t0_[:, 0:128].rearrange("p (h e) -> p h e", h=2), in0=o3[:, :, 0:64],
                                                                                   in1=sm[:, 28:30].unsqueeze(2).to_broadcast([128, 2, 64]), op=ALU.mult),
                                  r=[okey, "msm"], w=["mtA0"])
                            tk.op("dve", lambda e, hv=hv, t0_=t0_: e.tensor_tensor(out=hv, in0=hv, in1=t0_[:, 0:128].rearrange("p (h e) -> p h e", h=2), op=ALU.add),
                                  r=["mtA0", ("mH", tt)], w=[("mH", tt)])
            if debug.get("ml_stop") == 5:
                return
            gate = tm(0)
            proj_tm(wmix[:, o["ml_o"][0]:o["ml_o"][0] + 256], 256,
                    lambda bk, bkey, tt: tk.op("act", lambda e: e.activation(out=gate[:, tt, :], in_=bk[:, 0:256], func=AF.Sigmoid),
                                               r=[bkey], w=[gk(0, tt // 9)]))
            finish_branch(1, True, True, 0)

        def branch_gla(b):
            QT, KT, KTOK, VRAW = fm(0)[:, 0, :], fm(0)[:, 1, :], tm(2), tm(3)
            AT = fm(1)
            o = MIXOFF
            proj_fm(wmix[:, o["gla_q"][0]:o["gla_q"][0] + 128], 128,
                    lambda bk, bkey, t0, nt: tk.op("act", lambda e: e.activation(out=QT[:, t0:t0 + nt], in_=bk[:, :nt], func=AF.Copy, scale=32 ** -0.5),
                                                   r=[bkey], w=[gk(0, 0)]))
            proj_fm(wmix[:, o["gla_k"][0]:o["gla_k"][0] + 128], 128,
                    lambda bk, bkey, t0, nt: tk.op("dve", lambda e: e.tensor_copy(out=KT[:, t0:t0 + nt], in_=bk[:, :nt]), r=[bkey], w=[gk(0, 1)]))
            proj_tm(wmix[:, o["gla_k"][0]:o["gla_k"][0] + 128], 128,
                    lambda bk, bkey, tt: tk.op("act", lambda e: e.copy(out=KTOK[:, tt, 0:128], in_=bk[:, 0:128]), r=[bkey], w=[gk(2, tt // 9)]))
            proj_tm(wmix[:, o["gla_v"][0]:o["gla_v"][0] + 256], 256,
                    lambda bk, bkey, tt: tk.op("dve", lambda e: e.tensor_copy(out=VRAW[:, tt, :], in_=bk[:, 0:256]), r=[bkey], w=[gk(3, tt // 9)]))
            for d in range(2):
                tk.op("pool", lambda e, d=d: e.memset(AT[0:32, d, :], 1.0), w=[gk(1, d)])
                proj_fm(wmix[:, o["gla_a"][0] + 16 * d:o["gla_a"][0] + 16 * (d + 1)], 16,
                        lambda bk, bkey, t0, nt, d=d: tk.op("act", lambda e: e.copy(out=AT[0:16, d, t0:t0 + nt], in_=bk[0:16, :nt]), r=[bkey], w=[gk(1, d)]))
            tk.dma("pool", wa2[0:17, :, :], wa2_in[l].rearrange("d r c -> r d c"), "mwa2", w=["mwa2"])
            for d in range(2):
                tk.op("dve", lambda e, d=d: e.memset(Sst[d][:, :], 0.0), w=[f"mS{d}"])
                tk.op("pool", lambda e, d=d: e.memset(Sbf[d][:, :], 0.0), w=[f"mSb{d}"])
            for step in range(NT):
                for d in range(2):
                    tt = TT_ORDER[d][step]
                    first_visit = (step, d) <= (TT_ORDER[1 - d].index(tt), 1 - d)
                    lli = nxt("ll", (0, 1))
                    LL, llk = LLs[lli], f"mLL{lli}"
                    ts_ = slice(tt * 128, (tt + 1) * 128)
                    zb, zkey = bank("glz", (0, 1))
                    B.mm(zb[:, 0:128], AT[0:17, d, ts_], wa2[0:17, d, :], start=True, stop=True, r=[gk(1, d), "mwa2"], w=[zkey], sig=True)
                    tk.op("act", lambda e, zb=zb: e.activation(out=LL[:, :], in_=zb[:, 0:128], func=AF.Exp, scale=-1.0), r=[zkey], w=[llk])
                    tk.op("act", lambda e: e.activation(out=LL[:, :], in_=LL[:, :], func=AF.Ln, bias=1.0), r=[llk], w=[llk])
                    cb, ckey = bank("glc", (2, 3))
                    B.mm(cb[:, 0:128], tri[d], LL[:, :], start=True, stop=True, r=["consts", llk], w=[ckey], sig=True)
                    B.mm(cb[:, 128:256], LL[:, :], tri[d], start=True, stop=True, r=["consts", llk], w=[ckey], sig=True)
                    ei = [nxt("ee", (0, 1, 2, 3, 4, 5)) for _ in range(3)]
                    E1, E2, E3 = EE[ei[0]], EE[ei[1]], EE[ei[2]]
                    e1k, e2k, e3k = f"mEE{ei[0]}", f"mEE{ei[1]}", f"mEE{ei[2]}"
                    tk.op("act", lambda e, cb=cb, E1=E1: e.activation(out=E1[:, :], in_=cb[:, 128:256], func=AF.Exp, scale=-1.0 / 16), r=[ckey], w=[e1k])
                    tk.op("act", lambda e, cb=cb, E2=E2: e.activation(out=E2[:, :], in_=cb[:, 128:256], func=AF.Exp, scale=1.0 / 16), r=[ckey], w=[e2k])
                    tk.op("act", lambda e, cb=cb, E3=E3: e.activation(out=E3[:, :], in_=cb[:, 0:128], func=AF.Exp, scale=1.0 / 16), r=[ckey], w=[e3k])
                    qi = nxt("qk", (0, 1))
                    qk_, qkk = qkt[qi], f"mqk{qi}"
                    ktb, ktbk = ktk[qi], f"mktk{qi}"
                    for h in range(4):
                        tk.op("dve", lambda e, h=h, qk_=qk_, E1=E1: e.scalar_tensor_tensor(out=qk_[:, h, :], in0=QT[:, ts_], scalar=hm[:, h:h + 1], in1=E1[:, :],
                                                                                          op0=ALU.mult, op1=ALU.mult), r=[gk(0, 0), "consts", e1k], w=[(qkk, h)])
                    tk.op("pool", lambda e, qk_=qk_, E1=E1: e.tensor_tensor(out=qk_[:, 4, :], in0=QT[:, ts_], in1=E1[:, :], op=ALU.mult), r=[gk(0, 0), e1k], w=[(qkk, 4)])
                    tk.op("pool", lambda e, qk_=qk_, E2=E2: e.tensor_tensor(out=qk_[:, 5, :], in0=KT[:, ts_], in1=E2[:, :], op=ALU.mult), r=[gk(0, 1), e2k], w=[(qkk, 5)])
                    tk.op("pool", lambda e, ktb=ktb, E3=E3, tt=tt: e.tensor_tensor(out=ktb[:, :], in0=KTOK[:, tt, 0:128], in1=E3[:, :], op=ALU.mult),
                          r=[gk(2, tt // 9), e3k], w=[ktbk])
                    sbk, skey = bank("scS", (4, 5))
                    for h in range(4):
                        B.mm(sbk[:, h * 128:(h + 1) * 128], qk_[:, 5, :], qk_[:, h, :], start=True, stop=True, r=[(qkk, 5), (qkk, h)], w=[skey], sig=(h == 3))
                    pi = nxt("pt", (0, 1, 2))
                    at, atk = PT[pi], f"mPT{pi}"
                    tk.op("dve", lambda e, at=at, sbk=sbk, d=d: e.tensor_tensor(
                        out=at[:, :].rearrange("p (h t) -> p h t", h=4), in0=sbk[:, :].rearrange("p (h t) -> p h t", h=4),
                        in1=tri[d].unsqueeze(1).to_broadcast([128, 4, 128]), op=ALU.mult), r=[skey, "consts"], w=[atk])
                    ob, okey = bank("scO", (6,))
                    B.mm(ob[:, 0:256], qk_[:, 4, :], Sbf[d][:, :], start=True, stop=False, r=[(qkk, 4), f"mSb{d}"], w=[okey], sig=True)
                    for h in range(4):
                        B.mm(ob[:, h * 64:(h + 1) * 64], at[:, h * 128:(h + 1) * 128], VRAW[:, tt, h * 64:(h + 1) * 64], start=False, stop=(h == 3),
                             r=[atk, gk(3, tt // 9)], w=[okey], sig=True)
                    dbk, dkey = bank("scD", (7,))
                    B.mm(dbk[:, 0:256], ktb[:, :], VRAW[:, tt, :], start=True, stop=True, r=[ktbk, gk(3, tt // 9)], w=[dkey], sig=True)
                    dcol = 127 if d == 0 else 0
                    state_update(d, dbk, dkey, 256, E1[:, dcol:dcol + 1], e1k, bd4)
                    if first_visit:
                        tk.op("act", lambda e, ob=ob, tt=tt: e.copy(out=HACC[:, tt, :], in_=ob[:, 0:256]), r=[okey], w=[("mH", tt)])
                    else:
                        tk.op("dve", lambda e, ob=ob, tt=tt: e.tensor_tensor(out=HACC[:, tt, :], in0=HACC[:, tt, :], in1=ob[:, 0:256], op=ALU.add),
                              r=[okey, ("mH", tt)], w=[("mH", tt)])
            gate = tm(1)
            proj_tm(wmix[:, o["gla_g"][0]:o["gla_g"][0] + 256], 256,
                    lambda bk, bkey, tt: tk.op("act", lambda e: e.activation(out=gate[:, tt, :], in_=bk[:, 0:256], func=AF.Silu),
                                               r=[bkey], w=[gk(1, tt // 9)]))
            finish_branch(2, True, False, 1)

        def merge_out(b):
            for fo in range(8):
                cached_slab(wob[fo], f"mwo{fo}", w_outp[l][:, fo * 128:(fo + 1) * 128].rearrange("(kc p) c -> p kc c", p=128), 128, ("wo", fo))
            gcol = MIXOFF["gates"][0]
            gsrc = wmix[:, gcol:gcol + 4096].rearrange("(kc p) (j f c) -> p kc j f c", p=128, j=4, f=8)
            for ti, (t0, nt) in enumerate(TILES512):
                col = b if ti < 4 else 4
                for fc in range(8):
                    wi = nxt("wbs", (0, 1))
                    wbt, wbk = WBs[wi], f"mWB{wi}"
                    wkeys = [(wbk, j) for j in range(4)]
                    if (l, "wbs", fc) not in wcache:
                        wcache[(l, "wbs", fc)] = True
                        for j in range(4):
                            tk.dma("pool", wbt[:, j, :, :], w_branch[l, j, :, fc * 128:(fc + 1) * 128].rearrange("(c p) x -> p c x", p=128), f"{wbk}_{j}", w=[(wbk, j)])
                        tk.op("dve", lambda e, wbt=wbt: e.tensor_tensor(out=wbt[:, :, :, :], in0=wbt[:, :, :, :],
                                                                      in1=geff[:, l, :, :].unsqueeze(3).to_broadcast([128, 4, 2, 128]), op=ALU.mult),
                              r=wkeys + ["geff"], w=wkeys)
                        tk.dma("sp", WBSC[l, fc, :, :, :, :], wbt[:, :, :, :], f"{wbk}_h", r=wkeys, w=[("WBSC", l, fc)])
                    else:
                        tk.dma("sp", wbt[:, :, :, :], WBSC[l, fc, :, :, :, :], f"{wbk}_h", r=[("WBSC", l, fc)], w=wkeys)
                    acc, acck = tA[2], "mtA2"
                    for j in range(4):
                        gi = nxt("gw", (0, 1, 2, 3, 4))
                        gw_, gwk = gwb[gi], f"mgw{gi}"
                        cached_slab(gw_, gwk, gsrc[:, :, j, fc, :], 128, ("gate", fc, j))
                        gb_, gbk = bank("mgG", (0, 1))
                        for kc in range(8):
                            B.mm(gb_[:, :nt], gw_[:, kc, :], hT[:, kc, t0:t0 + nt], start=(kc == 0), stop=(kc == 7), r=[gwk, hkeys[kc]], w=[gbk])
                        yb_, ybk = bank("mgY", (2, 3))
                        for c in range(2):
                            B.mm(yb_[:, :nt], wbt[:, j, c, :], yT[:, j, c, t0:t0 + nt], start=(c == 0), stop=(c == 1),
                                 r=[(wbk, j)] + [("myT", j, tt) for tt in range(t0 // 128, (t0 + nt) // 128)], w=[ybk])
                        si = nxt("sg", (0, 1))
                        sg, sgk = tA[si], f"mtA{si}"
                        tk.op("act", lambda e, sg=sg, gb_=gb_: e.activation(out=sg[:, :nt], in_=gb_[:, :nt], func=AF.Sigmoid), r=[gbk], w=[sgk])
                        if j == 0:
                            tk.op("dve", lambda e, sg=sg, yb_=yb_: e.tensor_tensor(out=acc[:, :nt], in0=sg[:, :nt], in1=yb_[:, :nt], op=ALU.mult),
                                  r=[sgk, ybk], w=[acck])
                        else:
                            tk.op("dve", lambda e, sg=sg, yb_=yb_: e.tensor_tensor(out=sg[:, :nt], in0=sg[:, :nt], in1=yb_[:, :nt], op=ALU.mult),
                                  r=[sgk, ybk], w=[sgk])
                            if j < 3:
                                tk.op("dve", lambda e, sg=sg: e.tensor_tensor(out=acc[:, :nt], in0=acc[:, :nt], in1=sg[:, :nt], op=ALU.add),
                                      r=[sgk, acck], w=[acck])
                            else:
                                tk.op("dve", lambda e, sg=sg, fc=fc: e.tensor_tensor(out=accT[:, fc, :nt], in0=acc[:, :nt], in1=sg[:, :nt], op=ALU.add),
                                      r=[sgk, acck], w=[("macc", fc)])
                for hh in range(nt // 256):
                    s0, n = t0 + hh * 256, 256
                    tk.dma("sp", xsb[:, :, :], XM[b, :, :, s0:s0 + n], "mxs0", r=xm_keys(b, s0, n), w=["mxs0"])
                    for fo in range(8):
                        wb, wk = wob[fo], f"mwo{fo}"
                        yb_, ybk = bank("moY", (4, 5))
                        for kc in range(8):
                            B.mm(yb_[:, :n], wb[:, kc, :128], accT[:, kc, hh * 256:hh * 256 + n], start=(kc == 0), stop=(kc == 7),
                                 r=[f"mwo{i}" for i in range(8)] + [("macc", kc)], w=[ybk])
                        tk.op("dve", lambda e, fo=fo, yb_=yb_, col=col: e.scalar_tensor_tensor(
                            out=ztb[:, fo, :], in0=yb_[:, :n], scalar=gsc[:, l, 1, fo, col:col + 1], in1=xsb[:, fo, :], op0=ALU.mult, op1=ALU.add),
                            r=[ybk, "gsc", "mxs0"], w=[("mzt", fo)])
                    layer_norm_store(lt, ztb, "mzt", xob, "mxo0", l, 1, b, s0, n)

        for b in range(BPC):
            if b >= debug.get("mix_b", BPC):
                continue
            fence(True)
            for s0 in range(0, T, 256):
                col = b if s0 < SEQ else 4
                tk.dma("sp", xsb[:, :, :], XM[b, :, :, s0:s0 + 256], "mxs0", r=xm_keys(b, s0, 256), w=["mxs0"])
                for fc in range(8):
                    eng = ("dve", "pool")[fc % 2]
                    tk.op(eng, lambda e, fc=fc, col=col, s0=s0: e.tensor_scalar(
                        out=hT[:, fc, s0:s0 + 256], in0=xsb[:, fc, :], scalar1=sc1[:, l, 1, fc, col:col + 1],
                        scalar2=modp[:, l, 24 + fc, col:col + 1], op0=ALU.mult, op1=ALU.add), r=["mxs0", "sc1", "modp"], w=[("mhT", fc)])
            fence(False)
            want = debug.get("branches", "amgd")
            if "a" in want:
                branch_na(b)
            if "m" in want:
                branch_ml(b)
            if "g" in want:
                branch_gla(b)
            if "d" in want:
                branch_da(b)
            if dbg_y is not None and b == 0:
                for j in [jj for jj, ch in enumerate("amgd") if ch in want]:
                    tk.dma("sp", dbg_y[j, :, :, :], yT[:, j, :, :], "dbgy", r=[("myT", j, tt) for tt in range(NT)], w=[("dbgy", j)])
            if not debug.get("skip_merge"):
                fence(True)
                merge_out(b)
        tk.phase_end()
        es.close()


    prologue()
    nl = debug.get("layers", L)
    for l in range(nl):
        ffn_phase(l, 0, only=debug.get("ffn_tiles"))
        if debug.get("stop") == ("ffn1", l):
            break
        mixer_phase(l)
        if debug.get("stop") == ("mix", l):
            break
        ffn_phase(l, 1)
    epilogue()
    return nc


def _consts():
    c = np.zeros((128, 1024), np.float32)
    p = np.arange(128)
    c[:, 0:128] = np.eye(128)
    c[:, 128:256] = (p[:, None] <= p[None, :])
    c[:, 256:384] = (p[:, None] >= p[None, :])
    c[:, 384:640] = ((p[:, None] // 32) == (np.arange(256)[None, :] // 64))
    c[:, 640:770] = ((p[:, None] // 64) == (np.arange(130)[None, :] // 65))
    c[:, 770:774] = ((p[:, None] // 32) == np.arange(4)[None, :])
    c[:, 774] = np.where((p % 64) < 32, 32 ** -0.5, 0.0)
    c[:, 775] = np.where((p % 64) >= 32, 32 ** -0.5, 0.0)
    return c


def _rope():
    t = np.arange(SEQ)
    inv = (10000.0 ** (-np.arange(8, dtype=np.float32) / 8)).astype(np.float32)
    tab = np.zeros((2, 128, SEQ), np.float32)
    for p in range(128):
        i = p % 32
        pos = (t // GRID_W) if i < 16 else (t % GRID_W)
        ang = pos.astype(np.float32) * inv[i % 8]
        tab[0, p] = np.cos(ang)
        tab[1, p] = np.sin(ang) * (-1.0 if (i % 16) < 8 else 1.0)
    return tab


def _partner_cols(off):
    idx = np.arange(256)
    i = idx % 32
    partner = np.where((i % 16) < 8, idx + 8, idx - 8)
    return off + partner


_CONSTS = _consts()
_ROPE = _rope()


def make_in_maps(inp):
    f = lambda a: np.ascontiguousarray(np.asarray(a, dtype=np.float32))
    x, c, ctx, c_ctx = f(inp["x"]), f(inp["c"]), f(inp["ctx"]), f(inp["c_ctx"])
    b_adaT = f(f(inp["b_ada"]).reshape(DEPTH, 72, 128).transpose(2, 0, 1))
    ln = np.stack([f(inp["ln_g"]), f(inp["ln_b"])], axis=2)
    lnT = f(ln.reshape(DEPTH, 3, 2, 8, 128).transpose(4, 0, 1, 2, 3))
    wmix = f(inp["w_mix_in"])
    sw_cols = np.concatenate([_partner_cols(MIXOFF["da_q"][0]), _partner_cols(MIXOFF["da_k"][0])])
    rpb = f(inp["na_rpb"])
    nab = np.empty((DEPTH, 4, NPAT, 128, 128), np.float32)
    for pi, (valid, dr, dc) in enumerate(NA_PATS):
        g = rpb[:, :, dr, dc]
        nab[:, :, pi] = np.where(valid[None, None], g, np.float32(NEG))
    gains = np.ones((DEPTH, 4, 256), np.float32)
    gains[:, 1], gains[:, 2], gains[:, 3] = f(inp["ml_norm_g"]), f(inp["gla_norm_g"]), f(inp["da_norm_g"])
    gainsT = f(gains.reshape(DEPTH, 4, 2, 128).transpose(3, 0, 1, 2))
    wa2 = f(np.concatenate([f(inp["gla_w_a2"]), f(inp["gla_b_a"])[:, :, None, :]], axis=2))
    shared = {
        "w_ada": f(inp["w_ada"]), "b_adaT": b_adaT, "lnT": lnT,
        "ffn_w_in": f(inp["ffn_w_in"]), "ffn_w_out": f(inp["ffn_w_out"]),
        "w_mix_in": wmix, "w_mix_sw": f(wmix[:, :, sw_cols]), "w_branch": f(inp["w_branch"]), "w_out": f(inp["w_out"]),
        "nab": nab, "consts": _CONSTS, "rope": _ROPE, "ml_gate_b": f(inp["ml_gate_b"]), "gainsT": gainsT,
        "gla_wa2": wa2, "da_lambda": f(f(inp["da_lambda"]).reshape(DEPTH, 128)),
    }
    maps = []
    for r in range(NCORES):
        bs = slice(r * BPC, (r + 1) * BPC)
        cc = np.concatenate([c[bs], c_ctx[None, :]], axis=0)
        m = dict(shared)
        m["x_in"] = f(np.concatenate([x[bs], ctx[bs]], axis=1))
        m["cT"] = f(cc.reshape(5, 8, 128).transpose(2, 1, 0))
        maps.append(m)
    return maps


def kernel(**inputs):
    nc = build_program()
    maps = make_in_maps(inputs)
    res = run_bass_kernel_spmd(nc, maps, core_ids=list(range(NCORES)))
    return np.concatenate([r["out"] for r in res.results], axis=0).astype(np.float32)
```

```python
import math
from contextlib import ExitStack

import numpy as np
import concourse.bass as bass
import concourse.mybir as mybir
from concourse.bass_utils import run_bass_kernel_spmd

F32 = mybir.dt.float32
BF16 = mybir.dt.bfloat16
ALU = mybir.AluOpType
AF = mybir.ActivationFunctionType
AX = mybir.AxisListType

NCORES = 8
D = 1024
DEPTH = 2
BPC = 4
SEQ = 2048
CTX = 256
T = SEQ + CTX
NT = T // 128
DFF = 2816
NJ = DFF // 128
NMOD = 9
ALPHA = (2 * DEPTH) ** 0.25
LN_EPS = 1e-5 / (ALPHA * ALPHA)
TILES512 = [(0, 512), (512, 512), (1024, 512), (1536, 512), (2048, 256)]


class _Rec:
    def __getattr__(self, name):
        def f(*a, **k):
            self.call = (name, a, k)
        return f


class Tracker:
    ENGS = ("pe", "act", "dve", "pool", "sp")

    def __init__(self, nc, es):
        self.nc = nc
        self.es = es
        self.eng_obj = {"pe": nc.tensor, "act": nc.scalar, "dve": nc.vector, "pool": nc.gpsimd, "sp": nc.sync}
        self.sem = {e: es.enter_context(nc.semaphore("sem_" + e)) for e in ("pe", "act", "dve", "pool")}
        self.count = {e: 0 for e in self.sem}
        self.pending = {e: False for e in self.sem}
        self.dsem = {}
        self.dcount = {}
        self.waited = {e: {} for e in self.ENGS}
        self.writers = {}
        self.readers = {}
        self.stream = {e: [] for e in self.ENGS}
        self.nops = 0

    def _dma_sem(self, slot):
        if slot not in self.dsem:
            self.dsem[slot] = self.es.enter_context(self.nc.semaphore("d_" + str(len(self.dsem))))
            self.dcount[slot] = 0
        return self.dsem[slot]

    def _deps(self, r, w):
        deps = []
        for x in r:
            t = self.writers.get(x)
            if t is not None:
                deps.append(t)
        for x in w:
            t = self.writers.get(x)
            if t is not None:
                deps.append(t)
            deps.extend(self.readers.get(x, ()))
        return deps

    def _commit(self, tok, r, w):
        for x in r:
            self.readers.setdefault(x, []).append(tok)
        for x in w:
            self.writers[x] = tok
            self.readers[x] = []

    def _waits(self, eng, deps):
        need = {}
        for kind, key, val in deps:
            if kind == "E" and key == "pe" and eng == "pe":
                continue
            k = (kind, key)
            if val > need.get(k, 0):
                need[k] = val
        out = []
        for k, val in need.items():
            if self.waited[eng].get(k, 0) >= val:
                continue
            self.waited[eng][k] = val
            sem = self.sem[k[1]] if k[0] == "E" else self.dsem[k[1]]
            out.append((sem, val))
        return out

    def op(self, eng, fn, r=(), w=(), sig=True):
        rec = _Rec()
        fn(rec)
        name_, a_, k_ = rec.call
        fn = lambda e, name_=name_, a_=a_, k_=k_: getattr(e, name_)(*a_, **k_)
        deps = self._deps(r, w)
        waits = self._waits(eng, deps)
        if sig:
            self.count[eng] += 1
            tok = ("E", eng, self.count[eng])
        else:
            tok = ("E", eng, self.count[eng] + 1)
        self._commit(tok, r, w)
        sem = self.sem[eng]

        def emit(e, waits=waits, fn=fn, sig=sig, sem=sem):
            for s, v in waits:
                e.wait_ge(s, v)
            ins = fn(e)
            if sig:
                ins.then_inc(sem, 1)
        self.stream[eng].append(emit)
        self.nops += 1
        return tok

    def dma(self, q, out, in_, slot, r=(), w=()):
        deps = self._deps(r, w)
        waits = self._waits(q, deps)
        sem = self._dma_sem(slot)
        self.dcount[slot] += 16
        tok = ("D", slot, self.dcount[slot])
        self._commit(tok, r, w)

        def emit(e, waits=waits, sem=sem, out=out, in_=in_):
            for s, v in waits:
                e.wait_ge(s, v)
            e.dma_start(out=out, in_=in_).then_inc(sem, 16)
        self.stream[q].append(emit)
        return tok

    def finish(self, eng="sp"):
        deps = []
        for e in self.sem:
            if self.count[e]:
                deps.append(("E", e, self.count[e]))
        for s, c in self.dcount.items():
            if c:
                deps.append(("D", s, c))
        waits = self._waits(eng, deps)

        def emit(e, waits=waits):
            for s, v in waits:
                e.wait_ge(s, v)
        self.stream[eng].append(emit)

    def phase_end(self):
        self.finish("sp")
        self.flush()
        self.stream = {e: [] for e in self.ENGS}

    def flush(self):
        with self.nc.Block() as block:
            for name, deco in (("sp", block.sync), ("pe", block.tensor), ("act", block.scalar),
                               ("dve", block.vector), ("pool", block.gpsimd)):
                lst = self.stream[name]

                def body(e, lst=lst):
                    for f in lst:
                        f(e)
                deco(body)


def _mix_cols():
    widths = (("na_q", 256), ("na_k", 256), ("na_v", 256), ("ml_q", 256), ("ml_k", 256), ("ml_v", 256),
              ("ml_o", 256), ("ml_if", 16), ("gla_q", 128), ("gla_k", 128), ("gla_v", 256), ("gla_g", 256),
              ("gla_a", 32), ("da_q", 256), ("da_k", 256), ("da_v", 256), ("gates", 4096))
    off, o = {}, 0
    for n, w in widths:
        off[n] = (o, w)
        o += w
    return off, o


MIXOFF, NCOLS = _mix_cols()


def xm_keys(b, t0, n):
    return [("XM", b, tt) for tt in range(t0 // 128, (t0 + n) // 128)]


class Builder:
    def __init__(self, debug=None):
        self.debug = debug
        self.nc = bass.Bass("TRN2", target_bir_lowering=False)
        self.es = ExitStack()
        self.tk = Tracker(self.nc, self.es)
        self.rr = 0

    def dram_in(self, name, shape, dt=F32):
        return self.nc.dram_tensor(name, list(shape), dt, kind="ExternalInput").ap()

    def dram_out(self, name, shape, dt=F32):
        return self.nc.dram_tensor(name, list(shape), dt, kind="ExternalOutput").ap()

    def dram_scr(self, name, shape, dt):
        return self.nc.dram_tensor(name, list(shape), dt, kind="Internal").ap()

    def sb(self, name, shape, dt):
        return self.es.enter_context(self.nc.sbuf_tensor(name, list(shape), dt))

    def ps(self, name, shape, dt=F32):
        return self.es.enter_context(self.nc.psum_tensor(name, list(shape), dt))

    def mm(self, out, lhsT, rhs, start, stop, r, w, sig=None, sgc=False):
        if sig is None:
            sig = stop
        return self.tk.op("pe", lambda e: e.matmul(out, lhsT=lhsT, rhs=rhs, start=start, stop=stop, skip_group_check=sgc),
                          r=r, w=w, sig=sig)

    def anyeng(self, engs=("dve", "pool", "act")):
        self.rr += 1
        return engs[self.rr % len(engs)]


GRID_W = 64
NEG = -30000.0
LAMBDA_INIT = [0.8 - 0.6 * math.exp(-0.3 * l) for l in range(DEPTH)]


def na_geometry():
    pats, pat_index, per_tile = [], {}, []
    k = np.arange(128)
    q = np.arange(128)
    for i in range(16):
        qr = (2 * i + q // 64)[None, :]
        qc = (q % 64)[None, :]
        r0 = np.clip(qr - 4, 0, 32 - 8)
        c0 = np.clip(qc - 8, 0, GRID_W - 16)
        lst = []
        for j in range(16):
            kr = (2 * j + k // 64)[:, None]
            kc = (k % 64)[:, None]
            valid = (kr >= r0) & (kr < r0 + 8) & (kc >= c0) & (kc < c0 + 16)
            if not valid.any():
                continue
            dr = np.where(valid, kr - qr + 7, 0).astype(np.int64)
            dc = np.where(valid, np.clip(kc - qc + 15, 0, 30), 0).astype(np.int64)
            key = valid.tobytes() + dr.tobytes() + dc.tobytes()
            if key not in pat_index:
                pat_index[key] = len(pats)
                pats.append((valid, dr, dc))
            lst.append((j, pat_index[key]))
        per_tile.append(lst)
    return pats, per_tile


NA_PATS, NA_TILES = na_geometry()
NPAT = len(NA_PATS)


def build_program(debug=None):
    debug = debug or {}
    B = Builder(debug)
    nc, tk = B.nc, B.tk
    L = DEPTH

    x_in = B.dram_in("x_in", [BPC, T, D])
    cT_in = B.dram_in("cT", [128, 8, 5])
    w_ada = B.dram_in("w_ada", [L, D, NMOD * D])
    b_adaT = B.dram_in("b_adaT", [128, L, 72])
    lnT = B.dram_in("lnT", [128, L, 3, 2, 8])
    ffn_w_in = B.dram_in("ffn_w_in", [L, 2, D, 2 * DFF])
    ffn_w_out = B.dram_in("ffn_w_out", [L, 2, DFF, D])
    w_mix = B.dram_in("w_mix_in", [L, D, NCOLS])
    w_sw = B.dram_in("w_mix_sw", [L, D, 512])
    w_branch = B.dram_in("w_branch", [L, 4, 256, D])
    w_outp = B.dram_in("w_out", [L, D, D])
    nab_in = B.dram_in("nab", [L, 4, NPAT, 128, 128])
    consts_in = B.dram_in("consts", [128, 1024])
    rope_in = B.dram_in("rope", [2, 128, SEQ])
    gb_in = B.dram_in("ml_gate_b", [L, 16])
    gains_in = B.dram_in("gainsT", [128, L, 4, 2])
    wa2_in = B.dram_in("gla_wa2", [L, 2, 17, 128])
    dal_in = B.dram_in("da_lambda", [L, 128])
    out_d = B.dram_out("out", [BPC, SEQ, D])
    dbg_y = B.dram_out("dbg_y", [4, 128, 2, T], BF16) if debug.get("dump_y") else None
    XM = B.dram_scr("xm", [BPC, 128, 8, T], F32)
    WIN = B.dram_scr("win_bf", [L, 2, NJ // 2, 128, 8, 2, 256], BF16)
    WOUT = B.dram_scr("wout_bf", [L, 2, 8, 128, NJ, 128], BF16)
    WMS = B.dram_scr("wmix_bf", [L, 128, 128, 8, 256], BF16)
    WBSC = B.dram_scr("wbr_bf", [L, 8, 128, 4, 2, 128], BF16)
    wcache = {}

    consts = B.sb("consts_sb", [128, 1024], F32)
    ident = consts[:, 0:128]
    tri = [consts[:, 128:256], consts[:, 256:384]]
    bd4 = consts[:, 384:640]
    bd2 = consts[:, 640:770]
    hm = consts[:, 770:774]
    mAB = consts[:, 774:776]
    identb = B.sb("identb", [128, 128], BF16)
    ones_f = B.sb("ones_f", [128, 128], F32)
    cT = B.sb("cT_sb", [128, 8, 5], F32)
    modp = B.sb("modp", [128, L, 72, 5], F32)
    sc1 = B.sb("sc1", [128, L, 3, 8, 5], F32)
    gsc = B.sb("gsc", [128, L, 3, 8, 5], F32)
    badaT = B.sb("badaT", [128, L, 72], F32)
    lnp = B.sb("lnp", [128, L, 3, 2, 8], F32)
    geff = B.sb("geff", [128, L, 4, 2], F32)
    nlam = B.sb("nlam", [128, L], F32)
    gbias = B.sb("gbias", [128, L, 16], F32)
    dl = B.sb("dl", [128, 128], F32)
    dls = B.sb("dls", [128, 4], F32)
    pb = [B.ps(f"pb{i}", [128, 512], F32) for i in range(8)]
    rot = {}

    def nxt(name, items):
        i = rot.get(name, 0)
        rot[name] = i + 1
        return items[i % len(items)]

    def bank(name, ids):
        i = nxt(name, ids)
        return pb[i], f"pb{i}"

    def prologue():
        es = ExitStack()
        adaw = [es.enter_context(nc.sbuf_tensor(f"adaw{i}", [128, 1152], F32)) for i in range(2)]
        xtok = [es.enter_context(nc.sbuf_tensor(f"pxtok{i}", [128, D], F32)) for i in range(2)]
        stgs = [es.enter_context(nc.sbuf_tensor(f"pstg{i}", [128, 8, 128], F32)) for i in range(2)]
        tk.dma("sp", consts[:], consts_in[:, :], "c_consts", w=["consts"])
        tk.dma("sp", cT[:], cT_in[:, :, :], "c_ct", w=["cT"])
        tk.dma("sp", badaT[:], b_adaT[:, :, :], "c_bada", w=["badaT"])
        tk.dma("sp", lnp[:], lnT[:, :, :, :, :], "c_ln", w=["lnp"])
        tk.dma("sp", geff[:], gains_in[:, :, :, :], "c_geff", w=["geff"])
        for l in range(L):
            tk.dma("sp", gbias[:, l, :], gb_in[l:l + 1, :].partition_broadcast(128), f"c_gb{l}", w=[("gbias", l)])
        tk.op("pool", lambda e: e.memset(ones_f[:], 1.0), w=["ones_f"])
        tk.op("dve", lambda e: e.tensor_copy(out=identb[:], in_=ident), r=["consts"], w=["identb"])
        for l in range(L):
            tk.op("dve", lambda e, l=l: e.tensor_scalar(out=geff[:, l, 3, :], in0=geff[:, l, 3, :], scalar1=1.0 - LAMBDA_INIT[l],
                                                        scalar2=None, op0=ALU.mult), r=["geff"], w=["geff"])
            tk.dma("sp", dl[:], dal_in[l:l + 1, :].partition_broadcast(128), "c_dl", w=["dl"])
            tk.op("dve", lambda e: e.tensor_tensor(out=dl[:, 0:32], in0=dl[:, 0:32], in1=dl[:, 32:64], op=ALU.mult), r=["dl"], w=["dl"])
            tk.op("dve", lambda e: e.tensor_tensor(out=dl[:, 64:96], in0=dl[:, 64:96], in1=dl[:, 96:128], op=ALU.mult), r=["dl"], w=["dl"])
            tk.op("dve", lambda e: e.tensor_reduce(out=dls[:, 0:1], in_=dl[:, 0:32], axis=AX.X, op=ALU.add), r=["dl"], w=["dls"])
            tk.op("dve", lambda e: e.tensor_reduce(out=dls[:, 1:2], in_=dl[:, 64:96], axis=AX.X, op=ALU.add), r=["dl"], w=["dls"])
            tk.op("act", lambda e: e.activation(out=dls[:, 2:4], in_=dls[:, 0:2], func=AF.Exp), r=["dls"], w=["dls"])
            tk.op("dve", lambda e: e.tensor_tensor(out=dls[:, 0:1], in0=dls[:, 3:4], in1=dls[:, 2:3], op=ALU.subtract), r=["dls"], w=["dls"])
            tk.op("dve", lambda e, l=l: e.tensor_scalar(out=nlam[:, l:l + 1], in0=dls[:, 0:1], scalar1=-LAMBDA_INIT[l], scalar2=None,
                                                        op0=ALU.add), r=["dls"], w=["nlam"])
        tk.op("act", lambda e: e.activation(out=cT[:], in_=cT[:], func=AF.Silu), r=["cT"], w=["cT"])
        ai = 0
        for l in range(L):
            for cg in range(8):
                pbk = pb[cg % 2]
                for kc in range(8):
                    buf, bkey = adaw[ai % 2], f"adaw{ai % 2}"
                    ai += 1
                    tk.dma("sp", buf[:], w_ada[l, kc * 128:(kc + 1) * 128, cg * 1152:(cg + 1) * 1152], bkey, w=[bkey])
                    for m in range(9):
                        B.mm(pbk[:, m * 5:(m + 1) * 5], buf[:, m * 128:(m + 1) * 128], cT[:, kc, :],
                             start=(kc == 0 and m == 0), stop=(kc == 7), r=[bkey, "cT"], w=[f"pb{cg % 2}"], sig=(m == 8), sgc=True)
                tk.op("dve", lambda e, l=l, cg=cg, pbk=pbk: e.tensor_tensor(
                    out=modp[:, l, cg * 9:(cg + 1) * 9, :], in0=pbk[:, 0:45].rearrange("p (m c) -> p m c", c=5),
                    in1=badaT[:, l, cg * 9:(cg + 1) * 9].unsqueeze(2).to_broadcast([128, 9, 5]), op=ALU.add),
                    r=[f"pb{cg % 2}", "badaT"], w=["modp"])
        for l in range(L):
            for s in range(3):
                gmul = (0.5 if s != 1 else 1.0) / ALPHA
                tk.op("dve", lambda e, l=l, s=s: e.tensor_scalar(
                    out=sc1[:, l, s, :, :], in0=modp[:, l, (3 * s + 1) * 8:(3 * s + 2) * 8, :], scalar1=1.0, scalar2=None,
                    op0=ALU.add), r=["modp"], w=["sc1"])
                tk.op("dve", lambda e, l=l, s=s, gmul=gmul: e.tensor_scalar(
                    out=gsc[:, l, s, :, :], in0=modp[:, l, (3 * s + 2) * 8:(3 * s + 3) * 8, :], scalar1=gmul, scalar2=None,
                    op0=ALU.mult), r=["modp"], w=["gsc"])
        li = 0
        for b in range(BPC):
            for tt in range(NT):
                buf, bkey = xtok[li % 2], f"pxtok{li % 2}"
                stg, skey = stgs[li % 2], f"pstg{li % 2}"
                li += 1
                tk.dma("sp", buf[:], x_in[b, tt * 128:(tt + 1) * 128, :], bkey, w=[bkey])
                for half in range(2):
                    pbk, pkey = pb[2 + half], f"pb{2 + half}"
                    for q in range(4):
                        fc = half * 4 + q
                        tk.op("pe", lambda e, pbk=pbk, q=q, buf=buf, fc=fc: e.transpose(
                            pbk[:, q * 128:(q + 1) * 128], buf[:, fc * 128:(fc + 1) * 128], ident),
                            r=[bkey, "consts"], w=[pkey], sig=(q == 3))
                    if half:
                        tk.op("dve", lambda e, pbk=pbk, stg=stg: e.tensor_copy(
                            out=stg[:, 4:8, :], in_=pbk[:, :].rearrange("p (q t) -> p q t", q=4)), r=[pkey], w=[(skey, 1)])
                    else:
                        tk.op("act", lambda e, pbk=pbk, stg=stg: e.copy(
                            out=stg[:, 0:4, :], in_=pbk[:, :].rearrange("p (q t) -> p q t", q=4)), r=[pkey], w=[(skey, 0)])
                tk.dma("sp", XM[b, :, :, tt * 128:(tt + 1) * 128], stg[:, :, :], skey, r=[(skey, 0), (skey, 1)], w=[("XM", b, tt)])
        tk.phase_end()
        es.close()

    def ln_tiles(es, n, pfx):
        d = {"n": n, "cnt": 0}
        uid = nxt("uid", list(range(100)))
        d["sq"] = [es.enter_context(nc.sbuf_tensor(f"{pfx}sq{i}_u{uid}", [128, n], F32)) for i in range(2)]
        d["tmp"] = [es.enter_context(nc.sbuf_tensor(f"{pfx}tmp{i}_u{uid}", [128, n], F32)) for i in range(2)]
        for nm in ("mean", "rstd", "pre"):
            d[nm] = es.enter_context(nc.sbuf_tensor(f"{pfx}{nm}_u{uid}", [128, n], F32))
        d["pfx"] = pfx
        return d

    def layer_norm_store(lt, zt, ztk, xob, xok, l, s, b, t0, n):
        psS, psQ = pb[6], pb[7]
        pfx = lt["pfx"]
        mean, rstd, pre = lt["mean"], lt["rstd"], lt["pre"]
        mk, rk, pk = pfx + "mean", pfx + "rstd", pfx + "pre"
        for fc in range(8):
            i = lt["cnt"] % 2
            lt["cnt"] += 1
            sqb, sqk = lt["sq"][i], f"{pfx}sq{i}"
            tk.op("act", lambda e, sqb=sqb, fc=fc: e.activation(out=sqb[:, :n], in_=zt[:, fc, :n], func=AF.Square),
                  r=[(ztk, fc)], w=[sqk])
            B.mm(psS[:, :n], ones_f[:], zt[:, fc, :n], start=(fc == 0), stop=(fc == 7), r=["ones_f", (ztk, fc)], w=["pb6"], sig=True)
            B.mm(psQ[:, :n], ones_f[:], sqb[:, :n], start=(fc == 0), stop=(fc == 7), r=["ones_f", sqk], w=["pb7"], sig=True)
        tk.op("act", lambda e: e.activation(out=mean[:, :n], in_=psS[:, :n], func=AF.Copy, scale=1.0 / D), r=["pb6"], w=[mk])
        tk.op("dve", lambda e: e.tensor_tensor(out=pre[:, :n], in0=mean[:, :n], in1=mean[:, :n], op=ALU.mult), r=[mk], w=[pk])
        tk.op("dve", lambda e: e.scalar_tensor_tensor(out=rstd[:, :n], in0=psQ[:, :n], scalar=1.0 / D, in1=pre[:, :n],
                                                      op0=ALU.mult, op1=ALU.subtract), r=["pb7", pk], w=[rk])
        tk.op("dve", lambda e: e.tensor_scalar(out=rstd[:, :n], in0=rstd[:, :n], scalar1=0.0, scalar2=LN_EPS,
                                               op0=ALU.max, op1=ALU.add), r=[rk], w=[rk])
        tk.op("act", lambda e: e.activation(out=rstd[:, :n], in_=rstd[:, :n], func=AF.Sqrt), r=[rk], w=[rk])
        tk.op("dve", lambda e: e.reciprocal(out=rstd[:, :n], in_=rstd[:, :n]), r=[rk], w=[rk])
        tk.op("dve", lambda e: e.tensor_tensor(out=pre[:, :n], in0=mean[:, :n], in1=rstd[:, :n], op=ALU.mult), r=[mk, rk], w=[pk])
        for fc in range(8):
            i = lt["cnt"] % 2
            lt["cnt"] += 1
            tb, tkey = lt["tmp"][i], f"{pfx}tmp{i}"
            tk.op("pool", lambda e, tb=tb, fc=fc: e.tensor_tensor(out=tb[:, :n], in0=zt[:, fc, :n], in1=rstd[:, :n], op=ALU.mult),
                  r=[(ztk, fc), rk], w=[tkey])
            tk.op("dve", lambda e, tb=tb: e.tensor_tensor(out=tb[:, :n], in0=tb[:, :n], in1=pre[:, :n], op=ALU.subtract),
                  r=[tkey, pk], w=[tkey])
            tk.op("act", lambda e, tb=tb, fc=fc: e.activation(
                out=xob[:, fc, :n], in_=tb[:, :n], func=AF.Identity, scale=lnp[:, l, s, 0, fc:fc + 1], bias=lnp[:, l, s, 1, fc:fc + 1]),
                r=[tkey, "lnp"], w=[xok])
        tk.dma("sp", XM[b, :, :, t0:t0 + n], xob[:, :, :n], xok, r=[xok], w=xm_keys(b, t0, n))

    def ffn_phase(l, k, only=None):
        es = ExitStack()
        uid = nxt("uid", list(range(100)))
        A = lambda name, shape, dt: es.enter_context(nc.sbuf_tensor(f"{name}_u{uid}", list(shape), dt))
        xs = [A(f"fxs{i}", [128, 8, 512], F32) for i in range(2)]
        hT = A("fhT", [128, 8, 1024], BF16)
        gT = A("fgT", [128, NJ, 1024], BF16)
        sa = [A(f"fsa{i}", [128, 512], F32) for i in range(2)]
        zt = A("fzt", [128, 8, 1024], F32)
        winb = [A(f"fwin{i}", [128, 8, 2, 256], BF16) for i in range(2)]
        woutb = [A(f"fwout{i}", [128, NJ, 128], BF16) for i in range(2)]
        lt = ln_tiles(es, 512, "f")
        s = 0 if k == 0 else 2
        cnt = {"win": 0, "wout": 0, "sa": 0, "pa": 0, "py": 0}
        first = True
        for b in range(BPC):
            for blk in ((0, 1), (2, 3), (4,)):
                tiles = []
                for ti in blk:
                    if only is not None and b * 5 + ti >= only:
                        continue
                    if l == L - 1 and k == 1 and ti == 4:
                        continue
                    tiles.append((len(tiles), ti) + TILES512[ti])
                if not tiles:
                    continue
                for idx, ti, t0, n in tiles:
                    col = b if ti < 4 else 4
                    xb, xk = xs[idx], f"fxs{idx}"
                    tk.dma("sp", xb[:, :, :n], XM[b, :, :, t0:t0 + n], xk, r=xm_keys(b, t0, n), w=[xk])
                    for fc in range(8):
                        eng = ("dve", "pool")[fc % 2]
                        tk.op(eng, lambda e, fc=fc, xb=xb, col=col, n=n, idx=idx: e.tensor_scalar(
                            out=hT[:, fc, idx * 512:idx * 512 + n], in0=xb[:, fc, :n], scalar1=sc1[:, l, s, fc, col:col + 1],
                            scalar2=modp[:, l, (3 * s) * 8 + fc, col:col + 1], op0=ALU.mult, op1=ALU.add),
                            r=[xk, "sc1", "modp"], w=[("fhT", fc, idx)])
                for j in range(NJ):
                    if j % 2 == 0:
                        wb, wk = winb[cnt["win"] % 2], f"fwin{cnt['win'] % 2}"
                        cnt["win"] += 1
                        if first:
                            for ab in range(2):
                                c0 = ab * DFF + j * 128
                                tk.dma("pool", wb[:, :, ab, :], ffn_w_in[l, k, :, c0:c0 + 256].rearrange("(kc p) c -> p kc c", p=128),
                                       f"{wk}_{ab}", w=[(wk, ab)])
                            tk.dma("sp", WIN[l, k, j // 2, :, :, :, :], wb[:, :, :, :], f"{wk}_h", r=[(wk, 0), (wk, 1)], w=[("WIN", l, k, j // 2)])
                        else:
                            tk.dma("sp", wb[:, :, :, :], WIN[l, k, j // 2, :, :, :, :], f"{wk}_h", r=[("WIN", l, k, j // 2)], w=[(wk, 0), (wk, 1)])
                    jj = j % 2
                    for idx, ti, t0, n in tiles:
                        hs = slice(idx * 512, idx * 512 + n)
                        pa, pak = pb[cnt["pa"] % 2], f"pb{cnt['pa'] % 2}"
                        pbb, pbk = pb[2 + cnt["pa"] % 2], f"pb{2 + cnt['pa'] % 2}"
                        cnt["pa"] += 1
                        for kc in range(8):
                            B.mm(pa[:, :n], wb[:, kc, 0, jj * 128:(jj + 1) * 128], hT[:, kc, hs], start=(kc == 0), stop=(kc == 7),
                                 r=[(wk, 0), ("fhT", kc, idx)], w=[pak])
                        for kc in range(8):
                            B.mm(pbb[:, :n], wb[:, kc, 1, jj * 128:(jj + 1) * 128], hT[:, kc, hs], start=(kc == 0), stop=(kc == 7),
                                 r=[(wk, 1), ("fhT", kc, idx)], w=[pbk])
                        sab, sak = sa[cnt["sa"] % 2], f"fsa{cnt['sa'] % 2}"
                        cnt["sa"] += 1
                        tk.op("act", lambda e, sab=sab, pa=pa, n=n: e.activation(out=sab[:, :n], in_=pa[:, :n], func=AF.Silu), r=[pak], w=[sak])
                        tk.op("dve", lambda e, sab=sab, pbb=pbb, j=j, n=n, hs=hs: e.tensor_tensor(out=gT[:, j, hs], in0=sab[:, :n], in1=pbb[:, :n], op=ALU.mult),
                              r=[sak, pbk], w=[("fgT", j, idx)])
                for fc in range(8):
                    wb, wk = woutb[cnt["wout"] % 2], f"fwout{cnt['wout'] % 2}"
                    cnt["wout"] += 1
                    if first:
                        for hh in range(2):
                            tk.dma("pool", wb[:, hh * 11:(hh + 1) * 11, :],
                                   ffn_w_out[l, k, hh * 1408:(hh + 1) * 1408, fc * 128:(fc + 1) * 128].rearrange("(j p) c -> p j c", p=128),
                                   f"{wk}_{hh}", w=[(wk, hh)])
                        tk.dma("sp", WOUT[l, k, fc, :, :, :], wb[:, :, :], f"{wk}_h", r=[(wk, 0), (wk, 1)], w=[("WOUT", l, k, fc)])
                    else:
                        tk.dma("sp", wb[:, :, :], WOUT[l, k, fc, :, :, :], f"{wk}_h", r=[("WOUT", l, k, fc)], w=[(wk, 0), (wk, 1)])
                    for idx, ti, t0, n in tiles:
                        col = b if ti < 4 else 4
                        hs = slice(idx * 512, idx * 512 + n)
                        py, pyk = pb[4 + cnt["py"] % 2], f"pb{4 + cnt['py'] % 2}"
                        cnt["py"] += 1
                        for j in range(NJ):
                            B.mm(py[:, :n], wb[:, j, :], gT[:, j, hs], start=(j == 0), stop=(j == NJ - 1), r=[(wk, j // 11), ("fgT", j, idx)], w=[pyk])
                        tk.op("dve", lambda e, fc=fc, py=py, col=col, n=n, hs=hs, idx=idx: e.scalar_tensor_tensor(
                            out=zt[:, fc, hs], in0=py[:, :n], scalar=gsc[:, l, s, fc, col:col + 1], in1=xs[idx][:, fc, :n],
                            op0=ALU.mult, op1=ALU.add), r=[pyk, "gsc", f"fxs{idx}"], w=[(f"fzt{idx}", fc)])
                first = False
                for idx, ti, t0, n in tiles:
                    layer_norm_store(lt, zt[:, :, idx * 512:(idx + 1) * 512], f"fzt{idx}", xs[idx], f"fxs{idx}", l, s, b, t0, n)
        tk.phase_end()
        es.close()

    def epilogue():
        es = ExitStack()
        xs = [es.enter_context(nc.sbuf_tensor(f"exs{i}", [128, 8, 128], F32)) for i in range(2)]
        xtok = [es.enter_context(nc.sbuf_tensor(f"extok{i}", [128, D], F32)) for i in range(2)]
        li = 0
        for b in range(BPC):
            for tt in range(SEQ // 128):
                xb, xk = xs[li % 2], f"exs{li % 2}"
                ob, ok = xtok[li % 2], f"extok{li % 2}"
                li += 1
                tk.dma("sp", xb[:, :, :], XM[b, :, :, tt * 128:(tt + 1) * 128], xk, r=[("XM", b, tt)], w=[xk])
                for half in range(2):
                    pbk, pkey = pb[2 + half], f"pb{2 + half}"
                    for q in range(4):
                        fc = half * 4 + q
                        tk.op("pe", lambda e, pbk=pbk, q=q, xb=xb, fc=fc: e.transpose(
                            pbk[:, q * 128:(q + 1) * 128], xb[:, fc, :], ident), r=[xk, "consts"], w=[pkey], sig=(q == 3))
                    if half:
                        tk.op("dve", lambda e, pbk=pbk, ob=ob: e.tensor_copy(out=ob[:, 512:1024], in_=pbk[:, :]), r=[pkey], w=[(ok, 1)])
                    else:
                        tk.op("act", lambda e, pbk=pbk, ob=ob: e.copy(out=ob[:, 0:512], in_=pbk[:, :]), r=[pkey], w=[(ok, 0)])
                tk.dma("sp", out_d[b, tt * 128:(tt + 1) * 128, :], ob[:], ok, r=[(ok, 0), (ok, 1)], w=[("OUT", b, tt)])
        tk.phase_end()
        es.close()

    def mixer_phase(l):
        es = ExitStack()
        uid = nxt("uid", list(range(100)))
        A = lambda name, shape, dt: es.enter_context(nc.sbuf_tensor(f"{name}_u{uid}", list(shape), dt))
        hT = A("mhT", [128, 8, T], BF16)
        G = [A(f"mG{i}", [128, 2 * T], BF16) for i in range(4)]
        VA = A("mVA", [128, NT, 260], BF16)
        HACC = A("mH", [128, NT, 256], F32)
        yT = A("myT", [128, 4, 2, T], BF16)
        wsl = [A(f"mws{i}", [128, 8, 256], BF16) for i in range(2)]
        PT = [A(f"mPT{i}", [128, 512], BF16) for i in range(3)]
        tA = [A(f"mtA{i}", [128, 512], F32) for i in range(3)]
        ropeb = [A(f"mrope{i}", [128, 512], F32) for i in range(2)]
        nabb = A("mnab", [128, 5, 4, 128], BF16)
        sm = A("msm", [128, 64], F32)
        GIF = A("mGIF", [128, NT, 16], F32)
        LLc = A("mLLc", [128, NT, 8], F32)
        CS = A("mCS", [128, NT, 16], F32)
        UU = A("mUU", [128, NT, 8], F32)
        RR = A("mRR", [128, NT, 8], F32)
        DECP = A("mDECP", [128, NT, 2, 2], F32)
        Sst = [A(f"mS{i}", [128, 256], F32) for i in range(4)]
        Sbf = [A(f"mSb{i}", [128, 256], BF16) for i in range(4)]
        EE = [A(f"mEE{i}", [128, 128], F32) for i in range(6)]
        LLs = [A(f"mLL{i}", [128, 128], F32) for i in range(2)]
        qkt = [A(f"mqk{i}", [128, 6, 128], BF16) for i in range(2)]
        ktk = [A(f"mktk{i}", [128, 128], BF16) for i in range(2)]
        wa2 = A("mwa2", [32, 2, 128], BF16)
        WBs = [A(f"mWB{i}", [128, 4, 2, 128], BF16) for i in range(2)]
        accT = A("macc", [128, 8, 512], BF16)
        g32 = [G[i][:].bitcast(F32) for i in range(4)]
        xsb = g32[0][:, 0:2048].rearrange("p (c t) -> p c t", c=8)
        ztb = g32[1][:, 0:2048].rearrange("p (c t) -> p c t", c=8)
        xob = g32[2][:, 0:2048].rearrange("p (c t) -> p c t", c=8)
        lt = {"n": 256, "cnt": 0, "pfx": "m", "sq": [g32[3][:, 0:256], g32[3][:, 256:512]], "tmp": [g32[3][:, 512:768], g32[3][:, 768:1024]],
              "mean": g32[3][:, 1024:1280], "rstd": g32[3][:, 1280:1536], "pre": g32[3][:, 1536:1792]}
        hbf = HACC[:, :, :].bitcast(BF16).rearrange("p t f -> p (t f)")
        vbf = VA[:, :, :].rearrange("p t f -> p (t f)")
        gwb = [hbf[:, i * 1024:(i + 1) * 1024].rearrange("p (k c) -> p k c", k=8) for i in range(5)]
        wob = [hbf[:, (5 + i) * 1024:(6 + i) * 1024].rearrange("p (k c) -> p k c", k=8) for i in range(4)] + \
              [vbf[:, i * 1024:(i + 1) * 1024].rearrange("p (k c) -> p k c", k=8) for i in range(4)]
        GKEYS = [(f"mG{i}", h) for i in range(4) for h in range(2)] + [("mH", tt) for tt in range(NT)] + [("mVA", tt) for tt in range(NT)]
        TAILKEYS = ["mxs0", "mxo0", "msq0", "msq1", "mtmp0", "mtmp1", "mmean", "mrstd", "mpre"] + [("mzt", fc) for fc in range(8)] + \
                   [f"mgw{i}" for i in range(5)] + [f"mwo{i}" for i in range(8)]

        def fence(to_tail):
            r_, w_ = (GKEYS, TAILKEYS) if to_tail else (TAILKEYS, GKEYS)
            tk.op("dve", lambda e: e.memset(sm[:, 60:61], 0.0), r=r_, w=w_)

        ctx_out = (l < L - 1) and not debug.get("no_ctx_out")
        NTO = NT if ctx_out else SEQ // 128
        fm = lambda i: G[i][:].rearrange("p (c t) -> p c t", c=2)
        tm = lambda i: G[i][:].rearrange("p (t f) -> p t f", f=256)
        gk = lambda i, half: (f"mG{i}", half)
        wmix = w_mix[l]
        wcnt = {"n": 0}
        hkeys = [("mhT", fc) for fc in range(8)]
        TT_ORDER = {0: [16, 17] + list(range(16)), 1: [17, 16] + list(range(15, -1, -1))}

        def cached_slab(dst, dkey, src, n, ckey, slot=None):
            if (l, ckey) not in wcache:
                wcache[(l, ckey)] = len([1 for k_ in wcache if k_[0] == l])
                idx = wcache[(l, ckey)]
                tk.dma("pool", dst[:, :, :n], src, dkey, w=[dkey])
                tk.dma("sp", WMS[l, idx, :, :, :n], dst[:, :, :n], (slot or dkey) + "_h", r=[dkey], w=[("WMS", l, idx)])
            else:
                idx = wcache[(l, ckey)]
                tk.dma("sp", dst[:, :, :n], WMS[l, idx, :, :, :n], (slot or dkey) + "_h", r=[("WMS", l, idx)], w=[dkey])

        def load_w(src, n, ckey):
            i = wcnt["n"] % 2
            wcnt["n"] += 1
            wb, wk = wsl[i], f"mws{i}"
            cached_slab(wb, wk, src.rearrange("(kc p) c -> p kc c", p=128), n, ckey)
            return wb, wk

        def proj_fm(src, n, evac, banks=(0, 1)):
            wb, wk = load_w(src, n, ("c", src.offset, n))
            for (t0, nt) in TILES512:
                bk, bkey = bank("pj", banks)
                for kc in range(8):
                    B.mm(bk[:n, :nt], wb[:, kc, :n], hT[:, kc, t0:t0 + nt], start=(kc == 0), stop=(kc == 7), r=[wk, hkeys[kc]], w=[bkey])
                evac(bk, bkey, t0, nt)

        def proj_tm(src, n, evac, banks=(0, 1)):
            wb, wk = load_w(src, n, ("c", src.offset, n))
            for tt in range(NT):
                bk, bkey = bank("pj", banks)
                for kc in range(8):
                    B.mm(bk[:, :n], hT[:, kc, tt * 128:(tt + 1) * 128], wb[:, kc, :n], start=(kc == 0), stop=(kc == 7), r=[wk, hkeys[kc]], w=[bkey])
                evac(bk, bkey, tt)

        def half_of(t0):
            return 0 if t0 < T // 2 else 1

        def finish_branch(j, norm, center, gate_i):
            for tt in range(NTO):
                hk = ("mH", tt)
                src = HACC[:, tt, :]
                if norm:
                    x3 = HACC[:, tt, :].rearrange("p (h e) -> p h e", h=4)
                    t0_, t1_ = tA[0], tA[1]
                    if center:
                        tk.op("dve", lambda e, x3=x3: e.tensor_reduce(out=sm[:, 0:4], in_=x3, axis=AX.X, op=ALU.add), r=[hk], w=["msm"])
                        tk.op("dve", lambda e: e.tensor_scalar(out=sm[:, 0:4], in0=sm[:, 0:4], scalar1=-1.0 / 64, scalar2=None, op0=ALU.mult),
                              r=["msm"], w=["msm"])
                        tk.op("dve", lambda e, x3=x3, t0_=t0_: e.tensor_tensor(
                            out=t0_[:, 0:256].rearrange("p (h e) -> p h e", h=4), in0=x3,
                            in1=sm[:, 0:4].unsqueeze(2).to_broadcast([128, 4, 64]), op=ALU.add), r=[hk, "msm"], w=["mtA0"])
                        xc, xck = t0_[:, 0:256], "mtA0"
                    else:
                        xc, xck = src, hk
                    tk.op("act", lambda e, xc=xc, t1_=t1_: e.activation(out=t1_[:, 0:256], in_=xc, func=AF.Square), r=[xck], w=["mtA1"])
                    tk.op("dve", lambda e, t1_=t1_: e.tensor_reduce(out=sm[:, 4:8], in_=t1_[:, 0:256].rearrange("p (h e) -> p h e", h=4),
                                                                    axis=AX.X, op=ALU.add), r=["mtA1"], w=["msm"])
                    tk.op("dve", lambda e: e.tensor_scalar(out=sm[:, 4:8], in0=sm[:, 4:8], scalar1=1.0 / 64, scalar2=1e-6, op0=ALU.mult, op1=ALU.add),
                          r=["msm"], w=["msm"])
                    tk.op("act", lambda e: e.activation(out=sm[:, 4:8], in_=sm[:, 4:8], func=AF.Sqrt), r=["msm"], w=["msm"])
                    tk.op("dve", lambda e: e.reciprocal(out=sm[:, 4:8], in_=sm[:, 4:8]), r=["msm"], w=["msm"])
                    tk.op("dve", lambda e, xc=xc, t1_=t1_: e.tensor_tensor(
                        out=t1_[:, 0:256].rearrange("p (h e) -> p h e", h=4), in0=xc.rearrange("p (h e) -> p h e", h=4),
                        in1=sm[:, 4:8].unsqueeze(2).to_broadcast([128, 4, 64]), op=ALU.mult), r=[xck, "msm"], w=["mtA1"])
                    cur, curk = t1_[:, 0:256], "mtA1"
                    if gate_i is not None:
                        tk.op("dve", lambda e, t1_=t1_, tt=tt: e.tensor_tensor(out=t1_[:, 0:256], in0=t1_[:, 0:256], in1=tm(gate_i)[:, tt, :], op=ALU.mult),
                              r=["mtA1", gk(gate_i, tt // 9)], w=["mtA1"])
                else:
                    cur, curk = src, hk
                bk, bkey = bank("fin", (2, 3))
                for c in range(2):
                    tk.op("pe", lambda e, bk=bk, c=c, cur=cur: e.transpose(bk[:, c * 128:(c + 1) * 128], cur[:, c * 128:(c + 1) * 128], ident),
                          r=[curk, "consts"], w=[bkey], sig=(c == 1))
                tk.op("act", lambda e, bk=bk, tt=tt: e.copy(out=yT[:, j, :, tt * 128:(tt + 1) * 128], in_=bk[:, 0:256].rearrange("p (c t) -> p c t", c=2)),
                      r=[bkey], w=[("myT", j, tt)])

        def branch_na(b):
            QT, KT = fm(0), fm(1)
            for c in range(2):
                proj_fm(wmix[:, c * 128:(c + 1) * 128], 128,
                        lambda bk, bkey, t0, nt, c=c: tk.op("act", lambda e: e.activation(out=QT[:, c, t0:t0 + nt], in_=bk[:, :nt], func=AF.Copy, scale=0.125),
                                                            r=[bkey], w=[gk(0, c)]))
                proj_fm(wmix[:, 256 + c * 128:256 + (c + 1) * 128], 128,
                        lambda bk, bkey, t0, nt, c=c: tk.op("dve", lambda e: e.tensor_copy(out=KT[:, c, t0:t0 + nt], in_=bk[:, :nt]),
                                                            r=[bkey], w=[gk(1, c)]))
            tk.op("pool", lambda e: e.memset(VA[:, :, :].rearrange("p t (h e) -> p (t h) e", h=4)[:, :, 64:65], 1.0), w=[("mVA", tt) for tt in range(NT)])
            proj_tm(wmix[:, 512:768], 256,
                    lambda bk, bkey, tt: tk.op("dve", lambda e: e.tensor_copy(
                        out=VA[:, tt, :].rearrange("p (h e) -> p h e", h=4)[:, :, 0:64], in_=bk[:, 0:256].rearrange("p (h e) -> p h e", h=4)),
                        r=[bkey], w=[("mVA", tt)]))
            for i in range(NTO):
                local = NA_TILES[i] if i < 16 else []
                for slot, (j, pid) in enumerate(local):
                    tk.dma("pool", nabb[:, slot, :, :], nab_in[l, :, pid, :, :].rearrange("h k q -> k h q"), f"mnab{slot}", w=[("mnab", slot)])
                ktiles = [(j, slot) for slot, (j, pid) in enumerate(local)] + [(16, None), (17, None)]
                ob, okey = bank("naO", (4, 5))
                for h in range(4):
                    c, base = h // 2, (h % 2) * 64
                    for idx, (j, slot) in enumerate(ktiles):
                        sbk, skey = bank("naS", (0, 1, 2, 3))
                        B.mm(sbk[:, :128], KT[base:base + 64, c, j * 128:(j + 1) * 128], QT[base:base + 64, c, i * 128:(i + 1) * 128],
                             start=True, stop=(slot is None), r=[gk(1, c), gk(0, c)], w=[skey], sig=True)
                        if slot is not None:
                            B.mm(sbk[:, :128], identb[:], nabb[:, slot, h, :], start=False, stop=True, r=["identb", ("mnab", slot)], w=[skey], sig=True)
                        pi = nxt("pt", (0, 1, 2))
                        pt, ptk = PT[pi], f"mPT{pi}"
                        tk.op("act", lambda e, pt=pt, sbk=sbk: e.activation(out=pt[:, :128], in_=sbk[:, :128], func=AF.Exp), r=[skey], w=[ptk])
                        B.mm(ob[:, h * 65:(h + 1) * 65], pt[:, :128], VA[:, j, h * 65:(h + 1) * 65], start=(idx == 0), stop=(idx == len(ktiles) - 1),
                             r=[ptk, ("mVA", j)], w=[okey], sig=True)
                o3 = ob[:, 0:260].rearrange("p (h e) -> p h e", h=4)
                tk.op("dve", lambda e, o3=o3: e.reciprocal(out=sm[:, 8:12], in_=o3[:, :, 64]), r=[okey], w=["msm"])
                tk.op("dve", lambda e, o3=o3, i=i: e.tensor_tensor(out=HACC[:, i, :].rearrange("p (h e) -> p h e", h=4), in0=o3[:, :, 0:64],
                                                                  in1=sm[:, 8:12].unsqueeze(2).to_broadcast([128, 4, 64]), op=ALU.mult),
                      r=[okey, "msm"], w=[("mH", i)])
            finish_branch(0, False, False, None)

        def branch_da(b):
            qA, qB, kTc = fm(0)[:, 0, :], fm(0)[:, 1, :], fm(1)[:, 0, :]
            tk.op("pool", lambda e: e.memset(VA[:, :, :].rearrange("p t (h e) -> p (t h) e", h=4)[:, :, 64:65], 1.0), w=[("mVA", tt) for tt in range(NT)])
            proj_tm(wmix[:, MIXOFF["da_v"][0]:MIXOFF["da_v"][0] + 256], 256,
                    lambda bk, bkey, tt: tk.op("dve", lambda e: e.tensor_copy(
                        out=VA[:, tt, :].rearrange("p (h e) -> p h e", h=4)[:, :, 0:64], in_=bk[:, 0:256].rearrange("p (h e) -> p h e", h=4)),
                        r=[bkey], w=[("mVA", tt)]))
            for c in range(2):
                for which in ("q", "k"):
                    col0 = MIXOFF["da_" + which][0] + c * 128
                    sw0 = (0 if which == "q" else 256) + c * 128
                    w1, w1k = load_w(wmix[:, col0:col0 + 128], 128, ("da1", which, c))
                    w2, w2k = load_w(w_sw[l][:, sw0:sw0 + 128], 128, ("da2", which, c))
                    for (t0, nt) in TILES512:
                        b1, b1k = bank("pj", (0, 1))
                        for kc in range(8):
                            B.mm(b1[:, :nt], w1[:, kc, :128], hT[:, kc, t0:t0 + nt], start=(kc == 0), stop=(kc == 7), r=[w1k, hkeys[kc]], w=[b1k])
                        r_, rk_ = tA[2], "mtA2"
                        if t0 < SEQ:
                            b2, b2k = bank("pj2", (2, 3))
                            for kc in range(8):
                                B.mm(b2[:, :nt], w2[:, kc, :128], hT[:, kc, t0:t0 + nt], start=(kc == 0), stop=(kc == 7), r=[w2k, hkeys[kc]], w=[b2k])
                            tk.dma("sp", ropeb[0][:, :nt], rope_in[0, :, t0:t0 + nt], "mrope0", w=["mrope0"])
                            tk.dma("sp", ropeb[1][:, :nt], rope_in[1, :, t0:t0 + nt], "mrope1", w=["mrope1"])
                            tk.op("dve", lambda e, b1=b1, nt=nt: e.tensor_tensor(out=tA[0][:, :nt], in0=b1[:, :nt], in1=ropeb[0][:, :nt], op=ALU.mult),
                                  r=[b1k, "mrope0"], w=["mtA0"])
                            tk.op("dve", lambda e, b2=b2, nt=nt: e.tensor_tensor(out=tA[1][:, :nt], in0=b2[:, :nt], in1=ropeb[1][:, :nt], op=ALU.mult),
                                  r=[b2k, "mrope1"], w=["mtA1"])
                            tk.op("pool", lambda e, nt=nt: e.tensor_tensor(out=r_[:, :nt], in0=tA[0][:, :nt], in1=tA[1][:, :nt], op=ALU.add),
                                  r=["mtA0", "mtA1"], w=[rk_])
                        else:
                            tk.op("pool" if False else "dve", lambda e, b1=b1, nt=nt: e.tensor_copy(out=r_[:, :nt], in_=b1[:, :nt]), r=[b1k], w=[rk_])
                        if which == "q":
                            tk.op("act", lambda e, t0=t0, nt=nt: e.activation(out=qA[:, t0:t0 + nt], in_=r_[:, :nt], func=AF.Copy, scale=mAB[:, 0:1]),
                                  r=[rk_, "consts"], w=[gk(0, 0)])
                            tk.op("act", lambda e, t0=t0, nt=nt: e.activation(out=qB[:, t0:t0 + nt], in_=r_[:, :nt], func=AF.Copy, scale=mAB[:, 1:2]),
                                  r=[rk_, "consts"], w=[gk(0, 1)])
                        else:
                            tk.op("act", lambda e, t0=t0, nt=nt: e.copy(out=kTc[:, t0:t0 + nt], in_=r_[:, :nt]), r=[rk_], w=[gk(1, 0)])
                for qt, (q0, qn) in enumerate(TILES512):
                    if qt == 4 and not ctx_out:
                        continue
                    keyt = list(range(NT)) if qt < 4 else [16, 17]
                    nsub = qn // 128
                    for hl in range(2):
                        h, base = 2 * c + hl, hl * 64
                        obs = [bank("daO", (4, 5, 6, 7)) for _ in range(2)]
                        steps = [(kt, m) for kt in keyt for m in range(2)]

                        def smm(step):
                            kt, m = steps[step]
                            sbk, skey = bank("daS", (0, 1, 2, 3))
                            qsrc, qk_ = (qA, gk(0, 0)) if m == 0 else (qB, gk(0, 1))
                            B.mm(sbk[:, :qn], kTc[base:base + 64, kt * 128:(kt + 1) * 128], qsrc[base:base + 64, q0:q0 + qn],
                                 start=True, stop=True, r=[gk(1, 0), qk_], w=[skey], sig=True)
                            return sbk, skey
                        nxt_s = smm(0)
                        for step, (kt, m) in enumerate(steps):
                            sbk, skey = nxt_s
                            if step + 1 < len(steps):
                                nxt_s = smm(step + 1)
                            pi = nxt("pt", (0, 1, 2))
                            pt, ptk = PT[pi], f"mPT{pi}"
                            tk.op("act", lambda e, pt=pt, sbk=sbk: e.activation(out=pt[:, :qn], in_=sbk[:, :qn], func=AF.Exp), r=[skey], w=[ptk])
                            ob, okey = obs[m]
                            first, last = (kt == keyt[0]), (kt == keyt[-1])
                            for sub in range(nsub):
                                B.mm(ob[:, sub * 65:(sub + 1) * 65], pt[:, sub * 128:(sub + 1) * 128], VA[:, kt, h * 65:(h + 1) * 65],
                                     start=(first and sub == 0), stop=last, r=[ptk, ("mVA", kt)], w=[okey],
                                     sig=(sub == nsub - 1), sgc=True)
                        o1 = obs[0][0][:, 0:nsub * 65].rearrange("p (s e) -> p s e", e=65)
                        o2 = obs[1][0][:, 0:nsub * 65].rearrange("p (s e) -> p s e", e=65)
                        k1, k2 = obs[0][1], obs[1][1]
                        tk.op("dve", lambda e, o1=o1: e.reciprocal(out=sm[:, 16:16 + nsub], in_=o1[:, :, 64]), r=[k1], w=["msm"])
                        tk.op("dve", lambda e, o2=o2: e.reciprocal(out=sm[:, 20:20 + nsub], in_=o2[:, :, 64]), r=[k2], w=["msm"])
                        tk.op("dve", lambda e: e.tensor_scalar(out=sm[:, 20:20 + nsub], in0=sm[:, 20:20 + nsub], scalar1=nlam[:, l:l + 1], scalar2=None,
                                                               op0=ALU.mult), r=["msm", "nlam"], w=["msm"])
                        t0_, t1_ = tA[0], tA[1]
                        tk.op("dve", lambda e, o1=o1: e.tensor_tensor(out=t0_[:, 0:nsub * 64].rearrange("p (s e) -> p s e", e=64), in0=o1[:, :, 0:64],
                                                                      in1=sm[:, 16:16 + nsub].unsqueeze(2).to_broadcast([128, nsub, 64]), op=ALU.mult),
                              r=[k1, "msm"], w=["mtA0"])
                        tk.op("dve", lambda e, o2=o2: e.tensor_tensor(out=t1_[:, 0:nsub * 64].rearrange("p (s e) -> p s e", e=64), in0=o2[:, :, 0:64],
                                                                      in1=sm[:, 20:20 + nsub].unsqueeze(2).to_broadcast([128, nsub, 64]), op=ALU.mult),
                              r=[k2, "msm"], w=["mtA1"])
                        tt0 = q0 // 128
                        tk.op("dve", lambda e, h=h, tt0=tt0: e.tensor_tensor(
                            out=HACC[:, tt0:tt0 + nsub, h * 64:(h + 1) * 64], in0=t0_[:, 0:nsub * 64].rearrange("p (s e) -> p s e", e=64),
                            in1=t1_[:, 0:nsub * 64].rearrange("p (s e) -> p s e", e=64), op=ALU.add),
                            r=["mtA0", "mtA1"], w=[("mH", tt0 + s_) for s_ in range(nsub)])
            finish_branch(3, True, False, None)

        def state_update(ci, ds_bank, ds_key, ncol, dec_ap, dec_key, bdm):
            S, Sb = Sst[ci], Sbf[ci]
            sk, sbk_ = f"mS{ci}", f"mSb{ci}"
            t2, t2k = tA[2], "mtA2"
            tk.op("dve", lambda e: e.scalar_tensor_tensor(out=t2[:, :ncol], in0=ds_bank[:, :ncol], scalar=dec_ap, in1=bdm, op0=ALU.mult, op1=ALU.mult),
                  r=[ds_key, dec_key, "consts"], w=[t2k])
            tk.op("dve", lambda e: e.scalar_tensor_tensor(out=S[:, :ncol], in0=S[:, :ncol], scalar=dec_ap, in1=t2[:, :ncol], op0=ALU.mult, op1=ALU.add),
                  r=[sk, dec_key, t2k], w=[sk])
            tk.op("pool", lambda e: e.tensor_copy(out=Sb[:, :], in_=S[:, :]), r=[sk], w=[sbk_])

        def branch_ml(b):
            QT, KT, KTOK, VRAW = fm(0), fm(1), tm(2), tm(3)
            o = MIXOFF
            for c in range(2):
                proj_fm(wmix[:, o["ml_q"][0] + c * 128:o["ml_q"][0] + (c + 1) * 128], 128,
                        lambda bk, bkey, t0, nt, c=c: tk.op("act", lambda e: e.copy(out=QT[:, c, t0:t0 + nt], in_=bk[:, :nt]), r=[bkey], w=[gk(0, c)]))
                proj_fm(wmix[:, o["ml_k"][0] + c * 128:o["ml_k"][0] + (c + 1) * 128], 128,
                        lambda bk, bkey, t0, nt, c=c: tk.op("dve", lambda e: e.tensor_copy(out=KT[:, c, t0:t0 + nt], in_=bk[:, :nt]), r=[bkey], w=[gk(1, c)]))
            proj_tm(wmix[:, o["ml_k"][0]:o["ml_k"][0] + 256], 256,
                    lambda bk, bkey, tt: tk.op("act", lambda e: e.copy(out=KTOK[:, tt, :], in_=bk[:, 0:256]), r=[bkey], w=[gk(2, tt // 9)]))
            proj_tm(wmix[:, o["ml_v"][0]:o["ml_v"][0] + 256], 256,
                    lambda bk, bkey, tt: tk.op("dve", lambda e: e.tensor_copy(out=VRAW[:, tt, :], in_=bk[:, 0:256]), r=[bkey], w=[gk(3, tt // 9)]))
            proj_tm(wmix[:, o["ml_if"][0]:o["ml_if"][0] + 16], 16,
                    lambda bk, bkey, tt: tk.op("dve", lambda e: e.tensor_tensor(out=GIF[:, tt, :], in0=bk[:, 0:16], in1=gbias[:, l, :], op=ALU.add),
                                               r=[bkey, ("gbias", l)], w=["mGIF"]))
            if debug.get("ml_stop") == 1:
                return
            for d in range(2):
                tk.op("act", lambda e, d=d: e.activation(out=LLc[:, :, d * 4:(d + 1) * 4], in_=GIF[:, :, 4 + 8 * d:8 + 8 * d], func=AF.Exp, scale=-1.0),
                      r=["mGIF"], w=["mLLc"])
            tk.op("act", lambda e: e.activation(out=LLc[:, :, :], in_=LLc[:, :, :], func=AF.Ln, bias=1.0), r=["mLLc"], w=["mLLc"])
            for tt in range(NT):
                bk, bkey = bank("pj", (0, 1))
                B.mm(bk[:, 0:4], tri[0], LLc[:, tt, 0:4], start=True, stop=True, r=["consts", "mLLc"], w=[bkey], sig=True)
                B.mm(bk[:, 4:8], tri[1], LLc[:, tt, 4:8], start=True, stop=True, r=["consts", "mLLc"], w=[bkey], sig=True)
                B.mm(bk[:, 8:16], ones_f[:], LLc[:, tt, 0:8], start=True, stop=True, r=["ones_f", "mLLc"], w=[bkey], sig=True)
                tk.op("act", lambda e, bk=bk, tt=tt: e.copy(out=CS[:, tt, :], in_=bk[:, 0:16]), r=[bkey], w=["mCS"])
            if debug.get("ml_stop") == 2:
                return
            for d in range(2):
                tk.op("dve", lambda e, d=d: e.tensor_tensor(out=UU[:, :, d * 4:(d + 1) * 4], in0=GIF[:, :, 8 * d:8 * d + 4], in1=CS[:, :, d * 4:(d + 1) * 4], op=ALU.add),
                      r=["mGIF", "mCS"], w=["mUU"])
            tk.op("act", lambda e: e.activation(out=UU[:, :, :], in_=UU[:, :, :], func=AF.Exp), r=["mUU"], w=["mUU"])
            tk.op("act", lambda e: e.activation(out=RR[:, :, :], in_=CS[:, :, 0:8], func=AF.Exp, scale=-1.0, bias=math.log(0.125)), r=["mCS"], w=["mRR"])
            tk.op("act", lambda e: e.activation(out=CS[:, :, 8:16], in_=CS[:, :, 8:16], func=AF.Exp, scale=-1.0), r=["mCS"], w=["mCS"])
            for d in range(2):
                for c in range(2):
                    for hl in range(2):
                        tk.op("dve", lambda e, d=d, c=c, hl=hl: e.tensor_copy(
                            out=DECP[hl * 64:(hl + 1) * 64, :, d, c:c + 1], in_=CS[hl * 64:(hl + 1) * 64, :, 8 + d * 4 + 2 * c + hl:8 + d * 4 + 2 * c + hl + 1]),
                            r=["mCS"], w=["mDECP"])
            if debug.get("ml_stop") == 3:
                return
            tk.op("pool", lambda e: e.memset(VA[:, :, :].rearrange("p t (h e) -> p (t h) e", h=4)[:, :, 64:65], 1.0), w=[("mVA", tt) for tt in range(NT)])
            for tt in range(NT):
                tk.op("dve", lambda e, tt=tt: e.tensor_copy(out=VA[:, tt, :].rearrange("p (h e) -> p h e", h=4)[:, :, 0:64],
                                                           in_=VRAW[:, tt, :].rearrange("p (h e) -> p h e", h=4)),
                      r=[gk(3, tt // 9)], w=[("mVA", tt)])
            for si in range(4):
                tk.op("dve", lambda e, si=si: e.memset(Sst[si][:, :], 0.0), w=[f"mS{si}"])
                tk.op("pool", lambda e, si=si: e.memset(Sbf[si][:, :], 0.0), w=[f"mSb{si}"])
            for step in range(NT):
                for d in range(2):
                    tt = TT_ORDER[d][step]
                    first_visit = (step, d) <= (TT_ORDER[1 - d].index(tt), 1 - d)
                    ts_ = slice(tt * 128, (tt + 1) * 128)
                    for c in range(2):
                        si = 2 * d + c
                        pi = nxt("pt", (0, 1, 2))
                        at, atk = PT[pi], f"mPT{pi}"
                        ki = nxt("qk", (0, 1))
                        ktb, ktbk = ktk[ki], f"mktk{ki}"
                        for hl in range(2):
                            ucol = d * 4 + 2 * c + hl
                            sbk, skey = bank("scS", (0, 1, 6, 7))
                            B.mm(sbk[:, 0:128], KT[hl * 64:(hl + 1) * 64, c, ts_], QT[hl * 64:(hl + 1) * 64, c, ts_],
                                 start=True, stop=True, r=[gk(1, c), gk(0, c)], w=[skey], sig=True)
                            tk.op("dve", lambda e, at=at, sbk=sbk, d=d, hl=hl, ucol=ucol, tt=tt: e.scalar_tensor_tensor(
                                out=at[:, hl * 128:(hl + 1) * 128], in0=sbk[:, 0:128], scalar=UU[:, tt, ucol:ucol + 1], in1=tri[d],
                                op0=ALU.mult, op1=ALU.mult), r=[skey, "consts", "mUU"], w=[atk])
                            tk.op("act", lambda e, ktb=ktb, hl=hl, ucol=ucol, tt=tt, c=c: e.activation(
                                out=ktb[:, hl * 64:(hl + 1) * 64], in_=KTOK[:, tt, c * 128 + hl * 64:c * 128 + (hl + 1) * 64], func=AF.Copy,
                                scale=UU[:, tt, ucol:ucol + 1]), r=[gk(2, tt // 9), "mUU"], w=[ktbk])
                        ob, okey = bank("scO", (2, 3))
                        B.mm(ob[:, 0:130], QT[:, c, ts_], Sbf[si][:, 0:130], start=True, stop=False, r=[gk(0, c), f"mSb{si}"], w=[okey], sig=True)
                        for hl in range(2):
                            h = 2 * c + hl
                            B.mm(ob[:, hl * 65:(hl + 1) * 65], at[:, hl * 128:(hl + 1) * 128], VA[:, tt, h * 65:(h + 1) * 65], start=False, stop=(hl == 1),
                                 r=[atk, ("mVA", tt)], w=[okey], sig=True)
                        dbk, dkey = bank("scD", (4, 5))
                        B.mm(dbk[:, 0:130], ktb[:, :], VA[:, tt, 2 * c * 65:(2 * c + 2) * 65], start=True, stop=True,
                             r=[ktbk, ("mVA", tt)], w=[dkey], sig=True)
                        state_update(si, dbk, dkey, 130, DECP[:, tt, d, c:c + 1], "mDECP", bd2)
                        o3 = ob[:, 0:130].rearrange("p (h e) -> p h e", h=2)
                        rr = RR[:, tt, d * 4 + 2 * c:d * 4 + 2 * c + 2]
                        tk.op("dve", lambda e, o3=o3, rr=rr: e.tensor_tensor(out=sm[:, 24:26], in0=o3[:, :, 64], in1=rr, op=ALU.mult), r=[okey, "mRR"], w=["msm"])
                        tk.op("dve", lambda e: e.scalar_tensor_tensor(out=sm[:, 26:28], in0=sm[:, 24:26], scalar=-1.0, in1=sm[:, 24:26], op0=ALU.mult, op1=ALU.max),
                              r=["msm"], w=["msm"])
                        tk.op("dve", lambda e: e.tensor_scalar(out=sm[:, 26:28], in0=sm[:, 26:28], scalar1=1.0, scalar2=None, op0=ALU.max), r=["msm"], w=["msm"])
                        tk.op("dve", lambda e: e.reciprocal(out=sm[:, 26:28], in_=sm[:, 26:28]), r=["msm"], w=["msm"])
                        tk.op("dve", lambda e, rr=rr: e.tensor_tensor(out=sm[:, 28:30], in0=sm[:, 26:28], in1=rr, op=ALU.mult), r=["msm", "mRR"], w=["msm"])
                        hv = HACC[:, tt, 2 * c * 64:(2 * c + 2) * 64].rearrange("p (h e) -> p h e", h=2)
                        if first_visit:
                            tk.op("dve", lambda e, o3=o3, hv=hv: e.tensor_tensor(out=hv, in0=o3[:, :, 0:64],
                                                                               in1=sm[:, 28:30].unsqueeze(2).to_broadcast([128, 2, 64]), op=ALU.mult),
                                  r=[okey, "msm"], w=[("mH", tt)])
                        else:
                            t0_ = tA[0]
                            tk.op("dve", lambda e, o3=o3, t0_=t0_: e.tensor_tensor(out=t0_[:, 0:128].rearrange("p (h e) -> p h e", h=2), in0=o3[:, :, 0:64],
                                                                                   in1=sm[:, 28:30].unsqueeze(2).to_broadcast([128, 2, 64]), op=ALU.mult),
                                  r=[okey, "msm"], w=["mtA0"])
                            tk.op("dve", lambda e, hv=hv, t0_=t0_: e.tensor_tensor(out=hv, in0=hv, in1=t0_[:, 0:128].rearrange("p (h e) -> p h e", h=2), op=ALU.add),
                                  r=["mtA0", ("mH", tt)], w=[("mH", tt)])
            if debug.get("ml_stop") == 5:
                return
            gate = tm(0)
            proj_tm(wmix[:, o["ml_o"][0]:o["ml_o"][0] + 256], 256,
                    lambda bk, bkey, tt: tk.op("act", lambda e: e.activation(out=gate[:, tt, :], in_=bk[:, 0:256], func=AF.Sigmoid),
                                               r=[bkey], w=[gk(0, tt // 9)]))
            finish_branch(1, True, True, 0)

        def branch_gla(b):
            QT, KT, KTOK, VRAW = fm(0)[:, 0, :], fm(0)[:, 1, :], tm(2), tm(3)
            AT = fm(1)
            o = MIXOFF
            proj_fm(wmix[:, o["gla_q"][0]:o["gla_q"][0] + 128], 128,
                    lambda bk, bkey, t0, nt: tk.op("act", lambda e: e.activation(out=QT[:, t0:t0 + nt], in_=bk[:, :nt], func=AF.Copy, scale=32 ** -0.5),
                                                   r=[bkey], w=[gk(0, 0)]))
            proj_fm(wmix[:, o["gla_k"][0]:o["gla_k"][0] + 128], 128,
                    lambda bk, bkey, t0, nt: tk.op("dve", lambda e: e.tensor_copy(out=KT[:, t0:t0 + nt], in_=bk[:, :nt]), r=[bkey], w=[gk(0, 1)]))
            proj_tm(wmix[:, o["gla_k"][0]:o["gla_k"][0] + 128], 128,
                    lambda bk, bkey, tt: tk.op("act", lambda e: e.copy(out=KTOK[:, tt, 0:128], in_=bk[:, 0:128]), r=[bkey], w=[gk(2, tt // 9)]))
            proj_tm(wmix[:, o["gla_v"][0]:o["gla_v"][0] + 256], 256,
                    lambda bk, bkey, tt: tk.op("dve", lambda e: e.tensor_copy(out=VRAW[:, tt, :], in_=bk[:, 0:256]), r=[bkey], w=[gk(3, tt // 9)]))
            for d in range(2):
                tk.op("pool", lambda e, d=d: e.memset(AT[0:32, d, :], 1.0), w=[gk(1, d)])
                proj_fm(wmix[:, o["gla_a"][0] + 16 * d:o["gla_a"][0] + 16 * (d + 1)], 16,
                        lambda bk, bkey, t0, nt, d=d: tk.op("act", lambda e: e.copy(out=AT[0:16, d, t0:t0 + nt], in_=bk[0:16, :nt]), r=[bkey], w=[gk(1, d)]))
            tk.dma("pool", wa2[0:17, :, :], wa2_in[l].rearrange("d r c -> r d c"), "mwa2", w=["mwa2"])
            for d in range(2):
                tk.op("dve", lambda e, d=d: e.memset(Sst[d][:, :], 0.0), w=[f"mS{d}"])
                tk.op("pool", lambda e, d=d: e.memset(Sbf[d][:, :], 0.0), w=[f"mSb{d}"])
            for step in range(NT):
                for d in range(2):
                    tt = TT_ORDER[d][step]
                    first_visit = (step, d) <= (TT_ORDER[1 - d].index(tt), 1 - d)
                    lli = nxt("ll", (0, 1))
                    LL, llk = LLs[lli], f"mLL{lli}"
                    ts_ = slice(tt * 128, (tt + 1) * 128)
                    zb, zkey = bank("glz", (0, 1))
                    B.mm(zb[:, 0:128], AT[0:17, d, ts_], wa2[0:17, d, :], start=True, stop=True, r=[gk(1, d), "mwa2"], w=[zkey], sig=True)
                    tk.op("act", lambda e, zb=zb: e.activation(out=LL[:, :], in_=zb[:, 0:128], func=AF.Exp, scale=-1.0), r=[zkey], w=[llk])
                    tk.op("act", lambda e: e.activation(out=LL[:, :], in_=LL[:, :], func=AF.Ln, bias=1.0), r=[llk], w=[llk])
                    cb, ckey = bank("glc", (2, 3))
                    B.mm(cb[:, 0:128], tri[d], LL[:, :], start=True, stop=True, r=["consts", llk], w=[ckey], sig=True)
                    B.mm(cb[:, 128:256], LL[:, :], tri[d], start=True, stop=True, r=["consts", llk], w=[ckey], sig=True)
                    ei = [nxt("ee", (0, 1, 2, 3, 4, 5)) for _ in range(3)]
                    E1, E2, E3 = EE[ei[0]], EE[ei[1]], EE[ei[2]]
                    e1k, e2k, e3k = f"mEE{ei[0]}", f"mEE{ei[1]}", f"mEE{ei[2]}"
                    tk.op("act", lambda e, cb=cb, E1=E1: e.activation(out=E1[:, :], in_=cb[:, 128:256], func=AF.Exp, scale=-1.0 / 16), r=[ckey], w=[e1k])
                    tk.op("act", lambda e, cb=cb, E2=E2: e.activation(out=E2[:, :], in_=cb[:, 128:256], func=AF.Exp, scale=1.0 / 16), r=[ckey], w=[e2k])
                    tk.op("act", lambda e, cb=cb, E3=E3: e.activation(out=E3[:, :], in_=cb[:, 0:128], func=AF.Exp, scale=1.0 / 16), r=[ckey], w=[e3k])
                    qi = nxt("qk", (0, 1))
                    qk_, qkk = qkt[qi], f"mqk{qi}"
                    ktb, ktbk = ktk[qi], f"mktk{qi}"
                    for h in range(4):
                        tk.op("dve", lambda e, h=h, qk_=qk_, E1=E1: e.scalar_tensor_tensor(out=qk_[:, h, :], in0=QT[:, ts_], scalar=hm[:, h:h + 1], in1=E1[:, :],
                                                                                          op0=ALU.mult, op1=ALU.mult), r=[gk(0, 0), "consts", e1k], w=[(qkk, h)])
                    tk.op("pool", lambda e, qk_=qk_, E1=E1: e.tensor_tensor(out=qk_[:, 4, :], in0=QT[:, ts_], in1=E1[:, :], op=ALU.mult), r=[gk(0, 0), e1k], w=[(qkk, 4)])
                    tk.op("pool", lambda e, qk_=qk_, E2=E2: e.tensor_tensor(out=qk_[:, 5, :], in0=KT[:, ts_], in1=E2[:, :], op=ALU.mult), r=[gk(0, 1), e2k], w=[(qkk, 5)])
                    tk.op("pool", lambda e, ktb=ktb, E3=E3, tt=tt: e.tensor_tensor(out=ktb[:, :], in0=KTOK[:, tt, 0:128], in1=E3[:, :], op=ALU.mult),
                          r=[gk(2, tt // 9), e3k], w=[ktbk])
                    sbk, skey = bank("scS", (4, 5))
                    for h in range(4):
                        B.mm(sbk[:, h * 128:(h + 1) * 128], qk_[:, 5, :], qk_[:, h, :], start=True, stop=True, r=[(qkk, 5), (qkk, h)], w=[skey], sig=(h == 3))
                    pi = nxt("pt", (0, 1, 2))
                    at, atk = PT[pi], f"mPT{pi}"
                    tk.op("dve", lambda e, at=at, sbk=sbk, d=d: e.tensor_tensor(
                        out=at[:, :].rearrange("p (h t) -> p h t", h=4), in0=sbk[:, :].rearrange("p (h t) -> p h t", h=4),
                        in1=tri[d].unsqueeze(1).to_broadcast([128, 4, 128]), op=ALU.mult), r=[skey, "consts"], w=[atk])
                    ob, okey = bank("scO", (6,))
                    B.mm(ob[:, 0:256], qk_[:, 4, :], Sbf[d][:, :], start=True, stop=False, r=[(qkk, 4), f"mSb{d}"], w=[okey], sig=True)
                    for h in range(4):
                        B.mm(ob[:, h * 64:(h + 1) * 64], at[:, h * 128:(h + 1) * 128], VRAW[:, tt, h * 64:(h + 1) * 64], start=False, stop=(h == 3),
                             r=[atk, gk(3, tt // 9)], w=[okey], sig=True)
                    dbk, dkey = bank("scD", (7,))
                    B.mm(dbk[:, 0:256], ktb[:, :], VRAW[:, tt, :], start=True, stop=True, r=[ktbk, gk(3, tt // 9)], w=[dkey], sig=True)
                    dcol = 127 if d == 0 else 0
                    state_update(d, dbk, dkey, 256, E1[:, dcol:dcol + 1], e1k, bd4)
                    if first_visit:
                        tk.op("act", lambda e, ob=ob, tt=tt: e.copy(out=HACC[:, tt, :], in_=ob[:, 0:256]), r=[okey], w=[("mH", tt)])
                    else:
                        tk.op("dve", lambda e, ob=ob, tt=tt: e.tensor_tensor(out=HACC[:, tt, :], in0=HACC[:, tt, :], in1=ob[:, 0:256], op=ALU.add),
                              r=[okey, ("mH", tt)], w=[("mH", tt)])
            gate = tm(1)
            proj_tm(wmix[:, o["gla_g"][0]:o["gla_g"][0] + 256], 256,
                    lambda bk, bkey, tt: tk.op("act", lambda e: e.activation(out=gate[:, tt, :], in_=bk[:, 0:256], func=AF.Silu),
                                               r=[bkey], w=[gk(1, tt // 9)]))
            finish_branch(2, True, False, 1)

        def merge_out(b):
            for fo in range(8):
                cached_slab(wob[fo], f"mwo{fo}", w_outp[l][:, fo * 128:(fo + 1) * 128].rearrange("(kc p) c -> p kc c", p=128), 128, ("wo", fo))
            gcol = MIXOFF["gates"][0]
            gsrc = wmix[:, gcol:gcol + 4096].rearrange("(kc p) (j f c) -> p kc j f c", p=128, j=4, f=8)
            for ti, (t0, nt) in enumerate(TILES512):
                if ti == 4 and not ctx_out:
                    continue
                col = b if ti < 4 else 4
                for fc in range(8):
                    wi = nxt("wbs", (0, 1))
                    wbt, wbk = WBs[wi], f"mWB{wi}"
                    wkeys = [(wbk, j) for j in range(4)]
                    if (l, "wbs", fc) not in wcache:
                        wcache[(l, "wbs", fc)] = True
                        for j in range(4):
                            tk.dma("pool", wbt[:, j, :, :], w_branch[l, j, :, fc * 128:(fc + 1) * 128].rearrange("(c p) x -> p c x", p=128), f"{wbk}_{j}", w=[(wbk, j)])
                        tk.op("dve", lambda e, wbt=wbt: e.tensor_tensor(out=wbt[:, :, :, :], in0=wbt[:, :, :, :],
                                                                      in1=geff[:, l, :, :].unsqueeze(3).to_broadcast([128, 4, 2, 128]), op=ALU.mult),
                              r=wkeys + ["geff"], w=wkeys)
                        tk.dma("sp", WBSC[l, fc, :, :, :, :], wbt[:, :, :, :], f"{wbk}_h", r=wkeys, w=[("WBSC", l, fc)])
                    else:
                        tk.dma("sp", wbt[:, :, :, :], WBSC[l, fc, :, :, :, :], f"{wbk}_h", r=[("WBSC", l, fc)], w=wkeys)
                    acc, acck = tA[2], "mtA2"
                    for j in range(4):
                        gi = nxt("gw", (0, 1, 2, 3, 4))
                        gw_, gwk = gwb[gi], f"mgw{gi}"
                        cached_slab(gw_, gwk, gsrc[:, :, j, fc, :], 128, ("gate", fc, j))
                        gb_, gbk = bank("mgG", (0, 1))
                        for kc in range(8):
                            B.mm(gb_[:, :nt], gw_[:, kc, :], hT[:, kc, t0:t0 + nt], start=(kc == 0), stop=(kc == 7), r=[gwk, hkeys[kc]], w=[gbk])
                        yb_, ybk = bank("mgY", (2, 3))
                        for c in range(2):
                            B.mm(yb_[:, :nt], wbt[:, j, c, :], yT[:, j, c, t0:t0 + nt], start=(c == 0), stop=(c == 1),
                                 r=[(wbk, j)] + [("myT", j, tt) for tt in range(t0 // 128, (t0 + nt) // 128)], w=[ybk])
                        si = nxt("sg", (0, 1))
                        sg, sgk = tA[si], f"mtA{si}"
                        tk.op("act", lambda e, sg=sg, gb_=gb_: e.activation(out=sg[:, :nt], in_=gb_[:, :nt], func=AF.Sigmoid), r=[gbk], w=[sgk])
                        if j == 0:
                            tk.op("dve", lambda e, sg=sg, yb_=yb_: e.tensor_tensor(out=acc[:, :nt], in0=sg[:, :nt], in1=yb_[:, :nt], op=ALU.mult),
                                  r=[sgk, ybk], w=[acck])
                        else:
                            tk.op("dve", lambda e, sg=sg, yb_=yb_: e.tensor_tensor(out=sg[:, :nt], in0=sg[:, :nt], in1=yb_[:, :nt], op=ALU.mult),
                                  r=[sgk, ybk], w=[sgk])
                            if j < 3:
                                tk.op("dve", lambda e, sg=sg: e.tensor_tensor(out=acc[:, :nt], in0=acc[:, :nt], in1=sg[:, :nt], op=ALU.add),
                                      r=[sgk, acck], w=[acck])
                            else:
                                tk.op("dve", lambda e, sg=sg, fc=fc: e.tensor_tensor(out=accT[:, fc, :nt], in0=acc[:, :nt], in1=sg[:, :nt], op=ALU.add),
                                      r=[sgk, acck], w=[("macc", fc)])
                for hh in range(nt // 256):
                    s0, n = t0 + hh * 256, 256
                    tk.dma("sp", xsb[:, :, :], XM[b, :, :, s0:s0 + n], "mxs0", r=xm_keys(b, s0, n), w=["mxs0"])
                    for fo in range(8):
                        wb, wk = wob[fo], f"mwo{fo}"
                        yb_, ybk = bank("moY", (4, 5))
                        for kc in range(8):
                            B.mm(yb_[:, :n], wb[:, kc, :128], accT[:, kc, hh * 256:hh * 256 + n], start=(kc == 0), stop=(kc == 7),
                                 r=[f"mwo{i}" for i in range(8)] + [("macc", kc)], w=[ybk])
                        tk.op("dve", lambda e, fo=fo, yb_=yb_, col=col: e.scalar_tensor_tensor(
                            out=ztb[:, fo, :], in0=yb_[:, :n], scalar=gsc[:, l, 1, fo, col:col + 1], in1=xsb[:, fo, :], op0=ALU.mult, op1=ALU.add),
                            r=[ybk, "gsc", "mxs0"], w=[("mzt", fo)])
                    layer_norm_store(lt, ztb, "mzt", xob, "mxo0", l, 1, b, s0, n)

        for b in range(BPC):
            if b >= debug.get("mix_b", BPC):
                continue
            fence(True)
            for s0 in range(0, T, 256):
                col = b if s0 < SEQ else 4
                tk.dma("sp", xsb[:, :, :], XM[b, :, :, s0:s0 + 256], "mxs0", r=xm_keys(b, s0, 256), w=["mxs0"])
                for fc in range(8):
                    eng = ("dve", "pool")[fc % 2]
                    tk.op(eng, lambda e, fc=fc, col=col, s0=s0: e.tensor_scalar(
                        out=hT[:, fc, s0:s0 + 256], in0=xsb[:, fc, :], scalar1=sc1[:, l, 1, fc, col:col + 1],
                        scalar2=modp[:, l, 24 + fc, col:col + 1], op0=ALU.mult, op1=ALU.add), r=["mxs0", "sc1", "modp"], w=[("mhT", fc)])
            fence(False)
            want = debug.get("branches", "amgd")
            if "a" in want:
                branch_na(b)
            if "m" in want:
                branch_ml(b)
            if "g" in want:
                branch_gla(b)
            if "d" in want:
                branch_da(b)
            if dbg_y is not None and b == 0:
                for j in [jj for jj, ch in enumerate("amgd") if ch in want]:
                    tk.dma("sp", dbg_y[j, :, :, :], yT[:, j, :, :], "dbgy", r=[("myT", j, tt) for tt in range(NTO)], w=[("dbgy", j)])
            if not debug.get("skip_merge"):
                fence(True)
                merge_out(b)
        tk.phase_end()
        es.close()


    prologue()
    nl = debug.get("layers", L)
    for l in range(nl):
        ffn_phase(l, 0, only=debug.get("ffn_tiles"))
        if debug.get("stop") == ("ffn1", l):
            break
        mixer_phase(l)
        if debug.get("stop") == ("mix", l):
            break
        ffn_phase(l, 1)
    epilogue()
    return nc


def _consts():
    c = np.zeros((128, 1024), np.float32)
    p = np.arange(128)
    c[:, 0:128] = np.eye(128)
    c[:, 128:256] = (p[:, None] <= p[None, :])
    c[:, 256:384] = (p[:, None] >= p[None, :])
    c[:, 384:640] = ((p[:, None] // 32) == (np.arange(256)[None, :] // 64))
    c[:, 640:770] = ((p[:, None] // 64) == (np.arange(130)[None, :] // 65))
    c[:, 770:774] = ((p[:, None] // 32) == np.arange(4)[None, :])
    c[:, 774] = np.where((p % 64) < 32, 32 ** -0.5, 0.0)
    c[:, 775] = np.where((p % 64) >= 32, 32 ** -0.5, 0.0)
    return c


def _rope():
    t = np.arange(SEQ)
    inv = (10000.0 ** (-np.arange(8, dtype=np.float32) / 8)).astype(np.float32)
    tab = np.zeros((2, 128, SEQ), np.float32)
    for p in range(128):
        i = p % 32
        pos = (t // GRID_W) if i < 16 else (t % GRID_W)
        ang = pos.astype(np.float32) * inv[i % 8]
        tab[0, p] = np.cos(ang)
        tab[1, p] = np.sin(ang) * (-1.0 if (i % 16) < 8 else 1.0)
    return tab


def _partner_cols(off):
    idx = np.arange(256)
    i = idx % 32
    partner = np.where((i % 16) < 8, idx + 8, idx - 8)
    return off + partner


_CONSTS = _consts()
_ROPE = _rope()


def make_in_maps(inp):
    f = lambda a: np.ascontiguousarray(np.asarray(a, dtype=np.float32))
    x, c, ctx, c_ctx = f(inp["x"]), f(inp["c"]), f(inp["ctx"]), f(inp["c_ctx"])
    b_adaT = f(f(inp["b_ada"]).reshape(DEPTH, 72, 128).transpose(2, 0, 1))
    ln = np.stack([f(inp["ln_g"]), f(inp["ln_b"])], axis=2)
    lnT = f(ln.reshape(DEPTH, 3, 2, 8, 128).transpose(4, 0, 1, 2, 3))
    wmix = f(inp["w_mix_in"])
    sw_cols = np.concatenate([_partner_cols(MIXOFF["da_q"][0]), _partner_cols(MIXOFF["da_k"][0])])
    rpb = f(inp["na_rpb"])
    nab = np.empty((DEPTH, 4, NPAT, 128, 128), np.float32)
    for pi, (valid, dr, dc) in enumerate(NA_PATS):
        g = rpb[:, :, dr, dc]
        nab[:, :, pi] = np.where(valid[None, None], g, np.float32(NEG))
    gains = np.ones((DEPTH, 4, 256), np.float32)
    gains[:, 1], gains[:, 2], gains[:, 3] = f(inp["ml_norm_g"]), f(inp["gla_norm_g"]), f(inp["da_norm_g"])
    gainsT = f(gains.reshape(DEPTH, 4, 2, 128).transpose(3, 0, 1, 2))
    wa2 = f(np.concatenate([f(inp["gla_w_a2"]), f(inp["gla_b_a"])[:, :, None, :]], axis=2))
    shared = {
        "w_ada": f(inp["w_ada"]), "b_adaT": b_adaT, "lnT": lnT,
        "ffn_w_in": f(inp["ffn_w_in"]), "ffn_w_out": f(inp["ffn_w_out"]),
        "w_mix_in": wmix, "w_mix_sw": f(wmix[:, :, sw_cols]), "w_branch": f(inp["w_branch"]), "w_out": f(inp["w_out"]),
        "nab": nab, "consts": _CONSTS, "rope": _ROPE, "ml_gate_b": f(inp["ml_gate_b"]), "gainsT": gainsT,
        "gla_wa2": wa2, "da_lambda": f(f(inp["da_lambda"]).reshape(DEPTH, 128)),
    }
    maps = []
    for r in range(NCORES):
        bs = slice(r * BPC, (r + 1) * BPC)
        cc = np.concatenate([c[bs], c_ctx[None, :]], axis=0)
        m = dict(shared)
        m["x_in"] = f(np.concatenate([x[bs], ctx[bs]], axis=1))
        m["cT"] = f(cc.reshape(5, 8, 128).transpose(2, 1, 0))
        maps.append(m)
    return maps


def kernel(**inputs):
    nc = build_program()
    maps = make_in_maps(inputs)
    res = run_bass_kernel_spmd(nc, maps, core_ids=list(range(NCORES)))
    return np.concatenate([r["out"] for r in res.results], axis=0).astype(np.float32)
```
